# Optimizing a Trainium2 kernel written in Bass

```python
import jax, jax.numpy as jnp
from jax import lax
import numpy as np

D_MODEL = 2048
BATCH = 8
SEQ = 2048
DEPTH = 2

D_MIX = D_MODEL
CONV_CH = D_MIX // 4
CONV_WIDTH = 31
MLSTM_HEADS = 4
MLSTM_DH = D_MIX // 8
MLSTM_CH = MLSTM_HEADS * MLSTM_DH
QK_CONV_WIDTH = 4
MLSTM_CHUNK = 128
GM_GROUPS = 4
GM_CH = D_MIX // 4
GM_GROUP_CH = GM_CH // GM_GROUPS
GM_CHUNK = 128
D_FF = 4 * D_MODEL
EPS = 1e-6

IN_SIZES = (CONV_CH, CONV_CH,
            MLSTM_CH, MLSTM_CH, MLSTM_CH, MLSTM_CH,
            MLSTM_HEADS, MLSTM_HEADS,
            GM_CH, GM_CH)
N_IN = sum(IN_SIZES)
IN_SPLITS = tuple(int(s) for s in np.cumsum(IN_SIZES)[:-1])

kernel_name = "hybrid_conv_mlstm_gmlp_trunk"


def rmsnorm(x, g):
    xf = x.astype(jnp.float32)
    y = xf * lax.rsqrt(jnp.mean(xf * xf, axis=-1, keepdims=True) + EPS)
    return (y * g.astype(jnp.float32)).astype(x.dtype)


def layernorm(x, g, b):
    xf = x.astype(jnp.float32)
    mu = jnp.mean(xf, axis=-1, keepdims=True)
    var = jnp.mean(jnp.square(xf - mu), axis=-1, keepdims=True)
    y = (xf - mu) * lax.rsqrt(var + EPS)
    return (y * g.astype(jnp.float32) + b.astype(jnp.float32)).astype(x.dtype)


def causal_dwconv(x, w, b):
    K, C = w.shape
    y = lax.conv_general_dilated(
        x, w[:, None, :].astype(x.dtype), window_strides=(1,), padding=[(K - 1, 0)],
        dimension_numbers=("NWC", "WIO", "NWC"), feature_group_count=C)
    return y + b.astype(x.dtype)


def mlstm_chunkwise(q, k, v, li, lf):
    B, S, H, D = q.shape
    L = MLSTM_CHUNK
    NC = S // L
    f32 = jnp.float32

    def to_chunks(t):
        return t.astype(f32).reshape(B, NC, L, H, D).transpose(1, 0, 3, 2, 4)

    def gate_chunks(t):
        return t.astype(f32).reshape(B, NC, L, H).transpose(1, 0, 3, 2)

    causal = jnp.tril(jnp.ones((L, L), dtype=bool))

    def step(carry, xs):
        C, n, m = carry
        qc, kc, vc, lic, lfc = xs
        b = jnp.cumsum(lfc, axis=-1)
        dmat = jnp.where(causal, b[..., :, None] - b[..., None, :] + lic[..., None, :], -jnp.inf)
        inter = b + m[..., None]
        m_row = jnp.maximum(jnp.max(dmat, axis=-1), inter)
        p = jnp.exp(dmat - m_row[..., None]) * jnp.einsum("bhld,bhsd->bhls", qc, kc)
        g = jnp.exp(inter - m_row)
        num = jnp.einsum("bhls,bhse->bhle", p, vc) + g[..., None] * jnp.einsum("bhld,bhde->bhle", qc, C)
        den = jnp.sum(p, axis=-1) + g * jnp.einsum("bhld,bhd->bhl", qc, n)
        h = num / jnp.maximum(jnp.abs(den), jnp.exp(-m_row))[..., None]
        bL = b[..., -1]
        a = bL[..., None] - b + lic
        m_new = jnp.maximum(bL + m, jnp.max(a, axis=-1))
        wk = jnp.exp(a - m_new[..., None])
        decay = jnp.exp(bL + m - m_new)
        kw = kc * wk[..., None]
        C_new = decay[..., None, None] * C + jnp.einsum("bhld,bhle->bhde", kw, vc)
        n_new = decay[..., None] * n + jnp.sum(kw, axis=2)
        return (C_new, n_new, m_new), h

    init = (jnp.zeros((B, H, D, D), f32), jnp.zeros((B, H, D), f32), jnp.zeros((B, H), f32))
    _, hs = lax.scan(step, init, (to_chunks(q), to_chunks(k), to_chunks(v), gate_chunks(li), gate_chunks(lf)))
    return hs.transpose(1, 0, 3, 2, 4).reshape(B, S, H, D).astype(q.dtype)


def hybrid_mixer(x, ln_g, w_in, conv_w, conv_b, conv_norm_g, conv_norm_b, qk_conv_w, qk_conv_b,
                 igate_b, fgate_b, mlstm_norm_g, gm_norm_g, gm_norm_b, gm_w, gm_b, w_out):
    B, S, _ = x.shape
    h = rmsnorm(x, ln_g)
    z = h @ w_in
    cv, cg, q, k, v, o, ig, fg, gu, gv = jnp.split(z, IN_SPLITS, axis=-1)

    a = cv * jax.nn.sigmoid(cg)
    a = causal_dwconv(a, conv_w, conv_b)
    a = jax.nn.silu(layernorm(a, conv_norm_g, conv_norm_b))

    qk = jax.nn.silu(causal_dwconv(jnp.concatenate([q, k], axis=-1), qk_conv_w, qk_conv_b))
    q, k = jnp.split(qk, 2, axis=-1)
    heads = lambda t: t.reshape(B, S, MLSTM_HEADS, MLSTM_DH)
    li = ig.astype(jnp.float32) + igate_b.astype(jnp.float32)
    lf = jax.nn.log_sigmoid(fg.astype(jnp.float32) + fgate_b.astype(jnp.float32))
    hb = mlstm_chunkwise(heads(q), heads(k) * (MLSTM_DH ** -0.5), heads(v), li, lf)
    hbf = hb.astype(jnp.float32)
    hbf = hbf * lax.rsqrt(jnp.mean(hbf * hbf, axis=-1, keepdims=True) + EPS)
    hb = (hbf.reshape(B, S, MLSTM_CH) * mlstm_norm_g.astype(jnp.float32)).astype(x.dtype)
    hb = hb * jax.nn.sigmoid(o)

    gu = jax.nn.gelu(gu)
    gv = layernorm(jax.nn.gelu(gv), gm_norm_g, gm_norm_b)
    nch = S // GM_CHUNK
    gvr = gv.reshape(B, nch, GM_CHUNK, GM_GROUPS, GM_GROUP_CH)
    w_causal = gm_w * jnp.tril(jnp.ones((GM_CHUNK, GM_CHUNK), gm_w.dtype))
    sp = jnp.einsum("gts,bcsge->bctge", w_causal, gvr) + gm_b.T[None, None, :, :, None]
    c = gu * sp.reshape(B, S, GM_CH)

    return jnp.concatenate([a, hb, c], axis=-1) @ w_out


def setup_inputs(seed: int = 0) -> dict:
    key = jax.random.key(seed)
    ks = jax.random.split(key, 24)
    f32 = jnp.float32
    nrm = lambda k, shape, scale: jax.random.normal(k, shape, f32) * scale
    fbias = jnp.linspace(3.0, 6.0, MLSTM_HEADS, dtype=f32)[None, :] + nrm(ks[9], (DEPTH, MLSTM_HEADS), 0.1)
    return {
        "x": nrm(ks[0], (BATCH, SEQ, D_MODEL), 1.0),
        "ln_mix_g": 1.0 + nrm(ks[1], (DEPTH, D_MODEL), 0.02),
        "w_in": nrm(ks[2], (DEPTH, D_MODEL, N_IN), D_MODEL ** -0.5),
        "conv_w": nrm(ks[3], (DEPTH, CONV_WIDTH, CONV_CH), CONV_WIDTH ** -0.5),
        "conv_b": nrm(ks[4], (DEPTH, CONV_CH), 0.02),
        "conv_norm_g": 1.0 + nrm(ks[5], (DEPTH, CONV_CH), 0.02),
        "conv_norm_b": nrm(ks[6], (DEPTH, CONV_CH), 0.02),
        "qk_conv_w": nrm(ks[7], (DEPTH, QK_CONV_WIDTH, 2 * MLSTM_CH), QK_CONV_WIDTH ** -0.5),
        "qk_conv_b": nrm(ks[8], (DEPTH, 2 * MLSTM_CH), 0.02),
        "igate_b": nrm(ks[10], (DEPTH, MLSTM_HEADS), 0.1),
        "fgate_b": fbias,
        "mlstm_norm_g": 1.0 + nrm(ks[11], (DEPTH, MLSTM_CH), 0.02),
        "gm_norm_g": 1.0 + nrm(ks[12], (DEPTH, GM_CH), 0.02),
        "gm_norm_b": nrm(ks[13], (DEPTH, GM_CH), 0.02),
        "gm_w": nrm(ks[14], (DEPTH, GM_GROUPS, GM_CHUNK, GM_CHUNK), GM_CHUNK ** -0.5),
        "gm_b": 1.0 + nrm(ks[15], (DEPTH, GM_GROUPS, GM_CHUNK), 0.02),
        "w_out": nrm(ks[16], (DEPTH, D_MIX, D_MODEL), D_MIX ** -0.5),
        "ln_mlp_g": 1.0 + nrm(ks[17], (DEPTH, D_MODEL), 0.02),
        "w_up": nrm(ks[18], (DEPTH, D_MODEL, D_FF), D_MODEL ** -0.5),
        "w_down": nrm(ks[19], (DEPTH, D_FF, D_MODEL), D_FF ** -0.5),
        "final_g": 1.0 + nrm(ks[20], (D_MODEL,), 0.02),
    }


def reference(x, ln_mix_g, w_in, conv_w, conv_b, conv_norm_g, conv_norm_b, qk_conv_w, qk_conv_b,
              igate_b, fgate_b, mlstm_norm_g, gm_norm_g, gm_norm_b, gm_w, gm_b, w_out,
              ln_mlp_g, w_up, w_down, final_g):
    for l in range(DEPTH):
        x = x + hybrid_mixer(x, ln_mix_g[l], w_in[l], conv_w[l], conv_b[l], conv_norm_g[l], conv_norm_b[l],
                             qk_conv_w[l], qk_conv_b[l], igate_b[l], fgate_b[l], mlstm_norm_g[l],
                             gm_norm_g[l], gm_norm_b[l], gm_w[l], gm_b[l], w_out[l])
        hm = rmsnorm(x, ln_mlp_g[l]) @ w_up[l]
        x = x + jnp.square(jax.nn.relu(hm)) @ w_down[l]
    return rmsnorm(x, final_g)
```

```python
import numpy as np
import concourse.bass as bass
import concourse.mybir as mybir
from concourse.bass_utils import run_bass_kernel_spmd

F32 = mybir.dt.float32
BF16 = mybir.dt.bfloat16
AF = mybir.ActivationFunctionType
ALU = mybir.AluOpType

D = 2048
NIN = 6152
DFF = 8192
DEPTH = 2
EPS = 1e-6
KC = 16
LN16 = float(np.log(16.0))

C_CV, C_CG, C_Q, C_K, C_V, C_O, C_G, C_GU, C_GV = 0, 512, 1024, 2048, 3072, 4096, 5120, 5128, 5640

PP_LNMIX = 0
PP_LNMLP = 16
PP_CONVW = 32
PP_CONVB = 156
PP_CNG = 160
PP_CNB = 164
PP_QKW = 168
PP_QKB = 232
PP_IGB = 248
PP_FGB = 249
NPP = 256
PB_MNG = 0
PB_GNG = 1024
PB_GNB = 1536
PB_GMB = 2048
NPB = 2560
CS_IDENT = 0
CS_MASK = 128
CS_MASKT = 256
CS_ONES = 384
NCST = 512


import threading

_tls = threading.local()


class Coop:
    def __init__(self):
        self.cv = threading.Condition()
        self.cur = -1
        self.done = []
        self.active = False

    def _next(self, i):
        n = len(self.done)
        for d in range(1, n + 1):
            j = (i + d) % n
            if not self.done[j]:
                return j
        return -1

    def switch(self):
        if not self.active:
            return
        i = getattr(_tls, "task", None)
        if i is None:
            return
        w = getattr(_tls, "weight", 1)
        with self.cv:
            for _ in range(w):
                j = self._next(i)
                if j == i or j < 0:
                    return
                self.cur = j
                self.cv.notify_all()
                while self.cur != i:
                    self.cv.wait()

    def run(self, fns, weights=None, pools=None):
        n = len(fns)
        self.done = [False] * n
        errs = []

        def worker(i):
            _tls.task = i
            _tls.weight = (weights or [1] * n)[i]
            _tls.pools = (pools or [None] * n)[i]
            with self.cv:
                while self.cur != i:
                    self.cv.wait()
            try:
                fns[i]()
            except BaseException as e:
                errs.append(e)
            finally:
                with self.cv:
                    self.done[i] = True
                    self.cur = self._next(i)
                    self.cv.notify_all()

        self.active = True
        self.cur = 0
        ths = [threading.Thread(target=worker, args=(i,)) for i in range(n)]
        for t in ths:
            t.start()
        for t in ths:
            t.join()
        self.active = False
        self.cur = -1
        if errs:
            raise errs[0]


COOP = Coop()


class Sched:
    def __init__(self, nc, sems):
        self.nc = nc
        self.eng = {"pe": nc.tensor, "act": nc.scalar, "dve": nc.vector, "pool": nc.gpsimd, "sp": nc.sync}
        self.sem = dict(sems)
        self.scale = {k: 1 for k in sems}
        self.cnt = {k: 0 for k in sems}
        self.seen = {e: {} for e in self.eng}
        self.res = {}

    def add_dma_sem(self, name, sem):
        self.sem[name] = sem
        self.scale[name] = 16
        self.cnt[name] = 0

    def _deps(self, e, reads, writes):
        deps = {}

        def add(p):
            f, idx = p
            if f == e and e == "pe":
                return
            if idx > deps.get(f, 0):
                deps[f] = idx

        for k in reads:
            r = self.res.get(k)
            if r and r[0]:
                add(r[0])
        for k in writes:
            r = self.res.get(k)
            if r:
                if r[0]:
                    add(r[0])
                for p in r[1].items():
                    add(p)
        for f, idx in deps.items():
            if idx > self.seen[e].get(f, 0):
                self.eng[e].wait_ge(self.sem[f], idx * self.scale[f])
                self.seen[e][f] = idx

    def _mark(self, who, idx, reads, writes):
        for k in reads:
            r = self.res.get(k)
            if r is None:
                r = [None, {}]
                self.res[k] = r
            r[1][who] = idx
        for k in writes:
            self.res[k] = [(who, idx), {}]

    def op(self, e, emit, reads=(), writes=()):
        COOP.switch()
        self._deps(e, reads, writes)
        ins = emit()
        self.cnt[e] += 1
        idx = self.cnt[e]
        ins.then_inc(self.sem[e], 1)
        self._mark(e, idx, reads, writes)

    def dma(self, q, dsem, out, in_, reads=(), writes=()):
        COOP.switch()
        self._deps(q, reads, writes)
        ins = self.eng[q].dma_start(out=out, in_=in_)
        self.cnt[dsem] += 1
        ins.then_inc(self.sem[dsem], 16)
        self._mark(dsem, self.cnt[dsem], reads, writes)

    def wait_all(self, e, names):
        for f in names:
            idx = self.cnt[f]
            if idx > self.seen[e].get(f, 0):
                self.eng[e].wait_ge(self.sem[f], idx * self.scale[f])
                self.seen[e][f] = idx


def build_nc(S, T, depth=DEPTH):
    NT = S // T
    NCH = T // 128
    nc = bass.Bass("TRN2", target_bir_lowering=False)
    x_d = nc.dram_tensor("x", [S, D], F32, kind="ExternalInput").ap()
    win_d = nc.dram_tensor("w_in", [depth, D, NIN], F32, kind="ExternalInput").ap()
    wout_d = nc.dram_tensor("w_out", [depth, D, D], F32, kind="ExternalInput").ap()
    wup_d = nc.dram_tensor("w_up", [depth, D, DFF], F32, kind="ExternalInput").ap()
    wdn_d = nc.dram_tensor("w_down", [depth, DFF, D], F32, kind="ExternalInput").ap()
    pp_d = nc.dram_tensor("pp", [depth, 128, NPP], F32, kind="ExternalInput").ap()
    pb_d = nc.dram_tensor("pb", [depth, NPB], F32, kind="ExternalInput").ap()
    fg_d = nc.dram_tensor("final_g", [1, D], F32, kind="ExternalInput").ap()
    gmw_d = nc.dram_tensor("gm_wT", [depth, 128, 4, 128], F32, kind="ExternalInput").ap()
    cst_d = nc.dram_tensor("cst", [128, NCST], F32, kind="ExternalInput").ap()
    y_d = nc.dram_tensor("y", [S, D], F32, kind="ExternalOutput").ap()

    import contextlib
    es = contextlib.ExitStack()
    with es:
        def sb(name, shape, dt):
            return es.enter_context(nc.sbuf_tensor("s_" + name, shape, dt))

        def psum(name, shape, dt):
            return es.enter_context(nc.psum_tensor("p_" + name, shape, dt))

        def semaphore(name):
            return es.enter_context(nc.semaphore("m_" + name))

        S_ = Sched(nc, {e: semaphore("s_" + e) for e in ["pe", "act", "dve", "pool", "sp"]})
        NW = 3
        for i in range(NW):
            S_.add_dma_sem(f"w{i}", semaphore(f"w{i}"))
        for nm in ["misc", "xin0", "xin1", "xin2", "xin3", "st0", "st1", "fg", "pbt", "pbc0", "pbc1", "pbc2", "wg"]:
            S_.add_dma_sem(nm, semaphore(nm))

        x = sb("x", [128, NCH, D], F32)
        hT = sb("hT", [128, KC, T], BF16)
        wbuf = [sb(f"wb{i}", [128, 8192], BF16) for i in range(NW)]
        wg = sb("wg", [128, KC, 8], BF16)
        cst = sb("cst", [128, NCST], F32)
        identb = sb("identb", [128, 128], BF16)
        pp = [sb(f"pp{l}", [128, NPP], F32) for l in range(depth)]
        pbt = sb("pbt", [128, 1024], F32)
        WcT = [sb(f"WcT{l}", [128, 4, 128], BF16) for l in range(depth)]
        nfb = [sb(f"nfb{l}", [4, 1], F32) for l in range(depth)]
        FA = sb("FA", [128, 4, 544], F32)
        FB = sb("FB", [128, 4, 544], F32)
        FC = sb("FC", [128, 4, 512], F32)
        actG = sb("actG", [128, 4, T], BF16)
        qT = sb("qT", [128, 4, T], BF16)
        kT = sb("kT", [128, 4, T], BF16)
        gT = [sb(f"gT{i}", [128, 4, T], BF16) for i in range(2)]
        vaug = sb("vaug", [128, NCH, 2, 257], BF16)
        xs0 = sb("xs0", [128, D], BF16)
        xs = [xs0, xs0]
        ss = sb("ss", [128, 16], F32)
        rstd = sb("rstd", [128, 16], F32)
        Mend = sb("Mend", [4, NCH + 1], F32)
        nMend = sb("nMend", [4, NCH + 1], F32)
        DEC = sb("DEC", [4, NCH], F32)
        sel4 = sb("sel4", [4, 4, 128], F32)
        Bcar = [sb(f"Bcar{l}", [4, 1], F32) for l in range(depth)]
        Mcar = [sb(f"Mcar{l}", [4, 1], F32) for l in range(depth)]
        uP = sb("uP", [4, T], F32)
        tok = sb("tok", [128, NCH, 4, 4], F32)
        decb = sb("decb", [128, 4, NCH], F32)
        Cst = [[sb(f"Cst{l}_{h}", [128, 2, 257], F32) for h in range(4)] for l in range(depth)]
        Cbf1 = [sb(f"Cbf_{h}", [128, 2, 257], BF16) for h in range(4)]
        Cbf = [Cbf1 for l in range(depth)]
        histA = [sb(f"histA{l}", [128, 4, 30], F32) for l in range(depth)]
        histQ = [sb(f"histQ{l}", [128, 16, 3], F32) for l in range(depth)]
        zq = [sb(f"zq{i}", [128, T + 3], F32) for i in range(2)]
        cacc = [sb(f"cacc{i}", [128, T], F32) for i in range(2)]
        ppb = [sb(f"ppb{i}", [128, 128], F32) for i in range(2)]
        pbf = [sb(f"pbf{i}", [128, 128], BF16) for i in range(2)]
        pTs = [sb(f"pTs{i}", [128, 128], BF16) for i in range(2)]
        tB = [sb(f"tB{i}", [128, 257], F32) for i in range(2)]
        tot = [sb(f"tot{i}", [128, 257], F32) for i in range(2)]
        sm = [sb(f"sm{i}", [128, 8], F32) for i in range(2)]
        hb = [sb(f"hb{i}", [128, 256], BF16) for i in range(2)]
        kw = [sb(f"kw{i}", [128, 256], BF16) for i in range(2)]
        rt = cacc
        gt = zq
        cs1 = sb("cs1", [128, 16], F32)
        epsc = sb("epsc", [128, 1], F32)
        gU = zq[0]
        gM = zq[1]
        gL = cacc[0]
        gB = cacc[1]

        ps = [psum(f"ps{i}", [128, 512], F32) for i in range(6)]
        pb = [psum(f"pbk{i}", [128, 1024], BF16) for i in range(2)]

        state = {"ps": 0, "pb": 0, "tmp": 0}

        def nps():
            pools = getattr(_tls, "pools", None)
            if pools is not None:
                lst = pools["ps"]
                i = lst[pools["psi"] % len(lst)]
                pools["psi"] += 1
                return ps[i], ("ps", i)
            i = state["ps"]
            state["ps"] = (i + 1) % 6
            return ps[i], ("ps", i)

        def npb():
            pools = getattr(_tls, "pools", None)
            if pools is not None:
                lst = pools["pb"]
                i = lst[pools["pbi"] % len(lst)]
                pools["pbi"] += 1
                return pb[i], ("pb", i)
            i = state["pb"]
            state["pb"] = (i + 1) % 2
            return pb[i], ("pb", i)

        def k(t, *idx):
            return (t.name,) + idx

        def mm(out_ap, pairs, reads, writes):
            def emit():
                n = len(pairs)
                ins = None
                for i, (l, r) in enumerate(pairs):
                    ins = nc.tensor.matmul(out_ap, l, r, start=(i == 0), stop=(i == n - 1))
                return ins
            S_.op("pe", emit, reads, writes)

        def tr(out_ap, in_ap, ident_ap, reads, writes):
            S_.op("pe", lambda: nc.tensor.transpose(out_ap, in_ap, ident_ap), reads, writes)

        def act(out, in_, func, reads, writes, **kw_):
            S_.op("act", lambda: nc.scalar.activation(out=out, in_=in_, func=func, **kw_), reads, writes)

        def ts(e, out, in0, s1, s2, op0, op1, reads, writes):
            eng = nc.vector if e == "dve" else nc.gpsimd
            if op1 is None:
                S_.op(e, lambda: eng.tensor_scalar(out=out, in0=in0, scalar1=s1, scalar2=None, op0=op0), reads, writes)
            else:
                S_.op(e, lambda: eng.tensor_scalar(out=out, in0=in0, scalar1=s1, scalar2=s2, op0=op0, op1=op1), reads, writes)

        def tt(e, out, in0, in1, op, reads, writes):
            eng = nc.vector if e == "dve" else nc.gpsimd
            S_.op(e, lambda: eng.tensor_tensor(out=out, in0=in0, in1=in1, op=op), reads, writes)

        def stt(e, out, in0, scalar, in1, op0, op1, reads, writes):
            eng = nc.vector if e == "dve" else nc.gpsimd
            S_.op(e, lambda: eng.scalar_tensor_tensor(out=out, in0=in0, scalar=scalar, in1=in1, op0=op0, op1=op1), reads, writes)

        def rsqrt(out, in_, scale, eps, reads, writes):
            act(out, in_, AF.Sqrt, reads, writes, scale=scale, bias=eps)
            S_.op("dve", lambda: nc.vector.reciprocal(out=out, in_=out), writes, writes)

        def cp(e, out, in_, reads, writes):
            if e == "act":
                S_.op("act", lambda: nc.scalar.copy(out=out, in_=in_), reads, writes)
            else:
                eng = nc.vector if e == "dve" else nc.gpsimd
                S_.op(e, lambda: eng.tensor_copy(out=out, in_=in_), reads, writes)

        def memset(e, ap, val, writes):
            eng = nc.vector if e == "dve" else nc.gpsimd
            S_.op(e, lambda: eng.memset(ap, val), (), writes)

        wstream = []
        for tt_ in range(NT):
            for l in range(depth):
                def wi(c0):
                    return ("in", win_d[l, :, c0:c0 + 512].rearrange("(k p) n -> p k n", p=128))

                def wo(r0):
                    return ("row", wout_d[l, r0:r0 + 512, :].rearrange("(k p) n -> p k n", p=128))
                seq = [wi(C_CG), wi(C_CV),
                       wi(C_Q), wi(C_K), wi(C_V), wi(C_O), wi(C_GV), wo(512),
                       wi(C_Q + 512), wi(C_K + 512), wi(C_V + 512), wi(C_O + 512), wi(C_GU), wo(1536), wo(1024),
                       wo(0)]
                for j in range(DFF // 512):
                    seq.append(("in", wup_d[l, :, j * 512:(j + 1) * 512].rearrange("(k p) n -> p k n", p=128)))
                    seq.append(("row", wdn_d[l, j * 512:(j + 1) * 512, :].rearrange("(k p) n -> p k n", p=128)))
                wstream.extend(seq)
        wpos = {"issued": 0, "used": 0}

        def w_issue():
            i = wpos["issued"]
            if i >= len(wstream):
                return
            kind, src = wstream[i]
            slot = i % NW
            if kind == "in":
                dst = wbuf[slot][:, :].rearrange("p (k n) -> p k n", k=16)
            else:
                dst = wbuf[slot][:, :].rearrange("p (k n) -> p k n", k=4)
            S_.dma("pool", f"w{slot}", dst, src, reads=(), writes=[("wb", slot)])
            wpos["issued"] = i + 1

        def w_next():
            i = wpos["used"]
            wpos["used"] = i + 1
            kind, _ = wstream[i]
            slot = i % NW
            if kind == "in":
                v = wbuf[slot][:, :].rearrange("p (k n) -> p k n", k=16)
            else:
                v = wbuf[slot][:, :].rearrange("p (k n) -> p k n", k=4)
            return v, ("wb", slot)

        S_.dma("sp", "misc", cst[:, :], cst_d[:, :], writes=[k(cst)])
        for l in range(depth):
            S_.dma("sp", "misc", pp[l][:, :], pp_d[l, :, :], writes=[k(pp[l])])
        for l in range(depth):
            S_.dma("sp", "misc", FC[:, :, l * 128:(l + 1) * 128], gmw_d[l, :, :, :], reads=(), writes=[k(FC)])
        for e in ["pe", "act", "dve", "pool"]:
            S_.wait_all(e, ["misc"])
        for i in range(NW):
            w_issue()
        cp("dve", identb[:, :], cst[:, CS_IDENT:CS_IDENT + 128], [k(cst)], [k(identb)])
        memset("dve", epsc[:, :], EPS, [k(epsc)])
        memset("dve", sel4[:, :, :], 0.0, [k(sel4)])
        for h in range(4):
            ts("dve", sel4[:, h, :], cst[0:4, CS_ONES:CS_ONES + 128], cst[0:4, CS_IDENT + h:CS_IDENT + h + 1], None, ALU.mult, None,
               [k(cst)], [k(sel4)])
        for l in range(depth):
            ts("dve", nfb[l][:, :], pp[l][0:4, PP_FGB:PP_FGB + 1], -1.0, None, ALU.mult, None, [k(pp[l])], [k(nfb[l])])
            memset("dve", Bcar[l][:, :], 0.0, [k(Bcar[l])])
            memset("dve", Mcar[l][:, :], 0.0, [k(Mcar[l])])
            memset("dve", histA[l][:, :, :], 0.0, [k(histA[l])])
            memset("dve", histQ[l][:, :, :], 0.0, [k(histQ[l])])
            for h in range(4):
                memset("dve", Cst[l][h][:, :, :], 0.0, [k(Cst[l][h])])
            for g in range(4):
                tt("dve", WcT[l][:, g, :], FC[:, g, l * 128:(l + 1) * 128], cst[:, CS_MASKT:CS_MASKT + 128], ALU.mult,
                   [k(FC), k(cst)], [k(WcT[l])])
        memset("dve", vaug[:, :, :, 256:257], 1.0, [k(vaug)])

        def norm_act(l, c):
            act(hT[:, :, c * 128:(c + 1) * 128], x[:, c, :].rearrange("p (a b) -> p a b", a=KC), AF.Square,
                [k(x, c)], [k(hT, c), k(ss, c)], accum_out=ss[:, c:c + 1])
            rsqrt(rstd[:, c:c + 1], ss[:, c:c + 1], 1.0 / D, EPS, [k(ss, c)], [k(rstd, c)])
            act(xs0[:, :], x[:, c, :], AF.Copy, [k(x, c), k(rstd, c)], [k(xs0)], scale=rstd[:, c:c + 1])

        def norm_pe(l, goff, c):
            xb = xs0
            for kq in range(4):
                pbk, pk = npb()
                for i in range(4):
                    kk = kq * 4 + i
                    tr(pbk[:, i * 128:(i + 1) * 128], xb[:, kk * 128:(kk + 1) * 128], identb[:, :],
                       [k(xb), k(identb)], [pk])
                gb = pp[l][:, goff + kq * 4:goff + kq * 4 + 4].unsqueeze(2).to_broadcast([128, 4, 128])
                tt("dve", hT[:, kq * 4:kq * 4 + 4, c * 128:(c + 1) * 128],
                   pbk[:, 0:512].rearrange("p (a b) -> p a b", a=4), gb, ALU.mult, [pk, k(pp[l])], [k(hT, c)])

        def norm_to_hT(l, goff):
            for c in range(NCH):
                norm_act(l, c)
                norm_pe(l, goff, c)

        def norm_hook(l, goff):
            def hook(c):
                if c >= 1:
                    norm_pe(l, goff, c - 1)
                norm_act(l, c)
                if c == NCH - 1:
                    norm_pe(l, goff, c)
            return hook

        def hT_keys():
            return [k(hT, c) for c in range(NCH)]

        def z_fm(wv, wk_, j, out_ps):
            pairs = [(wv[:, kk, j * 128:(j + 1) * 128], hT[:, kk, 0:T]) for kk in range(KC)]
            return pairs

        def accum_into_x(aT, akey, wv, wkey, hook=None):
            for c in range(NCH):
                for db in range(4):
                    p, pk = nps()
                    mm(p[:, :], [(aT[:, kc, c * 128:(c + 1) * 128], wv[:, kc, db * 512:(db + 1) * 512]) for kc in range(4)],
                       [akey, wkey], [pk])
                    tt("dve", x[:, c, db * 512:(db + 1) * 512], x[:, c, db * 512:(db + 1) * 512], p[:, :], ALU.add,
                       [pk, k(x, c)], [k(x, c)])
                if hook is not None:
                    hook(c)

        def wg_prefetch(l):
            S_.dma("pool", "wg", wg[:, :, :], win_d[l, :, C_G:C_G + 8].rearrange("(k p) n -> p k n", p=128),
                   reads=(), writes=[k(wg)])

        def gates_p1(l):
            pI, kI = nps()
            pF, kF = nps()
            mm(pI[0:4, 0:T], [(wg[:, kk, 0:4], hT[:, kk, 0:T]) for kk in range(KC)], [k(wg)] + hT_keys(), [kI])
            mm(pF[0:4, 0:T], [(wg[:, kk, 4:8], hT[:, kk, 0:T]) for kk in range(KC)], [k(wg)] + hT_keys(), [kF])
            act(gU[0:4, 0:T], pI[0:4, 0:T], AF.Identity, [kI, k(pp[l])], [k(gU)], bias=pp[l][0:4, PP_IGB:PP_IGB + 1])
            act(gL[0:4, 0:T], pF[0:4, 0:T], AF.Exp, [kF, k(nfb[l])], [k(gL)], scale=-1.0, bias=nfb[l][:, 0:1])
            act(gL[0:4, 0:T], gL[0:4, 0:T], AF.Ln, [k(gL)], [k(gL)], bias=1.0)
            S_.op("dve", lambda: nc.vector.tensor_tensor_scan(out=gB[0:4, 0:T], data0=cst[0:4, CS_ONES:CS_ONES + 1].to_broadcast([4, T]), data1=gL[0:4, 0:T],
                                                              initial=Bcar[l][:, 0:1], op0=ALU.mult, op1=ALU.subtract),
                  [k(cst), k(gL), k(Bcar[l])], [k(gB)])
            tt("dve", gU[0:4, 0:T], gU[0:4, 0:T], gB[0:4, 0:T], ALU.subtract, [k(gU), k(gB)], [k(gU)])
            S_.op("dve", lambda: nc.vector.tensor_tensor_scan(out=gM[0:4, 0:T], data0=cst[0:4, CS_ONES:CS_ONES + 1].to_broadcast([4, T]), data1=gU[0:4, 0:T],
                                                              initial=Mcar[l][:, 0:1], op0=ALU.mult, op1=ALU.max),
                  [k(cst), k(gU), k(Mcar[l])], [k(gM)])
            cp("dve", Mend[:, 0:1], Mcar[l][:, 0:1], [k(Mcar[l])], [k(Mend)])
            for c in range(NCH):
                cp("dve", Mend[:, c + 1:c + 2], gM[0:4, c * 128 + 127:c * 128 + 128], [k(gM)], [k(Mend)])
            cp("dve", Bcar[l][:, 0:1], gB[0:4, T - 1:T], [k(gB)], [k(Bcar[l])])
            cp("dve", Mcar[l][:, 0:1], gM[0:4, T - 1:T], [k(gM)], [k(Mcar[l])])
            ts("dve", nMend[:, :], Mend[:, :], -1.0, -LN16, ALU.mult, ALU.add, [k(Mend)], [k(nMend)])
            tt("dve", DEC[:, :], Mend[:, 0:NCH], Mend[:, 1:NCH + 1], ALU.subtract, [k(Mend)], [k(DEC)])
            act(DEC[:, :], DEC[:, :], AF.Exp, [k(DEC)], [k(DEC)])
            ts("dve", FC[0:4, 0, 0:T], gM[0:4, 0:T], -1.0, None, ALU.mult, None, [k(gM)], [k(FC)])
            tt("dve", FC[0:4, 2, 0:T], gB[0:4, 0:T], gM[0:4, 0:T], ALU.add, [k(gB), k(gM)], [k(FC)])
            act(FC[0:4, 2, 0:T], FC[0:4, 2, 0:T], AF.Exp, [k(FC)], [k(FC)], scale=-1.0)
            for c in range(NCH):
                sl = slice(c * 128, (c + 1) * 128)
                act(FC[0:4, 1, sl], gM[0:4, sl], AF.Exp, [k(gM), k(Mend)], [k(FC)], scale=-1.0, bias=Mend[:, c:c + 1])
                act(FC[0:4, 3, sl], gU[0:4, sl], AF.Exp, [k(gU), k(nMend)], [k(FC)], bias=nMend[:, c + 1:c + 2])
            cp("dve", uP[:, :], gU[0:4, 0:T], [k(gU)], [k(uP)])

        def gates_p2(l):
            pT_, kT_ = nps()
            for c in range(NCH):
                for qi in range(4):
                    o0 = (c * 4 + qi) * 4
                    tr(pT_[:, o0:o0 + 4], FC[0:4, qi, c * 128:(c + 1) * 128], cst[0:4, CS_IDENT:CS_IDENT + 4],
                       [k(FC), k(cst)], [kT_])
            cp("dve", tok[:, :, :, :].rearrange("p c q h -> p (c q h)"), pT_[:, 0:NCH * 16], [kT_], [k(tok)])
            pD, kD = nps()
            for h in range(4):
                mm(pD[:, h * NCH:(h + 1) * NCH], [(sel4[:, h, :], DEC[:, :])], [k(sel4), k(DEC)], [kD])
            cp("dve", decb[:, :, :].rearrange("p h c -> p (h c)"), pD[:, 0:4 * NCH], [kD], [k(decb)])
        def qk_block(l, wv, wkey, dstT, gj0):
            for j in range(4):
                gj = gj0 + j
                zb = zq[j % 2]
                ca = cacc[j % 2]
                p, pk = nps()
                mm(p[:, 0:T], z_fm(wv, wkey, j, p), [wkey] + hT_keys(), [pk])
                cp("dve", zb[:, 0:3], histQ[l][:, gj, :], [k(histQ[l], gj)], [k(zb)])
                cp("act", zb[:, 3:3 + T], p[:, 0:T], [pk], [k(zb)])
                cp("dve", histQ[l][:, gj, :], zb[:, T:T + 3], [k(zb)], [k(histQ[l], gj)])
                w0 = PP_QKW + gj * 4
                e = "dve"
                ts(e, ca[:, :], zb[:, 0:T], pp[l][:, w0:w0 + 1], pp[l][:, PP_QKB + gj:PP_QKB + gj + 1], ALU.mult, ALU.add,
                   [k(zb), k(pp[l])], [k(ca)])
                for t_ in range(1, 4):
                    stt(e, ca[:, :], zb[:, t_:t_ + T], pp[l][:, w0 + t_:w0 + t_ + 1], ca[:, :], ALU.mult, ALU.add,
                        [k(zb), k(pp[l]), k(ca)], [k(ca)])
                act(dstT[:, j, :], ca[:, :], AF.Silu, [k(ca)], [k(dstT)])

        def mlstm_head(l, hg, hh):
            GO = FC
            h = hg * 2 + hh
            i2 = hh
            s_ = sm[i2]
            for c in range(NCH):
                sl = slice(c * 128, (c + 1) * 128)
                pS, kS = nps()
                mm(pS[:, 0:128], [(qT[:, hh * 2 + dc, sl], kT[:, hh * 2 + dc, sl]) for dc in range(2)],
                   [k(qT), k(kT)], [kS])
                mm(pS[:, 128:256], [(sel4[:, h, :], uP[:, sl])], [k(sel4), k(uP)], [kS])
                ts("dve", ppb[i2][:, :], pS[:, 128:256], tok[:, c, 0, h:h + 1], 0.0, ALU.add, ALU.min,
                   [kS, k(tok)], [k(ppb[i2])])
                yield
                act(ppb[i2][:, :], ppb[i2][:, :], AF.Exp, [k(ppb[i2])], [k(ppb[i2])])
                yield
                tt("pool", ppb[i2][:, :], ppb[i2][:, :], cst[:, CS_MASK:CS_MASK + 128], ALU.mult,
                   [k(ppb[i2]), k(cst)], [k(ppb[i2])])
                yield
                stt("dve", pbf[i2][:, :], pS[:, 0:128], 0.0625, ppb[i2][:, :], ALU.mult, ALU.mult,
                    [kS, k(ppb[i2])], [k(pbf[i2])])
                yield
                pb1, kb1 = npb()
                tr(pb1[:, 0:128], pbf[i2][:, :], identb[:, :], [k(pbf[i2]), k(identb)], [kb1])
                pB, kB = nps()
                mm(pB[:, 0:257], [(qT[:, hh * 2 + dc, sl], Cbf[l][h][:, dc, :]) for dc in range(2)],
                   [k(qT), k(Cbf[l][h])], [kB])
                yield
                cp("act", pTs[i2][:, :], pb1[:, 0:128], [kb1], [k(pTs[i2])])
                act(tB[i2][:, :], pB[:, 0:257], AF.Copy, [kB, k(tok)], [k(tB[i2])], scale=tok[:, c, 1, h:h + 1])
                yield
                pA, kA = nps()
                mm(pA[:, 0:257], [(pTs[i2][:, :], vaug[:, c, hh, :])], [k(pTs[i2]), k(vaug)], [kA])
                pb3, kb3 = npb()
                for dc in range(2):
                    tr(pb3[:, dc * 128:(dc + 1) * 128], kT[:, hh * 2 + dc, sl], identb[:, :], [k(kT), k(identb)], [kb3])
                yield
                tt("dve", tot[i2][:, :], tB[i2][:, :], pA[:, 0:257], ALU.add, [k(tB[i2]), kA], [k(tot[i2])])
                act(kw[i2][:, :], pb3[:, 0:256], AF.Copy, [kb3, k(tok)], [k(kw[i2])], scale=tok[:, c, 3, h:h + 1])
                yield
                act(s_[:, 0:1], tot[i2][:, 256:257], AF.Abs, [k(tot[i2])], [k(s_)])
                for dc in range(2):
                    pC, kC = nps()
                    mm(pC[:, 0:257], [(kw[i2][:, dc * 128:(dc + 1) * 128], vaug[:, c, hh, :])],
                       [k(kw[i2]), k(vaug)], [kC])
                    stt("dve", Cst[l][h][:, dc, :], Cst[l][h][:, dc, :], decb[:, h, c:c + 1], pC[:, 0:257],
                        ALU.mult, ALU.add, [k(Cst[l][h]), k(decb), kC], [k(Cst[l][h])])
                yield
                tt("dve", s_[:, 0:1], s_[:, 0:1], tok[:, c, 2, h:h + 1], ALU.max, [k(s_), k(tok)], [k(s_)])
                cp("pool", Cbf[l][h][:, :, :], Cst[l][h][:, :, :], [k(Cst[l][h])], [k(Cbf[l][h])])
                yield
                S_.op("dve", lambda: nc.vector.reciprocal(out=s_[:, 1:2], in_=s_[:, 0:1]), [k(s_)], [k(s_)])
                yield
                act(hb[i2][:, :], tot[i2][:, 0:256], AF.Square, [k(tot[i2]), k(s_)], [k(hb[i2]), k(s_, "ss")],
                    scale=s_[:, 1:2], accum_out=s_[:, 2:3])
                yield
                act(s_[:, 3:4], s_[:, 2:3], AF.Ln, [k(s_, "ss")], [k(s_, "r")], scale=1.0 / 256, bias=epsc[:, 0:1])
                yield
                act(s_[:, 3:4], s_[:, 3:4], AF.Exp, [k(s_, "r")], [k(s_, "r")], scale=-0.5)
                yield
                tt("dve", s_[:, 4:5], s_[:, 3:4], s_[:, 1:2], ALU.mult, [k(s_, "r"), k(s_)], [k(s_, "f")])
                yield
                stt("dve", hb[i2][:, :], tot[i2][:, 0:256], s_[:, 4:5], GO[:, c, hh * 256:(hh + 1) * 256],
                    ALU.mult, ALU.mult, [k(tot[i2]), k(s_, "f"), k(FC)], [k(hb[i2])])
                yield
                pb2, kb2 = npb()
                for ec in range(2):
                    tr(pb2[:, ec * 128:(ec + 1) * 128], hb[i2][:, ec * 128:(ec + 1) * 128], identb[:, :],
                       [k(hb[i2]), k(identb)], [kb2])
                yield
                cp("act", actG[:, hh * 2:hh * 2 + 2, sl], pb2[:, 0:256].rearrange("p (e t) -> p e t", e=2),
                   [kb2], [k(actG)])
                yield

        def conv_chunk(l, j, e):
            acc = FA[:, j, 0:T]
            a_in = FB
            w0 = PP_CONVW + j * 31
            ts(e, acc, a_in[:, j, 0:T], pp[l][:, w0:w0 + 1], pp[l][:, PP_CONVB + j:PP_CONVB + j + 1], ALU.mult, ALU.add,
               [k(FB, j), k(pp[l])], [k(FA, j)])
            yield
            for t_ in range(1, 31):
                stt(e, acc, a_in[:, j, t_:t_ + T], pp[l][:, w0 + t_:w0 + t_ + 1], acc, ALU.mult, ALU.add,
                    [k(FB, j), k(pp[l]), k(FA, j)], [k(FA, j)])
                yield

        def interleave(gens):
            gens = list(gens)
            while gens:
                for g in list(gens):
                    try:
                        next(g)
                    except StopIteration:
                        gens.remove(g)

        def group_B(l, hg, pre_out=None):
            for hh in range(2):
                h = hg * 2 + hh
                cp("act", Cbf[l][h][:, :, :], Cst[l][h][:, :, :], [k(Cst[l][h])], [k(Cbf[l][h])])
            def zphase():
                wv, wk_ = w_next()
                qk_block(l, wv, wk_, qT, hg * 4)
                w_issue()
                wv, wk_ = w_next()
                qk_block(l, wv, wk_, kT, 8 + hg * 4)
                w_issue()
                wv, wk_ = w_next()
                for c in range(NCH):
                    p, pk = nps()
                    mm(p[:, :], [(hT[:, kk, c * 128:(c + 1) * 128], wv[:, kk, :]) for kk in range(KC)], [wk_, k(hT, c)], [pk])
                    cp("act", vaug[:, c, :, 0:256], p[:, :].rearrange("p (h e) -> p h e", h=2), [pk], [k(vaug)])
                w_issue()
                wv, wk_ = w_next()
                for c in range(NCH):
                    p, pk = nps()
                    mm(p[:, :], [(hT[:, kk, c * 128:(c + 1) * 128], wv[:, kk, :]) for kk in range(KC)], [wk_, k(hT, c)], [pk])
                    act(FC[:, c, 0:512], p[:, :], AF.Sigmoid, [pk], [k(FC)])
                    tt("pool", FC[:, c, 0:512], FC[:, c, 0:512], pbt[:, PB_MNG + hg * 512:PB_MNG + (hg + 1) * 512], ALU.mult,
                       [k(FC), k(pbt)], [k(FC)])
                w_issue()

            def drain(g):
                for _ in g:
                    pass
            COOP.run([zphase, lambda: drain(conv_chunk(l, 2 * hg, "dve"))], weights=[1, 3], pools=[None, None])
            fns = [lambda: drain(mlstm_head(l, hg, 0)), lambda: drain(mlstm_head(l, hg, 1)),
                   lambda: drain(conv_chunk(l, 2 * hg + 1, "dve")),
                   (lambda: C_main(l)) if hg == 0 else (lambda: C_out(l))]
            pools = [{"ps": [0, 1], "pb": [0], "psi": 0, "pbi": 0}, {"ps": [2, 3], "pb": [1], "psi": 0, "pbi": 0},
                     None, {"ps": [4, 5], "pb": [0], "psi": 0, "pbi": 0}]
            COOP.run(fns, weights=[1, 1, 4, 1], pools=pools)
            st_ = pre_out() if pre_out is not None else None
            wv, wk_ = w_next()
            accum_into_x(actG, k(actG), wv, wk_)
            w_issue()
            return st_

        def group_A_z(l):
            sg = FA
            a_in = FB
            wv, wk_ = w_next()
            for j in range(4):
                p, pk = nps()
                mm(p[:, 0:T], z_fm(wv, wk_, j, p), [wk_] + hT_keys(), [pk])
                act(sg[:, j, 0:T], p[:, 0:T], AF.Sigmoid, [pk], [k(FA, j)])
            w_issue()
            wv, wk_ = w_next()
            for j in range(4):
                p, pk = nps()
                mm(p[:, 0:T], z_fm(wv, wk_, j, p), [wk_] + hT_keys(), [pk])
                cp("pool", a_in[:, j, 0:30], histA[l][:, j, :], [k(histA[l], j)], [k(FB, j)])
                tt("dve", a_in[:, j, 30:30 + T], p[:, 0:T], sg[:, j, 0:T], ALU.mult, [pk, k(FA, j)], [k(FB, j)])
                cp("pool", histA[l][:, j, :], a_in[:, j, T:T + 30], [k(FB, j)], [k(histA[l], j)])
            w_issue()

        def A_fin_p1(l):
            sg = FA
            a_in = FB
            for j in range(4):
                act(a_in[:, j, 0:T], sg[:, j, 0:T], AF.Square, [k(FA, j)], [k(FB, j)])
            p1, k1 = nps()
            p2, k2 = nps()
            ones = cst[:, CS_ONES:CS_ONES + 128]
            mm(p1[:, 0:T], [(ones, sg[:, j, 0:T]) for j in range(4)], [k(cst)] + [k(FA, j) for j in range(4)], [k1])
            mm(p2[:, 0:T], [(ones, a_in[:, j, 0:T]) for j in range(4)], [k(cst)] + [k(FB, j) for j in range(4)], [k2])
            mean = FC[:, 0, 0:T]
            var = FC[:, 1, 0:T]
            tmp = FC[:, 2, 0:T]
            ts("dve", mean, p1[:, 0:T], 1.0 / 512, None, ALU.mult, None, [k1], [k(FC)])
            tt("dve", tmp, mean, mean, ALU.mult, [k(FC)], [k(FC)])
            stt("dve", var, p2[:, 0:T], 1.0 / 512, tmp, ALU.mult, ALU.subtract, [k2, k(FC)], [k(FC)])
            rsqrt(var, var, 1.0, EPS, [k(FC)], [k(FC)])
            return None

        def A_fin_p2(l, st_):
            sg = FA
            actA = gT[1]
            mean = FC[:, 0, 0:T]
            var = FC[:, 1, 0:T]
            for j in range(4):
                e = "dve" if j % 2 == 0 else "pool"
                tt(e, sg[:, j, 0:T], sg[:, j, 0:T], mean, ALU.subtract, [k(FA, j), k(FC)], [k(FA, j)])
                tt(e, sg[:, j, 0:T], sg[:, j, 0:T], var, ALU.mult, [k(FA, j), k(FC)], [k(FA, j)])
                act(actA[:, j, :], sg[:, j, 0:T], AF.Silu, [k(FA, j), k(pp[l])], [k(gT[1])],
                    scale=pp[l][:, PP_CNG + j:PP_CNG + j + 1], bias=pp[l][:, PP_CNB + j:PP_CNB + j + 1])
            wv, wk_ = w_next()
            accum_into_x(actA, k(gT[1]), wv, wk_, hook=norm_hook(l, PP_LNMLP))
            w_issue()

        def C_main(l):
            guT = gT[0]
            gvn = gT[1]
            gng = zq[0][:, 0:512]
            gnb = zq[1][:, 0:512]
            gmb = cacc[0][:, 0:512]
            g_ = cacc[1][:, 0:512]
            jk = xs[0]
            S_.dma("sp", "pbc0", gng, pb_d[l:l + 1, PB_GNG:PB_GNG + 512].partition_broadcast(128), reads=(), writes=[k(zq[0])])
            S_.dma("sp", "pbc1", gnb, pb_d[l:l + 1, PB_GNB:PB_GNB + 512].partition_broadcast(128), reads=(), writes=[k(zq[1])])
            wv, wk_ = w_next()
            for c in range(NCH):
                p, pk = nps()
                mm(p[:, :], [(hT[:, kk, c * 128:(c + 1) * 128], wv[:, kk, :]) for kk in range(KC)], [wk_, k(hT, c)], [pk])
                act(g_, p[:, :], AF.Gelu, [pk], [k(cacc[1]), k(cs1, c, 0)], accum_out=cs1[:, 4 * c:4 * c + 1])
                act(jk[:, 0:512], g_, AF.Square, [k(cacc[1])], [k(jk), k(cs1, c, 1)], accum_out=cs1[:, 4 * c + 1:4 * c + 2])
                m_ = cs1[:, 4 * c:4 * c + 1]
                q1 = cs1[:, 4 * c + 1:4 * c + 2]
                v_ = cs1[:, 4 * c + 3:4 * c + 4]
                kc_ = [k(cs1, c, i) for i in range(2)]
                ts("dve", m_, m_, 1.0 / 512, None, ALU.mult, None, kc_, [k(cs1, c, 0)])
                tt("dve", v_, m_, m_, ALU.mult, kc_, [k(cs1, c, 3)])
                stt("dve", v_, q1, 1.0 / 512, v_, ALU.mult, ALU.subtract, kc_ + [k(cs1, c, 3)], [k(cs1, c, 3)])
                rsqrt(v_, v_, 1.0, EPS, [k(cs1, c, 3)], [k(cs1, c, 3)])
                ts("dve", g_, g_, m_, v_, ALU.subtract, ALU.mult, [k(cacc[1]), k(cs1, c, 0), k(cs1, c, 3)], [k(cacc[1])])
                tt("pool", g_, g_, gng, ALU.mult, [k(cacc[1]), k(zq[0])], [k(cacc[1])])
                tt("dve", gvn[:, c, :], g_, gnb, ALU.add, [k(cacc[1]), k(zq[1])], [k(gT[1])])
            w_issue()

        def C_main_b(l):
            guT = gT[0]
            gvn = gT[1]
            gmb = cacc[0][:, 0:512]
            g_ = cacc[1][:, 0:512]
            S_.dma("sp", "pbc2", gmb, pb_d[l:l + 1, PB_GMB:PB_GMB + 512].partition_broadcast(128), reads=(), writes=[k(cacc[0])])
            wv, wk_ = w_next()
            for j in range(4):
                p, pk = nps()
                mm(p[:, 0:T], z_fm(wv, wk_, j, p), [wk_] + hT_keys(), [pk])
                act(guT[:, j, 0:T], p[:, 0:T], AF.Gelu, [pk], [k(gT[0])])
            w_issue()
            for c in range(NCH):
                sl = slice(c * 128, (c + 1) * 128)
                p, pk = nps()

                def emit4(p=p, c=c):
                    ins = None
                    for g in range(4):
                        ins = nc.tensor.matmul(p[:, g * 128:(g + 1) * 128], gvn[:, c, g * 128:(g + 1) * 128], WcT[l][:, g, :],
                                               start=True, stop=True)
                    return ins
                S_.op("pe", emit4, [k(gT[1]), k(WcT[l])], [pk])
                tt("dve", g_, p[:, :], gmb, ALU.add, [pk, k(cacc[0])], [k(cacc[1])])
                tt("dve", guT[:, :, sl], g_.rearrange("p (g t) -> p g t", g=4), guT[:, :, sl], ALU.mult,
                   [k(cacc[1]), k(gT[0])], [k(gT[0])])

        def C_out(l):
            C_main_b(l)
            wv, wk_ = w_next()
            accum_into_x(gT[0], k(gT[0]), wv, wk_)
            w_issue()

        def ffn(l, nxt=None, tt_=0):
            NB_ = DFF // 512
            if nxt is not None:
                wg_prefetch(nxt)
            if l == depth - 1:
                fg_prefetch()
            for jf in range(NB_):
                g_ = gT[jf % 2]
                wv, wk_ = w_next()
                for j in range(4):
                    r_ = rt[j % 2]
                    p, pk = nps()
                    mm(p[:, 0:T], z_fm(wv, wk_, j, p), [wk_] + hT_keys(), [pk])
                    act(r_[:, :], p[:, 0:T], AF.Relu, [pk], [k(r_)])
                    tt("pool", g_[:, j, :], r_[:, :], r_[:, :], ALU.mult, [k(r_)], [k(g_)])
                w_issue()
                wv, wk_ = w_next()
                hk = None
                if jf == NB_ - 1:
                    hk = norm_hook(l + 1, PP_LNMIX) if l + 1 < depth else (lambda c: final_chunk(tt_, c))
                accum_into_x(g_, k(g_), wv, wk_, hook=hk)
                w_issue()

        fgt = FA[:, :, :].rearrange("p a b -> p (a b)")[:, 0:D]

        def fg_prefetch():
            S_.dma("sp", "fg", fgt, fg_d[0:1, :].partition_broadcast(128), reads=(), writes=[k(FA, j) for j in range(4)])

        def final_chunk(tt_, c):
            o_t = FB if c % 2 == 0 else FC
            o_ = o_t[:, :, :].rearrange("p a b -> p (a b)")[:, 0:D]
            okeys = [k(FB, j) for j in range(4)] if c % 2 == 0 else [k(FC)]
            act(xs0[:, :], x[:, c, :], AF.Square, [k(x, c)], [k(xs0), k(ss, c)], accum_out=ss[:, c:c + 1])
            rsqrt(rstd[:, c:c + 1], ss[:, c:c + 1], 1.0 / D, EPS, [k(ss, c)], [k(rstd, c)])
            stt("dve", o_, x[:, c, :], rstd[:, c:c + 1], fgt, ALU.mult, ALU.mult,
                [k(x, c), k(rstd, c)] + [k(FA, j) for j in range(4)], okeys)
            r0 = tt_ * T + c * 128
            S_.dma("sp", f"st{c % 2}", y_d[r0:r0 + 128, :], o_, reads=okeys, writes=())

        wg_prefetch(0)
        for tt_ in range(NT):
            for c in range(NCH):
                r0 = tt_ * T + c * 128
                S_.dma("sp", f"xin{c}", x[:, c, :], x_d[r0:r0 + 128, :], reads=(), writes=[k(x, c)])
            for l in range(depth):
                S_.dma("sp", "pbt", pbt[:, :], pb_d[l:l + 1, 0:1024].partition_broadcast(128), reads=(), writes=[k(pbt)])
                if l == 0:
                    norm_to_hT(l, PP_LNMIX)
                gates_p1(l)
                group_A_z(l)
                gates_p2(l)
                group_B(l, 0)
                st_ = group_B(l, 1, pre_out=lambda: A_fin_p1(l))
                A_fin_p2(l, st_)
                nxt = l + 1 if l + 1 < depth else (0 if tt_ + 1 < NT else None)
                ffn(l, nxt, tt_)
        S_.wait_all("sp", ["st0", "st1"])
        for e in ["act", "dve", "pool", "pe"]:
            S_.wait_all(e, ["st0", "st1"])
        assert wpos["used"] == len(wstream), (wpos, len(wstream))
    return nc


def _host_layout(inputs, depth=DEPTH):
    f = lambda a: np.ascontiguousarray(np.asarray(a, dtype=np.float32))
    pp = np.zeros((depth, 128, NPP), np.float32)
    pb = np.zeros((depth, NPB), np.float32)
    for l in range(depth):
        pp[l, :, PP_LNMIX:PP_LNMIX + 16] = f(inputs["ln_mix_g"])[l].reshape(16, 128).T
        pp[l, :, PP_LNMLP:PP_LNMLP + 16] = f(inputs["ln_mlp_g"])[l].reshape(16, 128).T
        cw = f(inputs["conv_w"])[l]
        pp[l, :, PP_CONVW:PP_CONVW + 124] = cw.T.reshape(4, 128, 31).transpose(1, 0, 2).reshape(128, 124)
        pp[l, :, PP_CONVB:PP_CONVB + 4] = f(inputs["conv_b"])[l].reshape(4, 128).T
        pp[l, :, PP_CNG:PP_CNG + 4] = f(inputs["conv_norm_g"])[l].reshape(4, 128).T
        pp[l, :, PP_CNB:PP_CNB + 4] = f(inputs["conv_norm_b"])[l].reshape(4, 128).T
        qw = f(inputs["qk_conv_w"])[l]
        pp[l, :, PP_QKW:PP_QKW + 64] = qw.T.reshape(16, 128, 4).transpose(1, 0, 2).reshape(128, 64)
        pp[l, :, PP_QKB:PP_QKB + 16] = f(inputs["qk_conv_b"])[l].reshape(16, 128).T
        pp[l, 0:4, PP_IGB] = f(inputs["igate_b"])[l]
        pp[l, 0:4, PP_FGB] = f(inputs["fgate_b"])[l]
        pb[l, PB_MNG:PB_MNG + 1024] = f(inputs["mlstm_norm_g"])[l]
        pb[l, PB_GNG:PB_GNG + 512] = f(inputs["gm_norm_g"])[l]
        pb[l, PB_GNB:PB_GNB + 512] = f(inputs["gm_norm_b"])[l]
        pb[l, PB_GMB:PB_GMB + 512] = f(inputs["gm_b"])[l].reshape(512)
    gmwT = np.ascontiguousarray(f(inputs["gm_w"])[:depth].transpose(0, 3, 1, 2))
    cst = np.zeros((128, NCST), np.float32)
    cst[:, CS_IDENT:CS_IDENT + 128] = np.eye(128, dtype=np.float32)
    cst[:, CS_MASK:CS_MASK + 128] = np.tril(np.ones((128, 128), np.float32))
    cst[:, CS_MASKT:CS_MASKT + 128] = np.triu(np.ones((128, 128), np.float32))
    cst[:, CS_ONES:CS_ONES + 128] = 1.0
    return {
        "w_in": f(inputs["w_in"])[:depth], "w_out": f(inputs["w_out"])[:depth],
        "w_up": f(inputs["w_up"])[:depth], "w_down": f(inputs["w_down"])[:depth],
        "pp": pp, "pb": pb, "final_g": f(inputs["final_g"]).reshape(1, D), "gm_wT": gmwT, "cst": cst,
    }


_NC_CACHE = {}


def kernel(**inputs):
    x = np.asarray(inputs["x"], dtype=np.float32)
    B, S, _ = x.shape
    T = 512
    key = (S, T)
    if key not in _NC_CACHE:
        _NC_CACHE[key] = build_nc(S, T)
    nc = _NC_CACHE[key]
    shared = _host_layout(inputs)
    in_maps = []
    for b in range(B):
        m = dict(shared)
        m["x"] = np.ascontiguousarray(x[b])
        in_maps.append(m)
    res = run_bass_kernel_spmd(nc, in_maps, core_ids=list(range(B)))
    return np.stack([np.asarray(r["y"], dtype=np.float32) for r in res.results], axis=0)
```

```python
import numpy as np
import concourse.bass as bass
import concourse.mybir as mybir
from concourse.bass_utils import run_bass_kernel_spmd

F32 = mybir.dt.float32
BF16 = mybir.dt.bfloat16
AF = mybir.ActivationFunctionType
ALU = mybir.AluOpType

D = 2048
NIN = 6152
DFF = 8192
DEPTH = 2
EPS = 1e-6
KC = 16
LN16 = float(np.log(16.0))

C_CV, C_CG, C_Q, C_K, C_V, C_O, C_G, C_GU, C_GV = 0, 512, 1024, 2048, 3072, 4096, 5120, 5128, 5640

PP_LNMIX = 0
PP_LNMLP = 16
PP_CONVW = 32
PP_CONVB = 156
PP_CNG = 160
PP_CNB = 164
PP_QKW = 168
PP_QKB = 232
PP_IGB = 248
PP_FGB = 249
NPP = 256
PB_MNG = 0
PB_GNG = 1024
PB_GNB = 1536
PB_GMB = 2048
NPB = 2560
CS_IDENT = 0
CS_MASK = 128
CS_MASKT = 256
CS_ONES = 384
NCST = 512


import threading

_tls = threading.local()


class Coop:
    def __init__(self):
        self.cv = threading.Condition()
        self.cur = -1
        self.done = []
        self.active = False

    def _next(self, i):
        n = len(self.done)
        for d in range(1, n + 1):
            j = (i + d) % n
            if not self.done[j]:
                return j
        return -1

    def switch(self):
        if not self.active:
            return
        i = getattr(_tls, "task", None)
        if i is None:
            return
        w = getattr(_tls, "weight", 1)
        with self.cv:
            for _ in range(w):
                j = self._next(i)
                if j == i or j < 0:
                    return
                self.cur = j
                self.cv.notify_all()
                while self.cur != i:
                    self.cv.wait()

    def run(self, fns, weights=None, pools=None):
        n = len(fns)
        self.done = [False] * n
        errs = []

        def worker(i):
            _tls.task = i
            _tls.weight = (weights or [1] * n)[i]
            _tls.pools = (pools or [None] * n)[i]
            with self.cv:
                while self.cur != i:
                    self.cv.wait()
            try:
                fns[i]()
            except BaseException as e:
                errs.append(e)
            finally:
                with self.cv:
                    self.done[i] = True
                    self.cur = self._next(i)
                    self.cv.notify_all()

        self.active = True
        self.cur = 0
        ths = [threading.Thread(target=worker, args=(i,)) for i in range(n)]
        for t in ths:
            t.start()
        for t in ths:
            t.join()
        self.active = False
        self.cur = -1
        if errs:
            raise errs[0]


COOP = Coop()


class Sched:
    def __init__(self, nc, sems):
        self.nc = nc
        self.eng = {"pe": nc.tensor, "act": nc.scalar, "dve": nc.vector, "pool": nc.gpsimd, "sp": nc.sync}
        self.sem = dict(sems)
        self.scale = {k: 1 for k in sems}
        self.cnt = {k: 0 for k in sems}
        self.seen = {e: {} for e in self.eng}
        self.res = {}

    def add_dma_sem(self, name, sem):
        self.sem[name] = sem
        self.scale[name] = 16
        self.cnt[name] = 0

    def _deps(self, e, reads, writes):
        deps = {}

        def add(p):
            f, idx = p
            if f == e and e == "pe":
                return
            if idx > deps.get(f, 0):
                deps[f] = idx

        for k in reads:
            r = self.res.get(k)
            if r and r[0]:
                add(r[0])
        for k in writes:
            r = self.res.get(k)
            if r:
                if r[0]:
                    add(r[0])
                for p in r[1].items():
                    add(p)
        for f, idx in deps.items():
            if idx > self.seen[e].get(f, 0):
                self.eng[e].wait_ge(self.sem[f], idx * self.scale[f])
                self.seen[e][f] = idx

    def _mark(self, who, idx, reads, writes):
        for k in reads:
            r = self.res.get(k)
            if r is None:
                r = [None, {}]
                self.res[k] = r
            r[1][who] = idx
        for k in writes:
            self.res[k] = [(who, idx), {}]

    def op(self, e, emit, reads=(), writes=()):
        COOP.switch()
        self._deps(e, reads, writes)
        ins = emit()
        self.cnt[e] += 1
        idx = self.cnt[e]
        ins.then_inc(self.sem[e], 1)
        self._mark(e, idx, reads, writes)

    def dma(self, q, dsem, out, in_, reads=(), writes=()):
        COOP.switch()
        self._deps(q, reads, writes)
        ins = self.eng[q].dma_start(out=out, in_=in_)
        self.cnt[dsem] += 1
        ins.then_inc(self.sem[dsem], 16)
        self._mark(dsem, self.cnt[dsem], reads, writes)

    def wait_all(self, e, names):
        for f in names:
            idx = self.cnt[f]
            if idx > self.seen[e].get(f, 0):
                self.eng[e].wait_ge(self.sem[f], idx * self.scale[f])
                self.seen[e][f] = idx


def build_nc(S, T, depth=DEPTH):
    NT = S // T
    NCH = T // 128
    nc = bass.Bass("TRN2", target_bir_lowering=False)
    x_d = nc.dram_tensor("x", [S, D], F32, kind="ExternalInput").ap()
    win_d = nc.dram_tensor("w_in", [depth, D, NIN], F32, kind="ExternalInput").ap()
    wout_d = nc.dram_tensor("w_out", [depth, D, D], F32, kind="ExternalInput").ap()
    wup_d = nc.dram_tensor("w_up", [depth, D, DFF], F32, kind="ExternalInput").ap()
    wdn_d = nc.dram_tensor("w_down", [depth, DFF, D], F32, kind="ExternalInput").ap()
    pp_d = nc.dram_tensor("pp", [depth, 128, NPP], F32, kind="ExternalInput").ap()
    pb_d = nc.dram_tensor("pb", [depth, NPB], F32, kind="ExternalInput").ap()
    fg_d = nc.dram_tensor("final_g", [1, D], F32, kind="ExternalInput").ap()
    gmw_d = nc.dram_tensor("gm_wT", [depth, 128, 4, 128], F32, kind="ExternalInput").ap()
    cst_d = nc.dram_tensor("cst", [128, NCST], F32, kind="ExternalInput").ap()
    y_d = nc.dram_tensor("y", [S, D], F32, kind="ExternalOutput").ap()

    import contextlib
    es = contextlib.ExitStack()
    with es:
        def sb(name, shape, dt):
            return es.enter_context(nc.sbuf_tensor("s_" + name, shape, dt))

        def psum(name, shape, dt):
            return es.enter_context(nc.psum_tensor("p_" + name, shape, dt))

        def semaphore(name):
            return es.enter_context(nc.semaphore("m_" + name))

        S_ = Sched(nc, {e: semaphore("s_" + e) for e in ["pe", "act", "dve", "pool", "sp"]})
        NW = 3
        for i in range(NW):
            S_.add_dma_sem(f"w{i}", semaphore(f"w{i}"))
        for nm in ["misc", "xin0", "xin1", "xin2", "xin3", "st0", "st1", "fg", "pbt", "pbc0", "pbc1", "pbc2", "wg"]:
            S_.add_dma_sem(nm, semaphore(nm))

        x = sb("x", [128, NCH, D], F32)
        hT = sb("hT", [128, KC, T], BF16)
        wbuf = [sb(f"wb{i}", [128, 8192], BF16) for i in range(NW)]
        wg = sb("wg", [128, KC, 8], BF16)
        cst = sb("cst", [128, NCST], F32)
        identb = sb("identb", [128, 128], BF16)
        pp = [sb(f"pp{l}", [128, NPP], F32) for l in range(depth)]
        pbt = sb("pbt", [128, 1024], F32)
        WcT = [sb(f"WcT{l}", [128, 4, 128], BF16) for l in range(depth)]
        nfb = [sb(f"nfb{l}", [4, 1], F32) for l in range(depth)]
        FA = sb("FA", [128, 4, 544], F32)
        FB = sb("FB", [128, 4, 544], F32)
        FC = sb("FC", [128, 4, 512], F32)
        actG = sb("actG", [128, 4, T], BF16)
        qT = sb("qT", [128, 4, T], BF16)
        kT = sb("kT", [128, 4, T], BF16)
        gT = [sb(f"gT{i}", [128, 4, T], BF16) for i in range(2)]
        vaug = sb("vaug", [128, NCH, 2, 257], BF16)
        xs0 = sb("xs0", [128, D], BF16)
        xs = [xs0, xs0]
        ss = sb("ss", [128, 16], F32)
        rstd = sb("rstd", [128, 16], F32)
        Mend = sb("Mend", [4, NCH + 1], F32)
        nMend = sb("nMend", [4, NCH + 1], F32)
        DEC = sb("DEC", [4, NCH], F32)
        sel4 = sb("sel4", [4, 4, 128], F32)
        Bcar = [sb(f"Bcar{l}", [4, 1], F32) for l in range(depth)]
        Mcar = [sb(f"Mcar{l}", [4, 1], F32) for l in range(depth)]
        uP = sb("uP", [4, T], F32)
        tok = sb("tok", [128, NCH, 4, 4], F32)
        decb = sb("decb", [128, 4, NCH], F32)
        Cst = [[sb(f"Cst{l}_{h}", [128, 2, 257], F32) for h in range(4)] for l in range(depth)]
        Cbf1 = [sb(f"Cbf_{h}", [128, 2, 257], BF16) for h in range(4)]
        Cbf = [Cbf1 for l in range(depth)]
        histA = [sb(f"histA{l}", [128, 4, 30], F32) for l in range(depth)]
        histQ = [sb(f"histQ{l}", [128, 16, 3], F32) for l in range(depth)]
        zq = [sb(f"zq{i}", [128, T + 3], F32) for i in range(2)]
        cacc = [sb(f"cacc{i}", [128, T], F32) for i in range(2)]
        ppb = [sb(f"ppb{i}", [128, 128], F32) for i in range(2)]
        pbf = [sb(f"pbf{i}", [128, 128], BF16) for i in range(2)]
        pTs = [sb(f"pTs{i}", [128, 128], BF16) for i in range(2)]
        tB = [sb(f"tB{i}", [128, 257], F32) for i in range(2)]
        tot = [sb(f"tot{i}", [128, 257], F32) for i in range(2)]
        sm = [sb(f"sm{i}", [128, 8], F32) for i in range(2)]
        hb = [sb(f"hb{i}", [128, 256], BF16) for i in range(2)]
        kw = [sb(f"kw{i}", [128, 256], BF16) for i in range(2)]
        rt = cacc
        gt = zq
        cs1 = sb("cs1", [128, 16], F32)
        epsc = sb("epsc", [128, 1], F32)
        gU = zq[0]
        gM = zq[1]
        gL = cacc[0]
        gB = cacc[1]

        ps = [psum(f"ps{i}", [128, 512], F32) for i in range(6)]
        pb = [psum(f"pbk{i}", [128, 1024], BF16) for i in range(2)]

        state = {"ps": 0, "pb": 0, "tmp": 0}

        def nps():
            pools = getattr(_tls, "pools", None)
            if pools is not None:
                lst = pools["ps"]
                i = lst[pools["psi"] % len(lst)]
                pools["psi"] += 1
                return ps[i], ("ps", i)
            i = state["ps"]
            state["ps"] = (i + 1) % 6
            return ps[i], ("ps", i)

        def npb():
            pools = getattr(_tls, "pools", None)
            if pools is not None:
                lst = pools["pb"]
                i = lst[pools["pbi"] % len(lst)]
                pools["pbi"] += 1
                return pb[i], ("pb", i)
            i = state["pb"]
            state["pb"] = (i + 1) % 2
            return pb[i], ("pb", i)

        def k(t, *idx):
            return (t.name,) + idx

        def mm(out_ap, pairs, reads, writes):
            def emit():
                n = len(pairs)
                ins = None
                for i, (l, r) in enumerate(pairs):
                    ins = nc.tensor.matmul(out_ap, l, r, start=(i == 0), stop=(i == n - 1))
                return ins
            S_.op("pe", emit, reads, writes)

        def tr(out_ap, in_ap, ident_ap, reads, writes):
            S_.op("pe", lambda: nc.tensor.transpose(out_ap, in_ap, ident_ap), reads, writes)

        def act(out, in_, func, reads, writes, **kw_):
            S_.op("act", lambda: nc.scalar.activation(out=out, in_=in_, func=func, **kw_), reads, writes)

        def ts(e, out, in0, s1, s2, op0, op1, reads, writes):
            eng = nc.vector if e == "dve" else nc.gpsimd
            if op1 is None:
                S_.op(e, lambda: eng.tensor_scalar(out=out, in0=in0, scalar1=s1, scalar2=None, op0=op0), reads, writes)
            else:
                S_.op(e, lambda: eng.tensor_scalar(out=out, in0=in0, scalar1=s1, scalar2=s2, op0=op0, op1=op1), reads, writes)

        def tt(e, out, in0, in1, op, reads, writes):
            eng = nc.vector if e == "dve" else nc.gpsimd
            S_.op(e, lambda: eng.tensor_tensor(out=out, in0=in0, in1=in1, op=op), reads, writes)

        def stt(e, out, in0, scalar, in1, op0, op1, reads, writes):
            eng = nc.vector if e == "dve" else nc.gpsimd
            S_.op(e, lambda: eng.scalar_tensor_tensor(out=out, in0=in0, scalar=scalar, in1=in1, op0=op0, op1=op1), reads, writes)

        def rsqrt(out, in_, scale, eps, reads, writes):
            act(out, in_, AF.Sqrt, reads, writes, scale=scale, bias=eps)
            S_.op("dve", lambda: nc.vector.reciprocal(out=out, in_=out), writes, writes)

        def cp(e, out, in_, reads, writes):
            if e == "act":
                S_.op("act", lambda: nc.scalar.copy(out=out, in_=in_), reads, writes)
            else:
                eng = nc.vector if e == "dve" else nc.gpsimd
                S_.op(e, lambda: eng.tensor_copy(out=out, in_=in_), reads, writes)

        def memset(e, ap, val, writes):
            eng = nc.vector if e == "dve" else nc.gpsimd
            S_.op(e, lambda: eng.memset(ap, val), (), writes)

        wstream = []
        for tt_ in range(NT):
            for l in range(depth):
                def wi(c0):
                    return ("in", win_d[l, :, c0:c0 + 512].rearrange("(k p) n -> p k n", p=128))

                def wo(r0):
                    return ("row", wout_d[l, r0:r0 + 512, :].rearrange("(k p) n -> p k n", p=128))
                seq = [wi(C_CG), wi(C_CV),
                       wi(C_Q), wi(C_K), wi(C_V), wi(C_O), wi(C_GV), wo(512),
                       wi(C_Q + 512), wi(C_K + 512), wi(C_V + 512), wi(C_O + 512), wi(C_GU), wo(1536), wo(1024),
                       wo(0)]
                for j in range(DFF // 512):
                    seq.append(("in", wup_d[l, :, j * 512:(j + 1) * 512].rearrange("(k p) n -> p k n", p=128)))
                    seq.append(("row", wdn_d[l, j * 512:(j + 1) * 512, :].rearrange("(k p) n -> p k n", p=128)))
                wstream.extend(seq)
        wpos = {"issued": 0, "used": 0}

        def w_issue():
            i = wpos["issued"]
            if i >= len(wstream):
                return
            kind, src = wstream[i]
            slot = i % NW
            if kind == "in":
                dst = wbuf[slot][:, :].rearrange("p (k n) -> p k n", k=16)
            else:
                dst = wbuf[slot][:, :].rearrange("p (k n) -> p k n", k=4)
            S_.dma("pool", f"w{slot}", dst, src, reads=(), writes=[("wb", slot)])
            wpos["issued"] = i + 1

        def w_next():
            i = wpos["used"]
            wpos["used"] = i + 1
            kind, _ = wstream[i]
            slot = i % NW
            if kind == "in":
                v = wbuf[slot][:, :].rearrange("p (k n) -> p k n", k=16)
            else:
                v = wbuf[slot][:, :].rearrange("p (k n) -> p k n", k=4)
            return v, ("wb", slot)

        S_.dma("sp", "misc", cst[:, :], cst_d[:, :], writes=[k(cst)])
        for l in range(depth):
            S_.dma("sp", "misc", pp[l][:, :], pp_d[l, :, :], writes=[k(pp[l])])
        for l in range(depth):
            S_.dma("sp", "misc", FC[:, :, l * 128:(l + 1) * 128], gmw_d[l, :, :, :], reads=(), writes=[k(FC)])
        for e in ["pe", "act", "dve", "pool"]:
            S_.wait_all(e, ["misc"])
        for i in range(NW):
            w_issue()
        cp("dve", identb[:, :], cst[:, CS_IDENT:CS_IDENT + 128], [k(cst)], [k(identb)])
        memset("dve", epsc[:, :], EPS, [k(epsc)])
        memset("dve", sel4[:, :, :], 0.0, [k(sel4)])
        for h in range(4):
            ts("dve", sel4[:, h, :], cst[0:4, CS_ONES:CS_ONES + 128], cst[0:4, CS_IDENT + h:CS_IDENT + h + 1], None, ALU.mult, None,
               [k(cst)], [k(sel4)])
        for l in range(depth):
            ts("dve", nfb[l][:, :], pp[l][0:4, PP_FGB:PP_FGB + 1], -1.0, None, ALU.mult, None, [k(pp[l])], [k(nfb[l])])
            memset("dve", Bcar[l][:, :], 0.0, [k(Bcar[l])])
            memset("dve", Mcar[l][:, :], 0.0, [k(Mcar[l])])
            memset("dve", histA[l][:, :, :], 0.0, [k(histA[l])])
            memset("dve", histQ[l][:, :, :], 0.0, [k(histQ[l])])
            for h in range(4):
                memset("dve", Cst[l][h][:, :, :], 0.0, [k(Cst[l][h])])
            for g in range(4):
                tt("dve", WcT[l][:, g, :], FC[:, g, l * 128:(l + 1) * 128], cst[:, CS_MASKT:CS_MASKT + 128], ALU.mult,
                   [k(FC), k(cst)], [k(WcT[l])])
        memset("dve", vaug[:, :, :, 256:257], 1.0, [k(vaug)])

        def norm_act(l, c):
            act(hT[:, :, c * 128:(c + 1) * 128], x[:, c, :].rearrange("p (a b) -> p a b", a=KC), AF.Square,
                [k(x, c)], [k(hT, c), k(ss, c)], accum_out=ss[:, c:c + 1])
            rsqrt(rstd[:, c:c + 1], ss[:, c:c + 1], 1.0 / D, EPS, [k(ss, c)], [k(rstd, c)])
            act(xs0[:, :], x[:, c, :], AF.Copy, [k(x, c), k(rstd, c)], [k(xs0)], scale=rstd[:, c:c + 1])

        def norm_pe(l, goff, c):
            xb = xs0
            for kq in range(4):
                pbk, pk = npb()
                for i in range(4):
                    kk = kq * 4 + i
                    tr(pbk[:, i * 128:(i + 1) * 128], xb[:, kk * 128:(kk + 1) * 128], identb[:, :],
                       [k(xb), k(identb)], [pk])
                gb = pp[l][:, goff + kq * 4:goff + kq * 4 + 4].unsqueeze(2).to_broadcast([128, 4, 128])
                tt("dve", hT[:, kq * 4:kq * 4 + 4, c * 128:(c + 1) * 128],
                   pbk[:, 0:512].rearrange("p (a b) -> p a b", a=4), gb, ALU.mult, [pk, k(pp[l])], [k(hT, c)])

        def norm_to_hT(l, goff):
            for c in range(NCH):
                norm_act(l, c)
                norm_pe(l, goff, c)

        def norm_hook(l, goff):
            def hook(c):
                if c >= 1:
                    norm_pe(l, goff, c - 1)
                norm_act(l, c)
                if c == NCH - 1:
                    norm_pe(l, goff, c)
            return hook

        def hT_keys():
            return [k(hT, c) for c in range(NCH)]

        def z_fm(wv, wk_, j, out_ps):
            pairs = [(wv[:, kk, j * 128:(j + 1) * 128], hT[:, kk, 0:T]) for kk in range(KC)]
            return pairs

        def accum_into_x(aT, akey, wv, wkey, hook=None):
            for c in range(NCH):
                for db in range(4):
                    p, pk = nps()
                    mm(p[:, :], [(aT[:, kc, c * 128:(c + 1) * 128], wv[:, kc, db * 512:(db + 1) * 512]) for kc in range(4)],
                       [akey, wkey], [pk])
                    tt("dve", x[:, c, db * 512:(db + 1) * 512], x[:, c, db * 512:(db + 1) * 512], p[:, :], ALU.add,
                       [pk, k(x, c)], [k(x, c)])
                if hook is not None:
                    hook(c)

        def wg_prefetch(l):
            S_.dma("pool", "wg", wg[:, :, :], win_d[l, :, C_G:C_G + 8].rearrange("(k p) n -> p k n", p=128),
                   reads=(), writes=[k(wg)])

        def gates_p1(l):
            pI, kI = nps()
            pF, kF = nps()
            mm(pI[0:4, 0:T], [(wg[:, kk, 0:4], hT[:, kk, 0:T]) for kk in range(KC)], [k(wg)] + hT_keys(), [kI])
            mm(pF[0:4, 0:T], [(wg[:, kk, 4:8], hT[:, kk, 0:T]) for kk in range(KC)], [k(wg)] + hT_keys(), [kF])
            act(gU[0:4, 0:T], pI[0:4, 0:T], AF.Identity, [kI, k(pp[l])], [k(gU)], bias=pp[l][0:4, PP_IGB:PP_IGB + 1])
            act(gL[0:4, 0:T], pF[0:4, 0:T], AF.Exp, [kF, k(nfb[l])], [k(gL)], scale=-1.0, bias=nfb[l][:, 0:1])
            act(gL[0:4, 0:T], gL[0:4, 0:T], AF.Ln, [k(gL)], [k(gL)], bias=1.0)
            S_.op("dve", lambda: nc.vector.tensor_tensor_scan(out=gB[0:4, 0:T], data0=cst[0:4, CS_ONES:CS_ONES + 1].to_broadcast([4, T]), data1=gL[0:4, 0:T],
                                                              initial=Bcar[l][:, 0:1], op0=ALU.mult, op1=ALU.subtract),
                  [k(cst), k(gL), k(Bcar[l])], [k(gB)])
            tt("dve", gU[0:4, 0:T], gU[0:4, 0:T], gB[0:4, 0:T], ALU.subtract, [k(gU), k(gB)], [k(gU)])
            S_.op("dve", lambda: nc.vector.tensor_tensor_scan(out=gM[0:4, 0:T], data0=cst[0:4, CS_ONES:CS_ONES + 1].to_broadcast([4, T]), data1=gU[0:4, 0:T],
                                                              initial=Mcar[l][:, 0:1], op0=ALU.mult, op1=ALU.max),
                  [k(cst), k(gU), k(Mcar[l])], [k(gM)])
            cp("dve", Mend[:, 0:1], Mcar[l][:, 0:1], [k(Mcar[l])], [k(Mend)])
            for c in range(NCH):
                cp("dve", Mend[:, c + 1:c + 2], gM[0:4, c * 128 + 127:c * 128 + 128], [k(gM)], [k(Mend)])
            cp("dve", Bcar[l][:, 0:1], gB[0:4, T - 1:T], [k(gB)], [k(Bcar[l])])
            cp("dve", Mcar[l][:, 0:1], gM[0:4, T - 1:T], [k(gM)], [k(Mcar[l])])
            ts("dve", nMend[:, :], Mend[:, :], -1.0, -LN16, ALU.mult, ALU.add, [k(Mend)], [k(nMend)])
            tt("dve", DEC[:, :], Mend[:, 0:NCH], Mend[:, 1:NCH + 1], ALU.subtract, [k(Mend)], [k(DEC)])
            act(DEC[:, :], DEC[:, :], AF.Exp, [k(DEC)], [k(DEC)])
            ts("dve", FC[0:4, 0, 0:T], gM[0:4, 0:T], -1.0, None, ALU.mult, None, [k(gM)], [k(FC)])
            tt("dve", FC[0:4, 2, 0:T], gB[0:4, 0:T], gM[0:4, 0:T], ALU.add, [k(gB), k(gM)], [k(FC)])
            act(FC[0:4, 2, 0:T], FC[0:4, 2, 0:T], AF.Exp, [k(FC)], [k(FC)], scale=-1.0)
            for c in range(NCH):
                sl = slice(c * 128, (c + 1) * 128)
                act(FC[0:4, 1, sl], gM[0:4, sl], AF.Exp, [k(gM), k(Mend)], [k(FC)], scale=-1.0, bias=Mend[:, c:c + 1])
                act(FC[0:4, 3, sl], gU[0:4, sl], AF.Exp, [k(gU), k(nMend)], [k(FC)], bias=nMend[:, c + 1:c + 2])
            cp("dve", uP[:, :], gU[0:4, 0:T], [k(gU)], [k(uP)])

        def gates_p2(l):
            pT_, kT_ = nps()
            for c in range(NCH):
                for qi in range(4):
                    o0 = (c * 4 + qi) * 4
                    tr(pT_[:, o0:o0 + 4], FC[0:4, qi, c * 128:(c + 1) * 128], cst[0:4, CS_IDENT:CS_IDENT + 4],
                       [k(FC), k(cst)], [kT_])
            cp("dve", tok[:, :, :, :].rearrange("p c q h -> p (c q h)"), pT_[:, 0:NCH * 16], [kT_], [k(tok)])
            pD, kD = nps()
            for h in range(4):
                mm(pD[:, h * NCH:(h + 1) * NCH], [(sel4[:, h, :], DEC[:, :])], [k(sel4), k(DEC)], [kD])
            cp("dve", decb[:, :, :].rearrange("p h c -> p (h c)"), pD[:, 0:4 * NCH], [kD], [k(decb)])
        def qk_block(l, wv, wkey, dstT, gj0):
            for j in range(4):
                gj = gj0 + j
                zb = zq[j % 2]
                ca = cacc[j % 2]
                p, pk = nps()
                mm(p[:, 0:T], z_fm(wv, wkey, j, p), [wkey] + hT_keys(), [pk])
                cp("dve", zb[:, 0:3], histQ[l][:, gj, :], [k(histQ[l], gj)], [k(zb)])
                cp("act", zb[:, 3:3 + T], p[:, 0:T], [pk], [k(zb)])
                cp("dve", histQ[l][:, gj, :], zb[:, T:T + 3], [k(zb)], [k(histQ[l], gj)])
                w0 = PP_QKW + gj * 4
                e = "dve"
                ts(e, ca[:, :], zb[:, 0:T], pp[l][:, w0:w0 + 1], pp[l][:, PP_QKB + gj:PP_QKB + gj + 1], ALU.mult, ALU.add,
                   [k(zb), k(pp[l])], [k(ca)])
                for t_ in range(1, 4):
                    stt(e, ca[:, :], zb[:, t_:t_ + T], pp[l][:, w0 + t_:w0 + t_ + 1], ca[:, :], ALU.mult, ALU.add,
                        [k(zb), k(pp[l]), k(ca)], [k(ca)])
                act(dstT[:, j, :], ca[:, :], AF.Silu, [k(ca)], [k(dstT)])

        def mlstm_head(l, hg, hh):
            GO = FC
            h = hg * 2 + hh
            i2 = hh
            s_ = sm[i2]
            st = {}

            def bufs(c):
                if c % 2 == 0:
                    return ppb[i2][:, :], k(ppb[i2]), pbf[i2][:, :], k(pbf[i2]), pTs[i2][:, :], k(pTs[i2])
                return tB[i2][:, 0:128], k(tB[i2]), kw[i2][:, 128:256], k(kw[i2]), kw[i2][:, 0:128], k(kw[i2])

            def front(c):
                sl = slice(c * 128, (c + 1) * 128)
                pp_, kpp, pf_, kpf, pt_, kpt = bufs(c)
                pS, kS = nps()
                mm(pS[:, 0:128], [(qT[:, hh * 2 + dc, sl], kT[:, hh * 2 + dc, sl]) for dc in range(2)],
                   [k(qT), k(kT)], [kS])
                mm(pS[:, 128:256], [(sel4[:, h, :], uP[:, sl])], [k(sel4), k(uP)], [kS])
                ts("dve", pp_, pS[:, 128:256], tok[:, c, 0, h:h + 1], 0.0, ALU.add, ALU.min, [kS, k(tok)], [kpp])
                yield
                act(pp_, pp_, AF.Exp, [kpp], [kpp])
                yield
                tt("pool", pp_, pp_, cst[:, CS_MASK:CS_MASK + 128], ALU.mult, [kpp, k(cst)], [kpp])
                yield
                stt("dve", pf_, pS[:, 0:128], 0.0625, pp_, ALU.mult, ALU.mult, [kS, kpp], [kpf])
                yield
                pb1, kb1 = npb()
                tr(pb1[:, 0:128], pf_, identb[:, :], [kpf, k(identb)], [kb1])
                yield
                cp("act", pt_, pb1[:, 0:128], [kb1], [kpt])
                yield

            def mid(c):
                sl = slice(c * 128, (c + 1) * 128)
                pp_, kpp, pf_, kpf, pt_, kpt = bufs(c)
                pB, kB = nps()
                mm(pB[:, 0:257], [(qT[:, hh * 2 + dc, sl], Cbf[l][h][:, dc, :]) for dc in range(2)],
                   [k(qT), k(Cbf[l][h])], [kB])
                pA, kA = nps()
                mm(pA[:, 0:257], [(pt_, vaug[:, c, hh, :])], [kpt, k(vaug)], [kA])
                yield
                act(tB[i2][:, :], pB[:, 0:257], AF.Copy, [kB, k(tok)], [k(tB[i2])], scale=tok[:, c, 1, h:h + 1])
                pb3, kb3 = npb()
                for dc in range(2):
                    tr(pb3[:, dc * 128:(dc + 1) * 128], kT[:, hh * 2 + dc, sl], identb[:, :], [k(kT), k(identb)], [kb3])
                yield
                tt("dve", tot[i2][:, :], tB[i2][:, :], pA[:, 0:257], ALU.add, [k(tB[i2]), kA], [k(tot[i2])])
                act(kw[i2][:, :], pb3[:, 0:256], AF.Copy, [kb3, k(tok)], [k(kw[i2])], scale=tok[:, c, 3, h:h + 1])
                yield
                for dc in range(2):
                    pC, kC = nps()
                    mm(pC[:, 0:257], [(kw[i2][:, dc * 128:(dc + 1) * 128], vaug[:, c, hh, :])],
                       [k(kw[i2]), k(vaug)], [kC])
                    stt("dve", Cst[l][h][:, dc, :], Cst[l][h][:, dc, :], decb[:, h, c:c + 1], pC[:, 0:257],
                        ALU.mult, ALU.add, [k(Cst[l][h]), k(decb), kC], [k(Cst[l][h])])
                    yield
                cp("pool", Cbf[l][h][:, :, :], Cst[l][h][:, :, :], [k(Cst[l][h])], [k(Cbf[l][h])])
                yield

            def tail(c):
                sl = slice(c * 128, (c + 1) * 128)
                act(s_[:, 0:1], tot[i2][:, 256:257], AF.Abs, [k(tot[i2])], [k(s_)])
                yield
                tt("dve", s_[:, 0:1], s_[:, 0:1], tok[:, c, 2, h:h + 1], ALU.max, [k(s_), k(tok)], [k(s_)])
                yield
                S_.op("dve", lambda: nc.vector.reciprocal(out=s_[:, 1:2], in_=s_[:, 0:1]), [k(s_)], [k(s_)])
                yield
                act(hb[i2][:, :], tot[i2][:, 0:256], AF.Square, [k(tot[i2]), k(s_)], [k(hb[i2]), k(s_, "ss")],
                    scale=s_[:, 1:2], accum_out=s_[:, 2:3])
                yield
                act(s_[:, 3:4], s_[:, 2:3], AF.Ln, [k(s_, "ss")], [k(s_, "r")], scale=1.0 / 256, bias=epsc[:, 0:1])
                yield
                act(s_[:, 3:4], s_[:, 3:4], AF.Exp, [k(s_, "r")], [k(s_, "r")], scale=-0.5)
                yield
                tt("dve", s_[:, 4:5], s_[:, 3:4], s_[:, 1:2], ALU.mult, [k(s_, "r"), k(s_)], [k(s_, "f")])
                yield
                stt("dve", hb[i2][:, :], tot[i2][:, 0:256], s_[:, 4:5], GO[:, c, hh * 256:(hh + 1) * 256],
                    ALU.mult, ALU.mult, [k(tot[i2]), k(s_, "f"), k(FC)], [k(hb[i2])])
                yield
                pb2, kb2 = npb()
                for ec in range(2):
                    tr(pb2[:, ec * 128:(ec + 1) * 128], hb[i2][:, ec * 128:(ec + 1) * 128], identb[:, :],
                       [k(hb[i2]), k(identb)], [kb2])
                yield
                cp("act", actG[:, hh * 2:hh * 2 + 2, sl], pb2[:, 0:256].rearrange("p (e t) -> p e t", e=2),
                   [kb2], [k(actG)])
                yield

            yield from front(0)
            for c in range(NCH):
                yield from mid(c)
                if c + 1 < NCH:
                    yield from front(c + 1)
                yield from tail(c)

        def conv_chunk(l, j, e):
            acc = FA[:, j, 0:T]
            a_in = FB
            w0 = PP_CONVW + j * 31
            ts(e, acc, a_in[:, j, 0:T], pp[l][:, w0:w0 + 1], pp[l][:, PP_CONVB + j:PP_CONVB + j + 1], ALU.mult, ALU.add,
               [k(FB, j), k(pp[l])], [k(FA, j)])
            yield
            for t_ in range(1, 31):
                stt(e, acc, a_in[:, j, t_:t_ + T], pp[l][:, w0 + t_:w0 + t_ + 1], acc, ALU.mult, ALU.add,
                    [k(FB, j), k(pp[l]), k(FA, j)], [k(FA, j)])
                yield

        def interleave(gens):
            gens = list(gens)
            while gens:
                for g in list(gens):
                    try:
                        next(g)
                    except StopIteration:
                        gens.remove(g)

        def group_B(l, hg, pre_out=None):
            for hh in range(2):
                h = hg * 2 + hh
                cp("act", Cbf[l][h][:, :, :], Cst[l][h][:, :, :], [k(Cst[l][h])], [k(Cbf[l][h])])
            def zphase():
                wv, wk_ = w_next()
                qk_block(l, wv, wk_, qT, hg * 4)
                w_issue()
                wv, wk_ = w_next()
                qk_block(l, wv, wk_, kT, 8 + hg * 4)
                w_issue()
                wv, wk_ = w_next()
                for c in range(NCH):
                    p, pk = nps()
                    mm(p[:, :], [(hT[:, kk, c * 128:(c + 1) * 128], wv[:, kk, :]) for kk in range(KC)], [wk_, k(hT, c)], [pk])
                    cp("act", vaug[:, c, :, 0:256], p[:, :].rearrange("p (h e) -> p h e", h=2), [pk], [k(vaug)])
                w_issue()
                wv, wk_ = w_next()
                for c in range(NCH):
                    p, pk = nps()
                    mm(p[:, :], [(hT[:, kk, c * 128:(c + 1) * 128], wv[:, kk, :]) for kk in range(KC)], [wk_, k(hT, c)], [pk])
                    act(FC[:, c, 0:512], p[:, :], AF.Sigmoid, [pk], [k(FC)])
                    tt("pool", FC[:, c, 0:512], FC[:, c, 0:512], pbt[:, PB_MNG + hg * 512:PB_MNG + (hg + 1) * 512], ALU.mult,
                       [k(FC), k(pbt)], [k(FC)])
                w_issue()

            def drain(g):
                for _ in g:
                    pass
            COOP.run([zphase, lambda: drain(conv_chunk(l, 2 * hg, "dve"))], weights=[1, 3], pools=[None, None])
            fns = [lambda: drain(mlstm_head(l, hg, 0)), lambda: drain(mlstm_head(l, hg, 1)),
                   lambda: drain(conv_chunk(l, 2 * hg + 1, "dve")),
                   (lambda: C_main(l)) if hg == 0 else (lambda: C_out(l))]
            pools = [{"ps": [0, 1], "pb": [0], "psi": 0, "pbi": 0}, {"ps": [2, 3], "pb": [1], "psi": 0, "pbi": 0},
                     None, {"ps": [4, 5], "pb": [0], "psi": 0, "pbi": 0}]
            COOP.run(fns, weights=[1, 1, 4, 1], pools=pools)
            st_ = pre_out() if pre_out is not None else None
            wv, wk_ = w_next()
            accum_into_x(actG, k(actG), wv, wk_)
            w_issue()
            return st_

        def group_A_z(l):
            sg = FA
            a_in = FB
            wv, wk_ = w_next()
            for j in range(4):
                p, pk = nps()
                mm(p[:, 0:T], z_fm(wv, wk_, j, p), [wk_] + hT_keys(), [pk])
                act(sg[:, j, 0:T], p[:, 0:T], AF.Sigmoid, [pk], [k(FA, j)])
            w_issue()
            wv, wk_ = w_next()
            for j in range(4):
                p, pk = nps()
                mm(p[:, 0:T], z_fm(wv, wk_, j, p), [wk_] + hT_keys(), [pk])
                cp("pool", a_in[:, j, 0:30], histA[l][:, j, :], [k(histA[l], j)], [k(FB, j)])
                tt("dve", a_in[:, j, 30:30 + T], p[:, 0:T], sg[:, j, 0:T], ALU.mult, [pk, k(FA, j)], [k(FB, j)])
                cp("pool", histA[l][:, j, :], a_in[:, j, T:T + 30], [k(FB, j)], [k(histA[l], j)])
            w_issue()

        def A_fin_p1(l):
            sg = FA
            a_in = FB
            for j in range(4):
                act(a_in[:, j, 0:T], sg[:, j, 0:T], AF.Square, [k(FA, j)], [k(FB, j)])
            p1, k1 = nps()
            p2, k2 = nps()
            ones = cst[:, CS_ONES:CS_ONES + 128]
            mm(p1[:, 0:T], [(ones, sg[:, j, 0:T]) for j in range(4)], [k(cst)] + [k(FA, j) for j in range(4)], [k1])
            mm(p2[:, 0:T], [(ones, a_in[:, j, 0:T]) for j in range(4)], [k(cst)] + [k(FB, j) for j in range(4)], [k2])
            mean = FC[:, 0, 0:T]
            var = FC[:, 1, 0:T]
            tmp = FC[:, 2, 0:T]
            ts("dve", mean, p1[:, 0:T], 1.0 / 512, None, ALU.mult, None, [k1], [k(FC)])
            tt("dve", tmp, mean, mean, ALU.mult, [k(FC)], [k(FC)])
            stt("dve", var, p2[:, 0:T], 1.0 / 512, tmp, ALU.mult, ALU.subtract, [k2, k(FC)], [k(FC)])
            rsqrt(var, var, 1.0, EPS, [k(FC)], [k(FC)])
            return None

        def A_fin_p2(l, st_):
            sg = FA
            actA = gT[1]
            mean = FC[:, 0, 0:T]
            var = FC[:, 1, 0:T]
            for j in range(4):
                e = "dve" if j % 2 == 0 else "pool"
                tt(e, sg[:, j, 0:T], sg[:, j, 0:T], mean, ALU.subtract, [k(FA, j), k(FC)], [k(FA, j)])
                tt(e, sg[:, j, 0:T], sg[:, j, 0:T], var, ALU.mult, [k(FA, j), k(FC)], [k(FA, j)])
                act(actA[:, j, :], sg[:, j, 0:T], AF.Silu, [k(FA, j), k(pp[l])], [k(gT[1])],
                    scale=pp[l][:, PP_CNG + j:PP_CNG + j + 1], bias=pp[l][:, PP_CNB + j:PP_CNB + j + 1])
            wv, wk_ = w_next()
            accum_into_x(actA, k(gT[1]), wv, wk_, hook=norm_hook(l, PP_LNMLP))
            w_issue()

        def C_main(l):
            guT = gT[0]
            gvn = gT[1]
            gng = zq[0][:, 0:512]
            gnb = zq[1][:, 0:512]
            gmb = cacc[0][:, 0:512]
            g_ = cacc[1][:, 0:512]
            jk = xs[0]
            S_.dma("sp", "pbc0", gng, pb_d[l:l + 1, PB_GNG:PB_GNG + 512].partition_broadcast(128), reads=(), writes=[k(zq[0])])
            S_.dma("sp", "pbc1", gnb, pb_d[l:l + 1, PB_GNB:PB_GNB + 512].partition_broadcast(128), reads=(), writes=[k(zq[1])])
            wv, wk_ = w_next()
            for c in range(NCH):
                p, pk = nps()
                mm(p[:, :], [(hT[:, kk, c * 128:(c + 1) * 128], wv[:, kk, :]) for kk in range(KC)], [wk_, k(hT, c)], [pk])
                act(g_, p[:, :], AF.Gelu, [pk], [k(cacc[1]), k(cs1, c, 0)], accum_out=cs1[:, 4 * c:4 * c + 1])
                act(jk[:, 0:512], g_, AF.Square, [k(cacc[1])], [k(jk), k(cs1, c, 1)], accum_out=cs1[:, 4 * c + 1:4 * c + 2])
                m_ = cs1[:, 4 * c:4 * c + 1]
                q1 = cs1[:, 4 * c + 1:4 * c + 2]
                v_ = cs1[:, 4 * c + 3:4 * c + 4]
                kc_ = [k(cs1, c, i) for i in range(2)]
                ts("dve", m_, m_, 1.0 / 512, None, ALU.mult, None, kc_, [k(cs1, c, 0)])
                tt("dve", v_, m_, m_, ALU.mult, kc_, [k(cs1, c, 3)])
                stt("dve", v_, q1, 1.0 / 512, v_, ALU.mult, ALU.subtract, kc_ + [k(cs1, c, 3)], [k(cs1, c, 3)])
                rsqrt(v_, v_, 1.0, EPS, [k(cs1, c, 3)], [k(cs1, c, 3)])
                ts("dve", g_, g_, m_, v_, ALU.subtract, ALU.mult, [k(cacc[1]), k(cs1, c, 0), k(cs1, c, 3)], [k(cacc[1])])
                tt("pool", g_, g_, gng, ALU.mult, [k(cacc[1]), k(zq[0])], [k(cacc[1])])
                tt("dve", gvn[:, c, :], g_, gnb, ALU.add, [k(cacc[1]), k(zq[1])], [k(gT[1])])
            w_issue()

        def C_main_b(l):
            guT = gT[0]
            gvn = gT[1]
            gmb = cacc[0][:, 0:512]
            g_ = cacc[1][:, 0:512]
            S_.dma("sp", "pbc2", gmb, pb_d[l:l + 1, PB_GMB:PB_GMB + 512].partition_broadcast(128), reads=(), writes=[k(cacc[0])])
            wv, wk_ = w_next()
            for j in range(4):
                p, pk = nps()
                mm(p[:, 0:T], z_fm(wv, wk_, j, p), [wk_] + hT_keys(), [pk])
                act(guT[:, j, 0:T], p[:, 0:T], AF.Gelu, [pk], [k(gT[0])])
            w_issue()
            for c in range(NCH):
                sl = slice(c * 128, (c + 1) * 128)
                p, pk = nps()

                def emit4(p=p, c=c):
                    ins = None
                    for g in range(4):
                        ins = nc.tensor.matmul(p[:, g * 128:(g + 1) * 128], gvn[:, c, g * 128:(g + 1) * 128], WcT[l][:, g, :],
                                               start=True, stop=True)
                    return ins
                S_.op("pe", emit4, [k(gT[1]), k(WcT[l])], [pk])
                tt("dve", g_, p[:, :], gmb, ALU.add, [pk, k(cacc[0])], [k(cacc[1])])
                tt("dve", guT[:, :, sl], g_.rearrange("p (g t) -> p g t", g=4), guT[:, :, sl], ALU.mult,
                   [k(cacc[1]), k(gT[0])], [k(gT[0])])

        def C_out(l):
            C_main_b(l)
            wv, wk_ = w_next()
            accum_into_x(gT[0], k(gT[0]), wv, wk_)
            w_issue()

        def ffn(l, nxt=None, tt_=0):
            NB_ = DFF // 512
            if nxt is not None:
                wg_prefetch(nxt)
            if l == depth - 1:
                fg_prefetch()
            for jf in range(NB_):
                g_ = gT[jf % 2]
                wv, wk_ = w_next()
                for j in range(4):
                    r_ = rt[j % 2]
                    p, pk = nps()
                    mm(p[:, 0:T], z_fm(wv, wk_, j, p), [wk_] + hT_keys(), [pk])
                    act(r_[:, :], p[:, 0:T], AF.Relu, [pk], [k(r_)])
                    tt("pool", g_[:, j, :], r_[:, :], r_[:, :], ALU.mult, [k(r_)], [k(g_)])
                w_issue()
                wv, wk_ = w_next()
                hk = None
                if jf == NB_ - 1:
                    hk = norm_hook(l + 1, PP_LNMIX) if l + 1 < depth else (lambda c: final_chunk(tt_, c))
                accum_into_x(g_, k(g_), wv, wk_, hook=hk)
                w_issue()

        fgt = FA[:, :, :].rearrange("p a b -> p (a b)")[:, 0:D]

        def fg_prefetch():
            S_.dma("sp", "fg", fgt, fg_d[0:1, :].partition_broadcast(128), reads=(), writes=[k(FA, j) for j in range(4)])

        def final_chunk(tt_, c):
            o_t = FB if c % 2 == 0 else FC
            o_ = o_t[:, :, :].rearrange("p a b -> p (a b)")[:, 0:D]
            okeys = [k(FB, j) for j in range(4)] if c % 2 == 0 else [k(FC)]
            act(xs0[:, :], x[:, c, :], AF.Square, [k(x, c)], [k(xs0), k(ss, c)], accum_out=ss[:, c:c + 1])
            rsqrt(rstd[:, c:c + 1], ss[:, c:c + 1], 1.0 / D, EPS, [k(ss, c)], [k(rstd, c)])
            stt("dve", o_, x[:, c, :], rstd[:, c:c + 1], fgt, ALU.mult, ALU.mult,
                [k(x, c), k(rstd, c)] + [k(FA, j) for j in range(4)], okeys)
            r0 = tt_ * T + c * 128
            S_.dma("sp", f"st{c % 2}", y_d[r0:r0 + 128, :], o_, reads=okeys, writes=())

        wg_prefetch(0)
        for tt_ in range(NT):
            for c in range(NCH):
                r0 = tt_ * T + c * 128
                S_.dma("sp", f"xin{c}", x[:, c, :], x_d[r0:r0 + 128, :], reads=(), writes=[k(x, c)])
            for l in range(depth):
                S_.dma("sp", "pbt", pbt[:, :], pb_d[l:l + 1, 0:1024].partition_broadcast(128), reads=(), writes=[k(pbt)])
                if l == 0:
                    norm_to_hT(l, PP_LNMIX)
                gates_p1(l)
                group_A_z(l)
                gates_p2(l)
                group_B(l, 0)
                st_ = group_B(l, 1, pre_out=lambda: A_fin_p1(l))
                A_fin_p2(l, st_)
                nxt = l + 1 if l + 1 < depth else (0 if tt_ + 1 < NT else None)
                ffn(l, nxt, tt_)
        S_.wait_all("sp", ["st0", "st1"])
        for e in ["act", "dve", "pool", "pe"]:
            S_.wait_all(e, ["st0", "st1"])
        assert wpos["used"] == len(wstream), (wpos, len(wstream))
    return nc


def _host_layout(inputs, depth=DEPTH):
    f = lambda a: np.ascontiguousarray(np.asarray(a, dtype=np.float32))
    pp = np.zeros((depth, 128, NPP), np.float32)
    pb = np.zeros((depth, NPB), np.float32)
    for l in range(depth):
        pp[l, :, PP_LNMIX:PP_LNMIX + 16] = f(inputs["ln_mix_g"])[l].reshape(16, 128).T
        pp[l, :, PP_LNMLP:PP_LNMLP + 16] = f(inputs["ln_mlp_g"])[l].reshape(16, 128).T
        cw = f(inputs["conv_w"])[l]
        pp[l, :, PP_CONVW:PP_CONVW + 124] = cw.T.reshape(4, 128, 31).transpose(1, 0, 2).reshape(128, 124)
        pp[l, :, PP_CONVB:PP_CONVB + 4] = f(inputs["conv_b"])[l].reshape(4, 128).T
        pp[l, :, PP_CNG:PP_CNG + 4] = f(inputs["conv_norm_g"])[l].reshape(4, 128).T
        pp[l, :, PP_CNB:PP_CNB + 4] = f(inputs["conv_norm_b"])[l].reshape(4, 128).T
        qw = f(inputs["qk_conv_w"])[l]
        pp[l, :, PP_QKW:PP_QKW + 64] = qw.T.reshape(16, 128, 4).transpose(1, 0, 2).reshape(128, 64)
        pp[l, :, PP_QKB:PP_QKB + 16] = f(inputs["qk_conv_b"])[l].reshape(16, 128).T
        pp[l, 0:4, PP_IGB] = f(inputs["igate_b"])[l]
        pp[l, 0:4, PP_FGB] = f(inputs["fgate_b"])[l]
        pb[l, PB_MNG:PB_MNG + 1024] = f(inputs["mlstm_norm_g"])[l]
        pb[l, PB_GNG:PB_GNG + 512] = f(inputs["gm_norm_g"])[l]
        pb[l, PB_GNB:PB_GNB + 512] = f(inputs["gm_norm_b"])[l]
        pb[l, PB_GMB:PB_GMB + 512] = f(inputs["gm_b"])[l].reshape(512)
    gmwT = np.ascontiguousarray(f(inputs["gm_w"])[:depth].transpose(0, 3, 1, 2))
    cst = np.zeros((128, NCST), np.float32)
    cst[:, CS_IDENT:CS_IDENT + 128] = np.eye(128, dtype=np.float32)
    cst[:, CS_MASK:CS_MASK + 128] = np.tril(np.ones((128, 128), np.float32))
    cst[:, CS_MASKT:CS_MASKT + 128] = np.triu(np.ones((128, 128), np.float32))
    cst[:, CS_ONES:CS_ONES + 128] = 1.0
    return {
        "w_in": f(inputs["w_in"])[:depth], "w_out": f(inputs["w_out"])[:depth],
        "w_up": f(inputs["w_up"])[:depth], "w_down": f(inputs["w_down"])[:depth],
        "pp": pp, "pb": pb, "final_g": f(inputs["final_g"]).reshape(1, D), "gm_wT": gmwT, "cst": cst,
    }


_NC_CACHE = {}


def kernel(**inputs):
    x = np.asarray(inputs["x"], dtype=np.float32)
    B, S, _ = x.shape
    T = 512
    key = (S, T)
    if key not in _NC_CACHE:
        _NC_CACHE[key] = build_nc(S, T)
    nc = _NC_CACHE[key]
    shared = _host_layout(inputs)
    in_maps = []
    for b in range(B):
        m = dict(shared)
        m["x"] = np.ascontiguousarray(x[b])
        in_maps.append(m)
    res = run_bass_kernel_spmd(nc, in_maps, core_ids=list(range(B)))
    return np.stack([np.asarray(r["y"], dtype=np.float32) for r in res.results], axis=0)
```

```python
import numpy as np
import concourse.bass as bass
import concourse.mybir as mybir
from concourse.bass_utils import run_bass_kernel_spmd

F32 = mybir.dt.float32
BF16 = mybir.dt.bfloat16
AF = mybir.ActivationFunctionType
ALU = mybir.AluOpType

D = 2048
NIN = 6152
DFF = 8192
DEPTH = 2
EPS = 1e-6
KC = 16
LN16 = float(np.log(16.0))

C_CV, C_CG, C_Q, C_K, C_V, C_O, C_G, C_GU, C_GV = 0, 512, 1024, 2048, 3072, 4096, 5120, 5128, 5640

PP_LNMIX = 0
PP_LNMLP = 16
PP_CONVW = 32
PP_CONVB = 156
PP_CNG = 160
PP_CNB = 164
PP_QKW = 168
PP_QKB = 232
PP_IGB = 248
PP_FGB = 249
NPP = 256
PB_MNG = 0
PB_GNG = 1024
PB_GNB = 1536
PB_GMB = 2048
NPB = 2560
CS_IDENT = 0
CS_MASK = 128
CS_MASKT = 256
CS_ONES = 384
NCST = 512


import threading

_tls = threading.local()


class Coop:
    def __init__(self):
        self.cv = threading.Condition()
        self.cur = -1
        self.done = []
        self.active = False

    def _next(self, i):
        n = len(self.done)
        for d in range(1, n + 1):
            j = (i + d) % n
            if not self.done[j]:
                return j
        return -1

    def switch(self):
        if not self.active:
            return
        i = getattr(_tls, "task", None)
        if i is None:
            return
        w = getattr(_tls, "weight", 1)
        with self.cv:
            for _ in range(w):
                j = self._next(i)
                if j == i or j < 0:
                    return
                self.cur = j
                self.cv.notify_all()
                while self.cur != i:
                    self.cv.wait()

    def run(self, fns, weights=None, pools=None):
        n = len(fns)
        self.done = [False] * n
        errs = []

        def worker(i):
            _tls.task = i
            _tls.weight = (weights or [1] * n)[i]
            _tls.pools = (pools or [None] * n)[i]
            with self.cv:
                while self.cur != i:
                    self.cv.wait()
            try:
                fns[i]()
            except BaseException as e:
                errs.append(e)
            finally:
                with self.cv:
                    self.done[i] = True
                    self.cur = self._next(i)
                    self.cv.notify_all()

        self.active = True
        self.cur = 0
        ths = [threading.Thread(target=worker, args=(i,)) for i in range(n)]
        for t in ths:
            t.start()
        for t in ths:
            t.join()
        self.active = False
        self.cur = -1
        if errs:
            raise errs[0]


COOP = Coop()


class Sched:
    def __init__(self, nc, sems):
        self.nc = nc
        self.eng = {"pe": nc.tensor, "act": nc.scalar, "dve": nc.vector, "pool": nc.gpsimd, "sp": nc.sync}
        self.sem = dict(sems)
        self.scale = {k: 1 for k in sems}
        self.cnt = {k: 0 for k in sems}
        self.seen = {e: {} for e in self.eng}
        self.res = {}

    def add_dma_sem(self, name, sem):
        self.sem[name] = sem
        self.scale[name] = 16
        self.cnt[name] = 0

    def _deps(self, e, reads, writes):
        deps = {}

        def add(p):
            f, idx = p
            if f == e and e == "pe":
                return
            if idx > deps.get(f, 0):
                deps[f] = idx

        for k in reads:
            r = self.res.get(k)
            if r and r[0]:
                add(r[0])
        for k in writes:
            r = self.res.get(k)
            if r:
                if r[0]:
                    add(r[0])
                for p in r[1].items():
                    add(p)
        for f, idx in deps.items():
            if idx > self.seen[e].get(f, 0):
                self.eng[e].wait_ge(self.sem[f], idx * self.scale[f])
                self.seen[e][f] = idx

    def _mark(self, who, idx, reads, writes):
        for k in reads:
            r = self.res.get(k)
            if r is None:
                r = [None, {}]
                self.res[k] = r
            r[1][who] = idx
        for k in writes:
            self.res[k] = [(who, idx), {}]

    def op(self, e, emit, reads=(), writes=()):
        COOP.switch()
        self._deps(e, reads, writes)
        ins = emit()
        self.cnt[e] += 1
        idx = self.cnt[e]
        ins.then_inc(self.sem[e], 1)
        self._mark(e, idx, reads, writes)

    def dma(self, q, dsem, out, in_, reads=(), writes=()):
        COOP.switch()
        self._deps(q, reads, writes)
        ins = self.eng[q].dma_start(out=out, in_=in_)
        self.cnt[dsem] += 1
        ins.then_inc(self.sem[dsem], 16)
        self._mark(dsem, self.cnt[dsem], reads, writes)

    def wait_all(self, e, names):
        for f in names:
            idx = self.cnt[f]
            if idx > self.seen[e].get(f, 0):
                self.eng[e].wait_ge(self.sem[f], idx * self.scale[f])
                self.seen[e][f] = idx


def build_nc(S, T, depth=DEPTH):
    NT = S // T
    NCH = T // 128
    nc = bass.Bass("TRN2", target_bir_lowering=False)
    x_d = nc.dram_tensor("x", [S, D], F32, kind="ExternalInput").ap()
    win_d = nc.dram_tensor("w_in", [depth, D, NIN], F32, kind="ExternalInput").ap()
    wout_d = nc.dram_tensor("w_out", [depth, D, D], F32, kind="ExternalInput").ap()
    wup_d = nc.dram_tensor("w_up", [depth, D, DFF], F32, kind="ExternalInput").ap()
    wdn_d = nc.dram_tensor("w_down", [depth, DFF, D], F32, kind="ExternalInput").ap()
    pp_d = nc.dram_tensor("pp", [depth, 128, NPP], F32, kind="ExternalInput").ap()
    pb_d = nc.dram_tensor("pb", [depth, NPB], F32, kind="ExternalInput").ap()
    fg_d = nc.dram_tensor("final_g", [1, D], F32, kind="ExternalInput").ap()
    gmw_d = nc.dram_tensor("gm_wT", [depth, 128, 4, 128], F32, kind="ExternalInput").ap()
    cst_d = nc.dram_tensor("cst", [128, NCST], F32, kind="ExternalInput").ap()
    y_d = nc.dram_tensor("y", [S, D], F32, kind="ExternalOutput").ap()

    import contextlib
    es = contextlib.ExitStack()
    with es:
        def sb(name, shape, dt):
            return es.enter_context(nc.sbuf_tensor("s_" + name, shape, dt))

        def psum(name, shape, dt):
            return es.enter_context(nc.psum_tensor("p_" + name, shape, dt))

        def semaphore(name):
            return es.enter_context(nc.semaphore("m_" + name))

        S_ = Sched(nc, {e: semaphore("s_" + e) for e in ["pe", "act", "dve", "pool", "sp"]})
        NW = 3
        for i in range(NW):
            S_.add_dma_sem(f"w{i}", semaphore(f"w{i}"))
        for nm in ["misc", "xin0", "xin1", "xin2", "xin3", "st0", "st1", "fg", "pbt", "pbc0", "pbc1", "pbc2", "wg"]:
            S_.add_dma_sem(nm, semaphore(nm))

        x = sb("x", [128, NCH, D], F32)
        hT = sb("hT", [128, KC, T], BF16)
        wbuf = [sb(f"wb{i}", [128, 8192], BF16) for i in range(NW)]
        wg = sb("wg", [128, KC, 8], BF16)
        cst = sb("cst", [128, NCST], F32)
        identb = sb("identb", [128, 128], BF16)
        pp = [sb(f"pp{l}", [128, NPP], F32) for l in range(depth)]
        pbt = sb("pbt", [128, 1024], F32)
        WcT = [sb(f"WcT{l}", [128, 4, 128], BF16) for l in range(depth)]
        nfb = [sb(f"nfb{l}", [4, 1], F32) for l in range(depth)]
        FA = sb("FA", [128, 4, 544], F32)
        FB = sb("FB", [128, 4, 544], F32)
        FC = sb("FC", [128, 4, 512], F32)
        actG = sb("actG", [128, 4, T], BF16)
        qT = sb("qT", [128, 4, T], BF16)
        kT = sb("kT", [128, 4, T], BF16)
        gT = [sb(f"gT{i}", [128, 4, T], BF16) for i in range(2)]
        vaug = sb("vaug", [128, NCH, 2, 257], BF16)
        xs0 = sb("xs0", [128, D], BF16)
        xs = [xs0, xs0]
        ss = sb("ss", [128, 16], F32)
        rstd = sb("rstd", [128, 16], F32)
        Mend = sb("Mend", [4, NCH + 1], F32)
        nMend = sb("nMend", [4, NCH + 1], F32)
        DEC = sb("DEC", [4, NCH], F32)
        sel4 = sb("sel4", [4, 4, 128], F32)
        Bcar = [sb(f"Bcar{l}", [4, 1], F32) for l in range(depth)]
        Mcar = [sb(f"Mcar{l}", [4, 1], F32) for l in range(depth)]
        uP = sb("uP", [4, T], F32)
        tok = sb("tok", [128, NCH, 4, 4], F32)
        decb = sb("decb", [128, 4, NCH], F32)
        Cst = [[sb(f"Cst{l}_{h}", [128, 2, 257], F32) for h in range(4)] for l in range(depth)]
        Cbf1 = [sb(f"Cbf_{h}", [128, 2, 257], BF16) for h in range(4)]
        Cbf = [Cbf1 for l in range(depth)]
        histA = [sb(f"histA{l}", [128, 4, 30], F32) for l in range(depth)]
        histQ = [sb(f"histQ{l}", [128, 16, 3], F32) for l in range(depth)]
        zq = [sb(f"zq{i}", [128, T + 3], F32) for i in range(2)]
        cacc = [sb(f"cacc{i}", [128, T], F32) for i in range(2)]
        ppb = [sb(f"ppb{i}", [128, 128], F32) for i in range(2)]
        pbf = [sb(f"pbf{i}", [128, 128], BF16) for i in range(2)]
        pTs = [sb(f"pTs{i}", [128, 128], BF16) for i in range(2)]
        tB = [sb(f"tB{i}", [128, 257], F32) for i in range(2)]
        tot = [sb(f"tot{i}", [128, 257], F32) for i in range(2)]
        sm = [sb(f"sm{i}", [128, 8], F32) for i in range(2)]
        hb = [sb(f"hb{i}", [128, 256], BF16) for i in range(2)]
        kw = [sb(f"kw{i}", [128, 256], BF16) for i in range(2)]
        rt = cacc
        gt = zq
        cs1 = sb("cs1", [128, 16], F32)
        epsc = sb("epsc", [128, 1], F32)
        gU = zq[0]
        gM = zq[1]
        gL = cacc[0]
        gB = cacc[1]

        ps = [psum(f"ps{i}", [128, 512], F32) for i in range(6)]
        pb = [psum(f"pbk{i}", [128, 1024], BF16) for i in range(2)]

        state = {"ps": 0, "pb": 0, "tmp": 0}

        def nps():
            pools = getattr(_tls, "pools", None)
            if pools is not None:
                lst = pools["ps"]
                i = lst[pools["psi"] % len(lst)]
                pools["psi"] += 1
                return ps[i], ("ps", i)
            i = state["ps"]
            state["ps"] = (i + 1) % 6
            return ps[i], ("ps", i)

        def npb():
            pools = getattr(_tls, "pools", None)
            if pools is not None:
                lst = pools["pb"]
                i = lst[pools["pbi"] % len(lst)]
                pools["pbi"] += 1
                return pb[i], ("pb", i)
            i = state["pb"]
            state["pb"] = (i + 1) % 2
            return pb[i], ("pb", i)

        def k(t, *idx):
            return (t.name,) + idx

        def mm(out_ap, pairs, reads, writes):
            def emit():
                n = len(pairs)
                ins = None
                for i, (l, r) in enumerate(pairs):
                    ins = nc.tensor.matmul(out_ap, l, r, start=(i == 0), stop=(i == n - 1))
                return ins
            S_.op("pe", emit, reads, writes)

        def tr(out_ap, in_ap, ident_ap, reads, writes):
            S_.op("pe", lambda: nc.tensor.transpose(out_ap, in_ap, ident_ap), reads, writes)

        def act(out, in_, func, reads, writes, **kw_):
            S_.op("act", lambda: nc.scalar.activation(out=out, in_=in_, func=func, **kw_), reads, writes)

        def ts(e, out, in0, s1, s2, op0, op1, reads, writes):
            eng = nc.vector if e == "dve" else nc.gpsimd
            if op1 is None:
                S_.op(e, lambda: eng.tensor_scalar(out=out, in0=in0, scalar1=s1, scalar2=None, op0=op0), reads, writes)
            else:
                S_.op(e, lambda: eng.tensor_scalar(out=out, in0=in0, scalar1=s1, scalar2=s2, op0=op0, op1=op1), reads, writes)

        def tt(e, out, in0, in1, op, reads, writes):
            eng = nc.vector if e == "dve" else nc.gpsimd
            S_.op(e, lambda: eng.tensor_tensor(out=out, in0=in0, in1=in1, op=op), reads, writes)

        def stt(e, out, in0, scalar, in1, op0, op1, reads, writes):
            eng = nc.vector if e == "dve" else nc.gpsimd
            S_.op(e, lambda: eng.scalar_tensor_tensor(out=out, in0=in0, scalar=scalar, in1=in1, op0=op0, op1=op1), reads, writes)

        def rsqrt(out, in_, scale, eps, reads, writes):
            act(out, in_, AF.Sqrt, reads, writes, scale=scale, bias=eps)
            S_.op("dve", lambda: nc.vector.reciprocal(out=out, in_=out), writes, writes)

        def cp(e, out, in_, reads, writes):
            if e == "act":
                S_.op("act", lambda: nc.scalar.copy(out=out, in_=in_), reads, writes)
            else:
                eng = nc.vector if e == "dve" else nc.gpsimd
                S_.op(e, lambda: eng.tensor_copy(out=out, in_=in_), reads, writes)

        def memset(e, ap, val, writes):
            eng = nc.vector if e == "dve" else nc.gpsimd
            S_.op(e, lambda: eng.memset(ap, val), (), writes)

        wstream = []
        for tt_ in range(NT):
            for l in range(depth):
                def wi(c0):
                    return ("in", win_d[l, :, c0:c0 + 512].rearrange("(k p) n -> p k n", p=128))

                def wo(r0):
                    return ("row", wout_d[l, r0:r0 + 512, :].rearrange("(k p) n -> p k n", p=128))
                seq = [wi(C_CG), wi(C_CV),
                       wi(C_Q), wi(C_K), wi(C_V), wi(C_O), wi(C_GV), wo(512),
                       wi(C_Q + 512), wi(C_K + 512), wi(C_V + 512), wi(C_O + 512), wi(C_GU), wo(1536), wo(1024),
                       wo(0)]
                for j in range(DFF // 512):
                    seq.append(("in", wup_d[l, :, j * 512:(j + 1) * 512].rearrange("(k p) n -> p k n", p=128)))
                    seq.append(("row", wdn_d[l, j * 512:(j + 1) * 512, :].rearrange("(k p) n -> p k n", p=128)))
                wstream.extend(seq)
        wpos = {"issued": 0, "used": 0}

        def w_issue():
            i = wpos["issued"]
            if i >= len(wstream):
                return
            kind, src = wstream[i]
            slot = i % NW
            if kind == "in":
                dst = wbuf[slot][:, :].rearrange("p (k n) -> p k n", k=16)
            else:
                dst = wbuf[slot][:, :].rearrange("p (k n) -> p k n", k=4)
            S_.dma("pool", f"w{slot}", dst, src, reads=(), writes=[("wb", slot)])
            wpos["issued"] = i + 1

        def w_next():
            i = wpos["used"]
            wpos["used"] = i + 1
            kind, _ = wstream[i]
            slot = i % NW
            if kind == "in":
                v = wbuf[slot][:, :].rearrange("p (k n) -> p k n", k=16)
            else:
                v = wbuf[slot][:, :].rearrange("p (k n) -> p k n", k=4)
            return v, ("wb", slot)

        S_.dma("sp", "misc", cst[:, :], cst_d[:, :], writes=[k(cst)])
        for l in range(depth):
            S_.dma("sp", "misc", pp[l][:, :], pp_d[l, :, :], writes=[k(pp[l])])
        for l in range(depth):
            S_.dma("sp", "misc", FC[:, :, l * 128:(l + 1) * 128], gmw_d[l, :, :, :], reads=(), writes=[k(FC)])
        for e in ["pe", "act", "dve", "pool"]:
            S_.wait_all(e, ["misc"])
        for i in range(NW):
            w_issue()
        cp("dve", identb[:, :], cst[:, CS_IDENT:CS_IDENT + 128], [k(cst)], [k(identb)])
        memset("dve", epsc[:, :], EPS, [k(epsc)])
        memset("dve", sel4[:, :, :], 0.0, [k(sel4)])
        for h in range(4):
            ts("dve", sel4[:, h, :], cst[0:4, CS_ONES:CS_ONES + 128], cst[0:4, CS_IDENT + h:CS_IDENT + h + 1], None, ALU.mult, None,
               [k(cst)], [k(sel4)])
        for l in range(depth):
            ts("dve", nfb[l][:, :], pp[l][0:4, PP_FGB:PP_FGB + 1], -1.0, None, ALU.mult, None, [k(pp[l])], [k(nfb[l])])
            memset("dve", Bcar[l][:, :], 0.0, [k(Bcar[l])])
            memset("dve", Mcar[l][:, :], 0.0, [k(Mcar[l])])
            memset("dve", histA[l][:, :, :], 0.0, [k(histA[l])])
            memset("dve", histQ[l][:, :, :], 0.0, [k(histQ[l])])
            for h in range(4):
                memset("dve", Cst[l][h][:, :, :], 0.0, [k(Cst[l][h])])
            for g in range(4):
                tt("dve", WcT[l][:, g, :], FC[:, g, l * 128:(l + 1) * 128], cst[:, CS_MASKT:CS_MASKT + 128], ALU.mult,
                   [k(FC), k(cst)], [k(WcT[l])])
        memset("dve", vaug[:, :, :, 256:257], 1.0, [k(vaug)])

        def norm_act(l, c):
            act(hT[:, :, c * 128:(c + 1) * 128], x[:, c, :].rearrange("p (a b) -> p a b", a=KC), AF.Square,
                [k(x, c)], [k(hT, c), k(ss, c)], accum_out=ss[:, c:c + 1])
            rsqrt(rstd[:, c:c + 1], ss[:, c:c + 1], 1.0 / D, EPS, [k(ss, c)], [k(rstd, c)])
            act(xs0[:, :], x[:, c, :], AF.Copy, [k(x, c), k(rstd, c)], [k(xs0)], scale=rstd[:, c:c + 1])

        def norm_pe(l, goff, c):
            xb = xs0
            for kq in range(4):
                pbk, pk = npb()
                for i in range(4):
                    kk = kq * 4 + i
                    tr(pbk[:, i * 128:(i + 1) * 128], xb[:, kk * 128:(kk + 1) * 128], identb[:, :],
                       [k(xb), k(identb)], [pk])
                gb = pp[l][:, goff + kq * 4:goff + kq * 4 + 4].unsqueeze(2).to_broadcast([128, 4, 128])
                tt("dve", hT[:, kq * 4:kq * 4 + 4, c * 128:(c + 1) * 128],
                   pbk[:, 0:512].rearrange("p (a b) -> p a b", a=4), gb, ALU.mult, [pk, k(pp[l])], [k(hT, c)])

        def norm_to_hT(l, goff):
            for c in range(NCH):
                norm_act(l, c)
                norm_pe(l, goff, c)

        def norm_hook(l, goff):
            def hook(c):
                if c >= 1:
                    norm_pe(l, goff, c - 1)
                norm_act(l, c)
                if c == NCH - 1:
                    norm_pe(l, goff, c)
            return hook

        def hT_keys():
            return [k(hT, c) for c in range(NCH)]

        def z_fm(wv, wk_, j, out_ps):
            pairs = [(wv[:, kk, j * 128:(j + 1) * 128], hT[:, kk, 0:T]) for kk in range(KC)]
            return pairs

        def accum_into_x(aT, akey, wv, wkey, hook=None):
            for c in range(NCH):
                for db in range(4):
                    p, pk = nps()
                    mm(p[:, :], [(aT[:, kc, c * 128:(c + 1) * 128], wv[:, kc, db * 512:(db + 1) * 512]) for kc in range(4)],
                       [akey, wkey], [pk])
                    tt("dve", x[:, c, db * 512:(db + 1) * 512], x[:, c, db * 512:(db + 1) * 512], p[:, :], ALU.add,
                       [pk, k(x, c)], [k(x, c)])
                if hook is not None:
                    hook(c)

        def wg_prefetch(l):
            S_.dma("pool", "wg", wg[:, :, :], win_d[l, :, C_G:C_G + 8].rearrange("(k p) n -> p k n", p=128),
                   reads=(), writes=[k(wg)])

        def gates_p1(l):
            pI, kI = nps()
            pF, kF = nps()
            mm(pI[0:4, 0:T], [(wg[:, kk, 0:4], hT[:, kk, 0:T]) for kk in range(KC)], [k(wg)] + hT_keys(), [kI])
            mm(pF[0:4, 0:T], [(wg[:, kk, 4:8], hT[:, kk, 0:T]) for kk in range(KC)], [k(wg)] + hT_keys(), [kF])
            act(gU[0:4, 0:T], pI[0:4, 0:T], AF.Identity, [kI, k(pp[l])], [k(gU)], bias=pp[l][0:4, PP_IGB:PP_IGB + 1])
            act(gL[0:4, 0:T], pF[0:4, 0:T], AF.Exp, [kF, k(nfb[l])], [k(gL)], scale=-1.0, bias=nfb[l][:, 0:1])
            act(gL[0:4, 0:T], gL[0:4, 0:T], AF.Ln, [k(gL)], [k(gL)], bias=1.0)
            S_.op("dve", lambda: nc.vector.tensor_tensor_scan(out=gB[0:4, 0:T], data0=cst[0:4, CS_ONES:CS_ONES + 1].to_broadcast([4, T]), data1=gL[0:4, 0:T],
                                                              initial=Bcar[l][:, 0:1], op0=ALU.mult, op1=ALU.subtract),
                  [k(cst), k(gL), k(Bcar[l])], [k(gB)])
            tt("dve", gU[0:4, 0:T], gU[0:4, 0:T], gB[0:4, 0:T], ALU.subtract, [k(gU), k(gB)], [k(gU)])
            S_.op("dve", lambda: nc.vector.tensor_tensor_scan(out=gM[0:4, 0:T], data0=cst[0:4, CS_ONES:CS_ONES + 1].to_broadcast([4, T]), data1=gU[0:4, 0:T],
                                                              initial=Mcar[l][:, 0:1], op0=ALU.mult, op1=ALU.max),
                  [k(cst), k(gU), k(Mcar[l])], [k(gM)])
            cp("dve", Mend[:, 0:1], Mcar[l][:, 0:1], [k(Mcar[l])], [k(Mend)])
            for c in range(NCH):
                cp("dve", Mend[:, c + 1:c + 2], gM[0:4, c * 128 + 127:c * 128 + 128], [k(gM)], [k(Mend)])
            cp("dve", Bcar[l][:, 0:1], gB[0:4, T - 1:T], [k(gB)], [k(Bcar[l])])
            cp("dve", Mcar[l][:, 0:1], gM[0:4, T - 1:T], [k(gM)], [k(Mcar[l])])
            ts("dve", nMend[:, :], Mend[:, :], -1.0, -LN16, ALU.mult, ALU.add, [k(Mend)], [k(nMend)])
            tt("dve", DEC[:, :], Mend[:, 0:NCH], Mend[:, 1:NCH + 1], ALU.subtract, [k(Mend)], [k(DEC)])
            act(DEC[:, :], DEC[:, :], AF.Exp, [k(DEC)], [k(DEC)])
            ts("dve", FC[0:4, 0, 0:T], gM[0:4, 0:T], -1.0, None, ALU.mult, None, [k(gM)], [k(FC)])
            tt("dve", FC[0:4, 2, 0:T], gB[0:4, 0:T], gM[0:4, 0:T], ALU.add, [k(gB), k(gM)], [k(FC)])
            act(FC[0:4, 2, 0:T], FC[0:4, 2, 0:T], AF.Exp, [k(FC)], [k(FC)], scale=-1.0)
            for c in range(NCH):
                sl = slice(c * 128, (c + 1) * 128)
                act(FC[0:4, 1, sl], gM[0:4, sl], AF.Exp, [k(gM), k(Mend)], [k(FC)], scale=-1.0, bias=Mend[:, c:c + 1])
                act(FC[0:4, 3, sl], gU[0:4, sl], AF.Exp, [k(gU), k(nMend)], [k(FC)], bias=nMend[:, c + 1:c + 2])
            cp("dve", uP[:, :], gU[0:4, 0:T], [k(gU)], [k(uP)])

        def gates_p2(l):
            pT_, kT_ = nps()
            for c in range(NCH):
                for qi in range(4):
                    o0 = (c * 4 + qi) * 4
                    tr(pT_[:, o0:o0 + 4], FC[0:4, qi, c * 128:(c + 1) * 128], cst[0:4, CS_IDENT:CS_IDENT + 4],
                       [k(FC), k(cst)], [kT_])
            cp("dve", tok[:, :, :, :].rearrange("p c q h -> p (c q h)"), pT_[:, 0:NCH * 16], [kT_], [k(tok)])
            pD, kD = nps()
            for h in range(4):
                mm(pD[:, h * NCH:(h + 1) * NCH], [(sel4[:, h, :], DEC[:, :])], [k(sel4), k(DEC)], [kD])
            cp("dve", decb[:, :, :].rearrange("p h c -> p (h c)"), pD[:, 0:4 * NCH], [kD], [k(decb)])
        def qk_block(l, wv, wkey, dstT, gj0):
            for j in range(4):
                gj = gj0 + j
                zb = zq[j % 2]
                ca = cacc[j % 2]
                p, pk = nps()
                mm(p[:, 0:T], z_fm(wv, wkey, j, p), [wkey] + hT_keys(), [pk])
                cp("dve", zb[:, 0:3], histQ[l][:, gj, :], [k(histQ[l], gj)], [k(zb)])
                cp("act", zb[:, 3:3 + T], p[:, 0:T], [pk], [k(zb)])
                cp("dve", histQ[l][:, gj, :], zb[:, T:T + 3], [k(zb)], [k(histQ[l], gj)])
                w0 = PP_QKW + gj * 4
                e = "dve"
                ts(e, ca[:, :], zb[:, 0:T], pp[l][:, w0:w0 + 1], pp[l][:, PP_QKB + gj:PP_QKB + gj + 1], ALU.mult, ALU.add,
                   [k(zb), k(pp[l])], [k(ca)])
                for t_ in range(1, 4):
                    stt(e, ca[:, :], zb[:, t_:t_ + T], pp[l][:, w0 + t_:w0 + t_ + 1], ca[:, :], ALU.mult, ALU.add,
                        [k(zb), k(pp[l]), k(ca)], [k(ca)])
                act(dstT[:, j, :], ca[:, :], AF.Silu, [k(ca)], [k(dstT)])

        def mlstm_head(l, hg, hh):
            GO = FC
            h = hg * 2 + hh
            i2 = hh
            s_ = sm[i2]
            for c in range(NCH):
                sl = slice(c * 128, (c + 1) * 128)
                pS, kS = nps()
                mm(pS[:, 0:128], [(qT[:, hh * 2 + dc, sl], kT[:, hh * 2 + dc, sl]) for dc in range(2)],
                   [k(qT), k(kT)], [kS])
                mm(pS[:, 128:256], [(sel4[:, h, :], uP[:, sl])], [k(sel4), k(uP)], [kS])
                ts("dve", ppb[i2][:, :], pS[:, 128:256], tok[:, c, 0, h:h + 1], 0.0, ALU.add, ALU.min,
                   [kS, k(tok)], [k(ppb[i2])])
                yield
                act(ppb[i2][:, :], ppb[i2][:, :], AF.Exp, [k(ppb[i2])], [k(ppb[i2])])
                yield
                tt("pool", ppb[i2][:, :], ppb[i2][:, :], cst[:, CS_MASK:CS_MASK + 128], ALU.mult,
                   [k(ppb[i2]), k(cst)], [k(ppb[i2])])
                yield
                stt("dve", pbf[i2][:, :], pS[:, 0:128], 0.0625, ppb[i2][:, :], ALU.mult, ALU.mult,
                    [kS, k(ppb[i2])], [k(pbf[i2])])
                yield
                pb1, kb1 = npb()
                tr(pb1[:, 0:128], pbf[i2][:, :], identb[:, :], [k(pbf[i2]), k(identb)], [kb1])
                pB, kB = nps()
                mm(pB[:, 0:257], [(qT[:, hh * 2 + dc, sl], Cbf[l][h][:, dc, :]) for dc in range(2)],
                   [k(qT), k(Cbf[l][h])], [kB])
                yield
                cp("act", pTs[i2][:, :], pb1[:, 0:128], [kb1], [k(pTs[i2])])
                act(tB[i2][:, :], pB[:, 0:257], AF.Copy, [kB, k(tok)], [k(tB[i2])], scale=tok[:, c, 1, h:h + 1])
                yield
                pA, kA = nps()
                mm(pA[:, 0:257], [(pTs[i2][:, :], vaug[:, c, hh, :])], [k(pTs[i2]), k(vaug)], [kA])
                pb3, kb3 = npb()
                for dc in range(2):
                    tr(pb3[:, dc * 128:(dc + 1) * 128], kT[:, hh * 2 + dc, sl], identb[:, :], [k(kT), k(identb)], [kb3])
                yield
                tt("dve", tot[i2][:, :], tB[i2][:, :], pA[:, 0:257], ALU.add, [k(tB[i2]), kA], [k(tot[i2])])
                act(kw[i2][:, :], pb3[:, 0:256], AF.Copy, [kb3, k(tok)], [k(kw[i2])], scale=tok[:, c, 3, h:h + 1])
                yield
                act(s_[:, 0:1], tot[i2][:, 256:257], AF.Abs, [k(tot[i2])], [k(s_)])
                for dc in range(2):
                    pC, kC = nps()
                    mm(pC[:, 0:257], [(kw[i2][:, dc * 128:(dc + 1) * 128], vaug[:, c, hh, :])],
                       [k(kw[i2]), k(vaug)], [kC])
                    stt("dve", Cst[l][h][:, dc, :], Cst[l][h][:, dc, :], decb[:, h, c:c + 1], pC[:, 0:257],
                        ALU.mult, ALU.add, [k(Cst[l][h]), k(decb), kC], [k(Cst[l][h])])
                yield
                tt("dve", s_[:, 0:1], s_[:, 0:1], tok[:, c, 2, h:h + 1], ALU.max, [k(s_), k(tok)], [k(s_)])
                cp("pool", Cbf[l][h][:, :, :], Cst[l][h][:, :, :], [k(Cst[l][h])], [k(Cbf[l][h])])
                yield
                S_.op("dve", lambda: nc.vector.reciprocal(out=s_[:, 1:2], in_=s_[:, 0:1]), [k(s_)], [k(s_)])
                yield
                act(hb[i2][:, :], tot[i2][:, 0:256], AF.Square, [k(tot[i2]), k(s_)], [k(hb[i2]), k(s_, "ss")],
                    scale=s_[:, 1:2], accum_out=s_[:, 2:3])
                yield
                act(s_[:, 3:4], s_[:, 2:3], AF.Ln, [k(s_, "ss")], [k(s_, "r")], scale=1.0 / 256, bias=epsc[:, 0:1])
                yield
                act(s_[:, 3:4], s_[:, 3:4], AF.Exp, [k(s_, "r")], [k(s_, "r")], scale=-0.5)
                yield
                tt("dve", s_[:, 4:5], s_[:, 3:4], s_[:, 1:2], ALU.mult, [k(s_, "r"), k(s_)], [k(s_, "f")])
                yield
                stt("dve", hb[i2][:, :], tot[i2][:, 0:256], s_[:, 4:5], GO[:, c, hh * 256:(hh + 1) * 256],
                    ALU.mult, ALU.mult, [k(tot[i2]), k(s_, "f"), k(FC)], [k(hb[i2])])
                yield
                pb2, kb2 = npb()
                for ec in range(2):
                    tr(pb2[:, ec * 128:(ec + 1) * 128], hb[i2][:, ec * 128:(ec + 1) * 128], identb[:, :],
                       [k(hb[i2]), k(identb)], [kb2])
                yield
                cp("act", actG[:, hh * 2:hh * 2 + 2, sl], pb2[:, 0:256].rearrange("p (e t) -> p e t", e=2),
                   [kb2], [k(actG)])
                yield

        def conv_chunk(l, j, e):
            acc = FA[:, j, 0:T]
            a_in = FB
            w0 = PP_CONVW + j * 31
            ts(e, acc, a_in[:, j, 0:T], pp[l][:, w0:w0 + 1], pp[l][:, PP_CONVB + j:PP_CONVB + j + 1], ALU.mult, ALU.add,
               [k(FB, j), k(pp[l])], [k(FA, j)])
            yield
            for t_ in range(1, 31):
                stt(e, acc, a_in[:, j, t_:t_ + T], pp[l][:, w0 + t_:w0 + t_ + 1], acc, ALU.mult, ALU.add,
                    [k(FB, j), k(pp[l]), k(FA, j)], [k(FA, j)])
                yield

        def interleave(gens):
            gens = list(gens)
            while gens:
                for g in list(gens):
                    try:
                        next(g)
                    except StopIteration:
                        gens.remove(g)

        def group_B(l, hg, pre_out=None):
            for hh in range(2):
                h = hg * 2 + hh
                cp("act", Cbf[l][h][:, :, :], Cst[l][h][:, :, :], [k(Cst[l][h])], [k(Cbf[l][h])])
            def zphase():
                wv, wk_ = w_next()
                qk_block(l, wv, wk_, qT, hg * 4)
                w_issue()
                wv, wk_ = w_next()
                qk_block(l, wv, wk_, kT, 8 + hg * 4)
                w_issue()
                wv, wk_ = w_next()
                for c in range(NCH):
                    p, pk = nps()
                    mm(p[:, :], [(hT[:, kk, c * 128:(c + 1) * 128], wv[:, kk, :]) for kk in range(KC)], [wk_, k(hT, c)], [pk])
                    cp("act", vaug[:, c, :, 0:256], p[:, :].rearrange("p (h e) -> p h e", h=2), [pk], [k(vaug)])
                w_issue()
                wv, wk_ = w_next()
                for c in range(NCH):
                    p, pk = nps()
                    mm(p[:, :], [(hT[:, kk, c * 128:(c + 1) * 128], wv[:, kk, :]) for kk in range(KC)], [wk_, k(hT, c)], [pk])
                    act(FC[:, c, 0:512], p[:, :], AF.Sigmoid, [pk], [k(FC)])
                    tt("pool", FC[:, c, 0:512], FC[:, c, 0:512], pbt[:, PB_MNG + hg * 512:PB_MNG + (hg + 1) * 512], ALU.mult,
                       [k(FC), k(pbt)], [k(FC)])
                w_issue()

            def drain(g):
                for _ in g:
                    pass
            COOP.run([zphase, lambda: drain(conv_chunk(l, 2 * hg, "dve"))], weights=[1, 3], pools=[None, None])
            fns = [lambda: drain(mlstm_head(l, hg, 0)), lambda: drain(mlstm_head(l, hg, 1)),
                   lambda: drain(conv_chunk(l, 2 * hg + 1, "dve")),
                   (lambda: C_main(l)) if hg == 0 else (lambda: C_out(l))]
            pools = [{"ps": [0, 1], "pb": [0], "psi": 0, "pbi": 0}, {"ps": [2, 3], "pb": [1], "psi": 0, "pbi": 0},
                     None, {"ps": [4, 5], "pb": [0], "psi": 0, "pbi": 0}]
            COOP.run(fns, weights=[1, 1, 4, 1], pools=pools)
            st_ = pre_out() if pre_out is not None else None
            wv, wk_ = w_next()
            accum_into_x(actG, k(actG), wv, wk_)
            w_issue()
            return st_

        def group_A_z(l):
            sg = FA
            a_in = FB
            wv, wk_ = w_next()
            for j in range(4):
                p, pk = nps()
                mm(p[:, 0:T], z_fm(wv, wk_, j, p), [wk_] + hT_keys(), [pk])
                act(sg[:, j, 0:T], p[:, 0:T], AF.Sigmoid, [pk], [k(FA, j)])
            w_issue()
            wv, wk_ = w_next()
            for j in range(4):
                p, pk = nps()
                mm(p[:, 0:T], z_fm(wv, wk_, j, p), [wk_] + hT_keys(), [pk])
                cp("pool", a_in[:, j, 0:30], histA[l][:, j, :], [k(histA[l], j)], [k(FB, j)])
                tt("dve", a_in[:, j, 30:30 + T], p[:, 0:T], sg[:, j, 0:T], ALU.mult, [pk, k(FA, j)], [k(FB, j)])
                cp("pool", histA[l][:, j, :], a_in[:, j, T:T + 30], [k(FB, j)], [k(histA[l], j)])
            w_issue()

        def A_fin_p1(l):
            sg = FA
            a_in = FB
            for j in range(4):
                act(a_in[:, j, 0:T], sg[:, j, 0:T], AF.Square, [k(FA, j)], [k(FB, j)])
            p1, k1 = nps()
            p2, k2 = nps()
            ones = cst[:, CS_ONES:CS_ONES + 128]
            mm(p1[:, 0:T], [(ones, sg[:, j, 0:T]) for j in range(4)], [k(cst)] + [k(FA, j) for j in range(4)], [k1])
            mm(p2[:, 0:T], [(ones, a_in[:, j, 0:T]) for j in range(4)], [k(cst)] + [k(FB, j) for j in range(4)], [k2])
            mean = FC[:, 0, 0:T]
            var = FC[:, 1, 0:T]
            tmp = FC[:, 2, 0:T]
            ts("dve", mean, p1[:, 0:T], 1.0 / 512, None, ALU.mult, None, [k1], [k(FC)])
            tt("dve", tmp, mean, mean, ALU.mult, [k(FC)], [k(FC)])
            stt("dve", var, p2[:, 0:T], 1.0 / 512, tmp, ALU.mult, ALU.subtract, [k2, k(FC)], [k(FC)])
            rsqrt(var, var, 1.0, EPS, [k(FC)], [k(FC)])
            return None

        def A_fin_p2(l, st_):
            sg = FA
            actA = gT[1]
            mean = FC[:, 0, 0:T]
            var = FC[:, 1, 0:T]
            for j in range(4):
                e = "dve" if j % 2 == 0 else "pool"
                tt(e, sg[:, j, 0:T], sg[:, j, 0:T], mean, ALU.subtract, [k(FA, j), k(FC)], [k(FA, j)])
                tt(e, sg[:, j, 0:T], sg[:, j, 0:T], var, ALU.mult, [k(FA, j), k(FC)], [k(FA, j)])
                act(actA[:, j, :], sg[:, j, 0:T], AF.Silu, [k(FA, j), k(pp[l])], [k(gT[1])],
                    scale=pp[l][:, PP_CNG + j:PP_CNG + j + 1], bias=pp[l][:, PP_CNB + j:PP_CNB + j + 1])
            wv, wk_ = w_next()
            accum_into_x(actA, k(gT[1]), wv, wk_, hook=norm_hook(l, PP_LNMLP))
            w_issue()

        def C_main(l):
            guT = gT[0]
            gvn = gT[1]
            gng = zq[0][:, 0:512]
            gnb = zq[1][:, 0:512]
            gmb = cacc[0][:, 0:512]
            g_ = cacc[1][:, 0:512]
            jk = xs[0]
            S_.dma("sp", "pbc0", gng, pb_d[l:l + 1, PB_GNG:PB_GNG + 512].partition_broadcast(128), reads=(), writes=[k(zq[0])])
            S_.dma("sp", "pbc1", gnb, pb_d[l:l + 1, PB_GNB:PB_GNB + 512].partition_broadcast(128), reads=(), writes=[k(zq[1])])
            wv, wk_ = w_next()
            for c in range(NCH):
                p, pk = nps()
                mm(p[:, :], [(hT[:, kk, c * 128:(c + 1) * 128], wv[:, kk, :]) for kk in range(KC)], [wk_, k(hT, c)], [pk])
                act(g_, p[:, :], AF.Gelu, [pk], [k(cacc[1]), k(cs1, c, 0)], accum_out=cs1[:, 4 * c:4 * c + 1])
                act(jk[:, 0:512], g_, AF.Square, [k(cacc[1])], [k(jk), k(cs1, c, 1)], accum_out=cs1[:, 4 * c + 1:4 * c + 2])
                m_ = cs1[:, 4 * c:4 * c + 1]
                q1 = cs1[:, 4 * c + 1:4 * c + 2]
                v_ = cs1[:, 4 * c + 3:4 * c + 4]
                kc_ = [k(cs1, c, i) for i in range(2)]
                ts("dve", m_, m_, 1.0 / 512, None, ALU.mult, None, kc_, [k(cs1, c, 0)])
                tt("dve", v_, m_, m_, ALU.mult, kc_, [k(cs1, c, 3)])
                stt("dve", v_, q1, 1.0 / 512, v_, ALU.mult, ALU.subtract, kc_ + [k(cs1, c, 3)], [k(cs1, c, 3)])
                rsqrt(v_, v_, 1.0, EPS, [k(cs1, c, 3)], [k(cs1, c, 3)])
                ts("dve", g_, g_, m_, v_, ALU.subtract, ALU.mult, [k(cacc[1]), k(cs1, c, 0), k(cs1, c, 3)], [k(cacc[1])])
                tt("pool", g_, g_, gng, ALU.mult, [k(cacc[1]), k(zq[0])], [k(cacc[1])])
                tt("dve", gvn[:, c, :], g_, gnb, ALU.add, [k(cacc[1]), k(zq[1])], [k(gT[1])])
            w_issue()

        def C_main_b(l):
            guT = gT[0]
            gvn = gT[1]
            gmb = cacc[0][:, 0:512]
            g_ = cacc[1][:, 0:512]
            S_.dma("sp", "pbc2", gmb, pb_d[l:l + 1, PB_GMB:PB_GMB + 512].partition_broadcast(128), reads=(), writes=[k(cacc[0])])
            wv, wk_ = w_next()
            for j in range(4):
                p, pk = nps()
                mm(p[:, 0:T], z_fm(wv, wk_, j, p), [wk_] + hT_keys(), [pk])
                act(guT[:, j, 0:T], p[:, 0:T], AF.Gelu, [pk], [k(gT[0])])
            w_issue()
            for c in range(NCH):
                sl = slice(c * 128, (c + 1) * 128)
                p, pk = nps()

                def emit4(p=p, c=c):
                    ins = None
                    for g in range(4):
                        ins = nc.tensor.matmul(p[:, g * 128:(g + 1) * 128], gvn[:, c, g * 128:(g + 1) * 128], WcT[l][:, g, :],
                                               start=True, stop=True)
                    return ins
                S_.op("pe", emit4, [k(gT[1]), k(WcT[l])], [pk])
                tt("dve", g_, p[:, :], gmb, ALU.add, [pk, k(cacc[0])], [k(cacc[1])])
                tt("dve", guT[:, :, sl], g_.rearrange("p (g t) -> p g t", g=4), guT[:, :, sl], ALU.mult,
                   [k(cacc[1]), k(gT[0])], [k(gT[0])])

        def C_out(l):
            C_main_b(l)
            wv, wk_ = w_next()
            accum_into_x(gT[0], k(gT[0]), wv, wk_)
            w_issue()

        def ffn(l, nxt=None, tt_=0):
            NB_ = DFF // 512
            if nxt is not None:
                wg_prefetch(nxt)
            if l == depth - 1:
                fg_prefetch()
            for jf in range(NB_):
                g_ = gT[jf % 2]
                wv, wk_ = w_next()
                for j in range(4):
                    r_ = rt[j % 2]
                    p, pk = nps()
                    mm(p[:, 0:T], z_fm(wv, wk_, j, p), [wk_] + hT_keys(), [pk])
                    act(r_[:, :], p[:, 0:T], AF.Relu, [pk], [k(r_)])
                    tt("pool", g_[:, j, :], r_[:, :], r_[:, :], ALU.mult, [k(r_)], [k(g_)])
                w_issue()
                wv, wk_ = w_next()
                hk = None
                if jf == NB_ - 1:
                    if l + 1 < depth:
                        hk = norm_hook(l + 1, PP_LNMIX)
                    elif tt_ + 1 < NT:
                        def hk(c, tt_=tt_):
                            final_chunk(tt_, c)
                            r0 = (tt_ + 1) * T + c * 128
                            S_.dma("sp", f"xin{c}", x[:, c, :], x_d[r0:r0 + 128, :], reads=(), writes=[k(x, c)])
                            if c >= 1:
                                norm_act(0, c - 1)
                                norm_pe(0, PP_LNMIX, c - 1)
                            if c == NCH - 1:
                                norm_act(0, c)
                                norm_pe(0, PP_LNMIX, c)
                    else:
                        hk = (lambda c: final_chunk(tt_, c))
                accum_into_x(g_, k(g_), wv, wk_, hook=hk)
                w_issue()

        fgt = FA[:, :, :].rearrange("p a b -> p (a b)")[:, 0:D]

        def fg_prefetch():
            S_.dma("sp", "fg", fgt, fg_d[0:1, :].partition_broadcast(128), reads=(), writes=[k(FA, j) for j in range(4)])

        def final_chunk(tt_, c):
            o_t = FB if c % 2 == 0 else FC
            o_ = o_t[:, :, :].rearrange("p a b -> p (a b)")[:, 0:D]
            okeys = [k(FB, j) for j in range(4)] if c % 2 == 0 else [k(FC)]
            act(xs0[:, :], x[:, c, :], AF.Square, [k(x, c)], [k(xs0), k(ss, c)], accum_out=ss[:, c:c + 1])
            rsqrt(rstd[:, c:c + 1], ss[:, c:c + 1], 1.0 / D, EPS, [k(ss, c)], [k(rstd, c)])
            stt("dve", o_, x[:, c, :], rstd[:, c:c + 1], fgt, ALU.mult, ALU.mult,
                [k(x, c), k(rstd, c)] + [k(FA, j) for j in range(4)], okeys)
            r0 = tt_ * T + c * 128
            S_.dma("sp", f"st{c % 2}", y_d[r0:r0 + 128, :], o_, reads=okeys, writes=())

        wg_prefetch(0)
        for tt_ in range(NT):
            if tt_ == 0:
                for c in range(NCH):
                    r0 = tt_ * T + c * 128
                    S_.dma("sp", f"xin{c}", x[:, c, :], x_d[r0:r0 + 128, :], reads=(), writes=[k(x, c)])
            for l in range(depth):
                S_.dma("sp", "pbt", pbt[:, :], pb_d[l:l + 1, 0:1024].partition_broadcast(128), reads=(), writes=[k(pbt)])
                if l == 0 and tt_ == 0:
                    norm_to_hT(l, PP_LNMIX)
                gates_p1(l)
                group_A_z(l)
                gates_p2(l)
                group_B(l, 0)
                st_ = group_B(l, 1, pre_out=lambda: A_fin_p1(l))
                A_fin_p2(l, st_)
                nxt = l + 1 if l + 1 < depth else (0 if tt_ + 1 < NT else None)
                ffn(l, nxt, tt_)
        S_.wait_all("sp", ["st0", "st1"])
        for e in ["act", "dve", "pool", "pe"]:
            S_.wait_all(e, ["st0", "st1"])
        assert wpos["used"] == len(wstream), (wpos, len(wstream))
    return nc


def _host_layout(inputs, depth=DEPTH):
    f = lambda a: np.ascontiguousarray(np.asarray(a, dtype=np.float32))
    pp = np.zeros((depth, 128, NPP), np.float32)
    pb = np.zeros((depth, NPB), np.float32)
    for l in range(depth):
        pp[l, :, PP_LNMIX:PP_LNMIX + 16] = f(inputs["ln_mix_g"])[l].reshape(16, 128).T
        pp[l, :, PP_LNMLP:PP_LNMLP + 16] = f(inputs["ln_mlp_g"])[l].reshape(16, 128).T
        cw = f(inputs["conv_w"])[l]
        pp[l, :, PP_CONVW:PP_CONVW + 124] = cw.T.reshape(4, 128, 31).transpose(1, 0, 2).reshape(128, 124)
        pp[l, :, PP_CONVB:PP_CONVB + 4] = f(inputs["conv_b"])[l].reshape(4, 128).T
        pp[l, :, PP_CNG:PP_CNG + 4] = f(inputs["conv_norm_g"])[l].reshape(4, 128).T
        pp[l, :, PP_CNB:PP_CNB + 4] = f(inputs["conv_norm_b"])[l].reshape(4, 128).T
        qw = f(inputs["qk_conv_w"])[l]
        pp[l, :, PP_QKW:PP_QKW + 64] = qw.T.reshape(16, 128, 4).transpose(1, 0, 2).reshape(128, 64)
        pp[l, :, PP_QKB:PP_QKB + 16] = f(inputs["qk_conv_b"])[l].reshape(16, 128).T
        pp[l, 0:4, PP_IGB] = f(inputs["igate_b"])[l]
        pp[l, 0:4, PP_FGB] = f(inputs["fgate_b"])[l]
        pb[l, PB_MNG:PB_MNG + 1024] = f(inputs["mlstm_norm_g"])[l]
        pb[l, PB_GNG:PB_GNG + 512] = f(inputs["gm_norm_g"])[l]
        pb[l, PB_GNB:PB_GNB + 512] = f(inputs["gm_norm_b"])[l]
        pb[l, PB_GMB:PB_GMB + 512] = f(inputs["gm_b"])[l].reshape(512)
    gmwT = np.ascontiguousarray(f(inputs["gm_w"])[:depth].transpose(0, 3, 1, 2))
    cst = np.zeros((128, NCST), np.float32)
    cst[:, CS_IDENT:CS_IDENT + 128] = np.eye(128, dtype=np.float32)
    cst[:, CS_MASK:CS_MASK + 128] = np.tril(np.ones((128, 128), np.float32))
    cst[:, CS_MASKT:CS_MASKT + 128] = np.triu(np.ones((128, 128), np.float32))
    cst[:, CS_ONES:CS_ONES + 128] = 1.0
    return {
        "w_in": f(inputs["w_in"])[:depth], "w_out": f(inputs["w_out"])[:depth],
        "w_up": f(inputs["w_up"])[:depth], "w_down": f(inputs["w_down"])[:depth],
        "pp": pp, "pb": pb, "final_g": f(inputs["final_g"]).reshape(1, D), "gm_wT": gmwT, "cst": cst,
    }


_NC_CACHE = {}


def kernel(**inputs):
    x = np.asarray(inputs["x"], dtype=np.float32)
    B, S, _ = x.shape
    T = 512
    key = (S, T)
    if key not in _NC_CACHE:
        _NC_CACHE[key] = build_nc(S, T)
    nc = _NC_CACHE[key]
    shared = _host_layout(inputs)
    in_maps = []
    for b in range(B):
        m = dict(shared)
        m["x"] = np.ascontiguousarray(x[b])
        in_maps.append(m)
    res = run_bass_kernel_spmd(nc, in_maps, core_ids=list(range(B)))
    return np.stack([np.asarray(r["y"], dtype=np.float32) for r in res.results], axis=0)
```

```python
import numpy as np
import concourse.bass as bass
import concourse.mybir as mybir
from concourse.bass_utils import run_bass_kernel_spmd

F32 = mybir.dt.float32
BF16 = mybir.dt.bfloat16
AF = mybir.ActivationFunctionType
ALU = mybir.AluOpType

D = 2048
NIN = 6152
DFF = 8192
DEPTH = 2
EPS = 1e-6
KC = 16
LN16 = float(np.log(16.0))

C_CV, C_CG, C_Q, C_K, C_V, C_O, C_G, C_GU, C_GV = 0, 512, 1024, 2048, 3072, 4096, 5120, 5128, 5640

PP_LNMIX = 0
PP_LNMLP = 16
PP_CONVW = 32
PP_CONVB = 156
PP_CNG = 160
PP_CNB = 164
PP_QKW = 168
PP_QKB = 232
PP_IGB = 248
PP_FGB = 249
NPP = 256
PB_MNG = 0
PB_GNG = 1024
PB_GNB = 1536
PB_GMB = 2048
NPB = 2560
CS_IDENT = 0
CS_MASK = 128
CS_MASKT = 256
CS_ONES = 384
NCST = 512


import threading

_tls = threading.local()


class Coop:
    def __init__(self):
        self.cv = threading.Condition()
        self.cur = -1
        self.done = []
        self.active = False

    def _next(self, i):
        n = len(self.done)
        for d in range(1, n + 1):
            j = (i + d) % n
            if not self.done[j]:
                return j
        return -1

    def switch(self):
        if not self.active:
            return
        i = getattr(_tls, "task", None)
        if i is None:
            return
        w = getattr(_tls, "weight", 1)
        with self.cv:
            for _ in range(w):
                j = self._next(i)
                if j == i or j < 0:
                    return
                self.cur = j
                self.cv.notify_all()
                while self.cur != i:
                    self.cv.wait()

    def run(self, fns, weights=None, pools=None):
        n = len(fns)
        self.done = [False] * n
        errs = []

        def worker(i):
            _tls.task = i
            _tls.weight = (weights or [1] * n)[i]
            _tls.pools = (pools or [None] * n)[i]
            with self.cv:
                while self.cur != i:
                    self.cv.wait()
            try:
                fns[i]()
            except BaseException as e:
                errs.append(e)
            finally:
                with self.cv:
                    self.done[i] = True
                    self.cur = self._next(i)
                    self.cv.notify_all()

        self.active = True
        self.cur = 0
        ths = [threading.Thread(target=worker, args=(i,)) for i in range(n)]
        for t in ths:
            t.start()
        for t in ths:
            t.join()
        self.active = False
        self.cur = -1
        if errs:
            raise errs[0]


COOP = Coop()


class Sched:
    def __init__(self, nc, sems):
        self.nc = nc
        self.eng = {"pe": nc.tensor, "act": nc.scalar, "dve": nc.vector, "pool": nc.gpsimd, "sp": nc.sync}
        self.sem = dict(sems)
        self.scale = {k: 1 for k in sems}
        self.cnt = {k: 0 for k in sems}
        self.seen = {e: {} for e in self.eng}
        self.res = {}

    def add_dma_sem(self, name, sem):
        self.sem[name] = sem
        self.scale[name] = 16
        self.cnt[name] = 0

    def _deps(self, e, reads, writes):
        deps = {}

        def add(p):
            f, idx = p
            if f == e and e == "pe":
                return
            if idx > deps.get(f, 0):
                deps[f] = idx

        for k in reads:
            r = self.res.get(k)
            if r and r[0]:
                add(r[0])
        for k in writes:
            r = self.res.get(k)
            if r:
                if r[0]:
                    add(r[0])
                for p in r[1].items():
                    add(p)
        for f, idx in deps.items():
            if idx > self.seen[e].get(f, 0):
                self.eng[e].wait_ge(self.sem[f], idx * self.scale[f])
                self.seen[e][f] = idx

    def _mark(self, who, idx, reads, writes):
        for k in reads:
            r = self.res.get(k)
            if r is None:
                r = [None, {}]
                self.res[k] = r
            r[1][who] = idx
        for k in writes:
            self.res[k] = [(who, idx), {}]

    def op(self, e, emit, reads=(), writes=()):
        COOP.switch()
        self._deps(e, reads, writes)
        ins = emit()
        self.cnt[e] += 1
        idx = self.cnt[e]
        ins.then_inc(self.sem[e], 1)
        self._mark(e, idx, reads, writes)

    def dma(self, q, dsem, out, in_, reads=(), writes=()):
        COOP.switch()
        self._deps(q, reads, writes)
        ins = self.eng[q].dma_start(out=out, in_=in_)
        self.cnt[dsem] += 1
        ins.then_inc(self.sem[dsem], 16)
        self._mark(dsem, self.cnt[dsem], reads, writes)

    def wait_all(self, e, names):
        for f in names:
            idx = self.cnt[f]
            if idx > self.seen[e].get(f, 0):
                self.eng[e].wait_ge(self.sem[f], idx * self.scale[f])
                self.seen[e][f] = idx


def build_nc(S, T, depth=DEPTH):
    NT = S // T
    NCH = T // 128
    nc = bass.Bass("TRN2", target_bir_lowering=False)
    x_d = nc.dram_tensor("x", [S, D], F32, kind="ExternalInput").ap()
    win_d = nc.dram_tensor("w_in", [depth, D, NIN], F32, kind="ExternalInput").ap()
    wout_d = nc.dram_tensor("w_out", [depth, D, D], F32, kind="ExternalInput").ap()
    wup_d = nc.dram_tensor("w_up", [depth, D, DFF], F32, kind="ExternalInput").ap()
    wdn_d = nc.dram_tensor("w_down", [depth, DFF, D], F32, kind="ExternalInput").ap()
    pp_d = nc.dram_tensor("pp", [depth, 128, NPP], F32, kind="ExternalInput").ap()
    pb_d = nc.dram_tensor("pb", [depth, NPB], F32, kind="ExternalInput").ap()
    fg_d = nc.dram_tensor("final_g", [1, D], F32, kind="ExternalInput").ap()
    gmw_d = nc.dram_tensor("gm_wT", [depth, 128, 4, 128], F32, kind="ExternalInput").ap()
    cst_d = nc.dram_tensor("cst", [128, NCST], F32, kind="ExternalInput").ap()
    y_d = nc.dram_tensor("y", [S, D], F32, kind="ExternalOutput").ap()

    import contextlib
    es = contextlib.ExitStack()
    with es:
        def sb(name, shape, dt):
            return es.enter_context(nc.sbuf_tensor("s_" + name, shape, dt))

        def psum(name, shape, dt):
            return es.enter_context(nc.psum_tensor("p_" + name, shape, dt))

        def semaphore(name):
            return es.enter_context(nc.semaphore("m_" + name))

        S_ = Sched(nc, {e: semaphore("s_" + e) for e in ["pe", "act", "dve", "pool", "sp"]})
        NW = 3
        for i in range(NW):
            S_.add_dma_sem(f"w{i}", semaphore(f"w{i}"))
        for nm in ["misc", "xin0", "xin1", "xin2", "xin3", "st0", "st1", "fg", "pbt", "pbc0", "pbc1", "pbc2", "wg"]:
            S_.add_dma_sem(nm, semaphore(nm))

        x = sb("x", [128, NCH, D], F32)
        hT = sb("hT", [128, KC, T], BF16)
        wbuf = [sb(f"wb{i}", [128, 8192], BF16) for i in range(NW)]
        wg = sb("wg", [128, KC, 8], BF16)
        cst = sb("cst", [128, NCST], F32)
        identb = sb("identb", [128, 128], BF16)
        pp = [sb(f"pp{l}", [128, NPP], F32) for l in range(depth)]
        pbt = sb("pbt", [128, 1024], F32)
        WcT = [sb(f"WcT{l}", [128, 4, 128], BF16) for l in range(depth)]
        nfb = [sb(f"nfb{l}", [4, 1], F32) for l in range(depth)]
        FA = sb("FA", [128, 4, 544], F32)
        FB = sb("FB", [128, 4, 544], F32)
        FC = sb("FC", [128, 4, 512], F32)
        actG = sb("actG", [128, 4, T], BF16)
        qT = sb("qT", [128, 4, T], BF16)
        kT = sb("kT", [128, 4, T], BF16)
        gT = [sb(f"gT{i}", [128, 4, T], BF16) for i in range(2)]
        vaug = sb("vaug", [128, NCH, 2, 257], BF16)
        xs0 = sb("xs0", [128, D], BF16)
        xs = [xs0, xs0]
        ss = sb("ss", [128, 16], F32)
        rstd = sb("rstd", [128, 16], F32)
        Mend = sb("Mend", [4, NCH + 1], F32)
        nMend = sb("nMend", [4, NCH + 1], F32)
        DEC = sb("DEC", [4, NCH], F32)
        sel4 = sb("sel4", [4, 4, 128], F32)
        Bcar = [sb(f"Bcar{l}", [4, 1], F32) for l in range(depth)]
        Mcar = [sb(f"Mcar{l}", [4, 1], F32) for l in range(depth)]
        uP = sb("uP", [4, T], F32)
        tok = sb("tok", [128, NCH, 4, 4], F32)
        decb = sb("decb", [128, 4, NCH], F32)
        Cst = [[sb(f"Cst{l}_{h}", [128, 2, 257], F32) for h in range(4)] for l in range(depth)]
        Cbf1 = [sb(f"Cbf_{h}", [128, 2, 257], BF16) for h in range(4)]
        Cbf = [Cbf1 for l in range(depth)]
        histA = [sb(f"histA{l}", [128, 4, 30], F32) for l in range(depth)]
        histQ = [sb(f"histQ{l}", [128, 16, 3], F32) for l in range(depth)]
        zq = [sb(f"zq{i}", [128, T + 3], F32) for i in range(2)]
        cacc = [sb(f"cacc{i}", [128, T], F32) for i in range(2)]
        ppb = [sb(f"ppb{i}", [128, 128], F32) for i in range(2)]
        pbf = [sb(f"pbf{i}", [128, 128], BF16) for i in range(2)]
        pTs = [sb(f"pTs{i}", [128, 128], BF16) for i in range(2)]
        tB = [sb(f"tB{i}", [128, 257], F32) for i in range(2)]
        tot = [sb(f"tot{i}", [128, 257], F32) for i in range(2)]
        sm = [sb(f"sm{i}", [128, 8], F32) for i in range(2)]
        hb = [sb(f"hb{i}", [128, 256], BF16) for i in range(2)]
        kw = [sb(f"kw{i}", [128, 256], BF16) for i in range(2)]
        rt = cacc
        gt = zq
        cs1 = sb("cs1", [128, 16], F32)
        epsc = sb("epsc", [128, 1], F32)
        gU = zq[0]
        gM = zq[1]
        gL = cacc[0]
        gB = cacc[1]

        ps = [psum(f"ps{i}", [128, 512], F32) for i in range(6)]
        pb = [psum(f"pbk{i}", [128, 1024], BF16) for i in range(2)]

        state = {"ps": 0, "pb": 0, "tmp": 0}

        def nps():
            pools = getattr(_tls, "pools", None)
            if pools is not None:
                lst = pools["ps"]
                i = lst[pools["psi"] % len(lst)]
                pools["psi"] += 1
                return ps[i], ("ps", i)
            i = state["ps"]
            state["ps"] = (i + 1) % 6
            return ps[i], ("ps", i)

        def npb():
            pools = getattr(_tls, "pools", None)
            if pools is not None:
                lst = pools["pb"]
                i = lst[pools["pbi"] % len(lst)]
                pools["pbi"] += 1
                return pb[i], ("pb", i)
            i = state["pb"]
            state["pb"] = (i + 1) % 2
            return pb[i], ("pb", i)

        def k(t, *idx):
            return (t.name,) + idx

        def mm(out_ap, pairs, reads, writes):
            def emit():
                n = len(pairs)
                ins = None
                for i, (l, r) in enumerate(pairs):
                    ins = nc.tensor.matmul(out_ap, l, r, start=(i == 0), stop=(i == n - 1))
                return ins
            S_.op("pe", emit, reads, writes)

        def tr(out_ap, in_ap, ident_ap, reads, writes):
            S_.op("pe", lambda: nc.tensor.transpose(out_ap, in_ap, ident_ap), reads, writes)

        def act(out, in_, func, reads, writes, **kw_):
            S_.op("act", lambda: nc.scalar.activation(out=out, in_=in_, func=func, **kw_), reads, writes)

        def ts(e, out, in0, s1, s2, op0, op1, reads, writes):
            eng = nc.vector if e == "dve" else nc.gpsimd
            if op1 is None:
                S_.op(e, lambda: eng.tensor_scalar(out=out, in0=in0, scalar1=s1, scalar2=None, op0=op0), reads, writes)
            else:
                S_.op(e, lambda: eng.tensor_scalar(out=out, in0=in0, scalar1=s1, scalar2=s2, op0=op0, op1=op1), reads, writes)

        def tt(e, out, in0, in1, op, reads, writes):
            eng = nc.vector if e == "dve" else nc.gpsimd
            S_.op(e, lambda: eng.tensor_tensor(out=out, in0=in0, in1=in1, op=op), reads, writes)

        def stt(e, out, in0, scalar, in1, op0, op1, reads, writes):
            eng = nc.vector if e == "dve" else nc.gpsimd
            S_.op(e, lambda: eng.scalar_tensor_tensor(out=out, in0=in0, scalar=scalar, in1=in1, op0=op0, op1=op1), reads, writes)

        def rsqrt(out, in_, scale, eps, reads, writes):
            assert eps == EPS
            act(out, in_, AF.Ln, list(reads) + [k(epsc)], writes, scale=scale, bias=epsc[:, 0:1])
            act(out, out, AF.Exp, writes, writes, scale=-0.5)

        def cp(e, out, in_, reads, writes):
            if e == "act":
                S_.op("act", lambda: nc.scalar.copy(out=out, in_=in_), reads, writes)
            else:
                eng = nc.vector if e == "dve" else nc.gpsimd
                S_.op(e, lambda: eng.tensor_copy(out=out, in_=in_), reads, writes)

        def memset(e, ap, val, writes):
            eng = nc.vector if e == "dve" else nc.gpsimd
            S_.op(e, lambda: eng.memset(ap, val), (), writes)

        wstream = []
        for tt_ in range(NT):
            for l in range(depth):
                def wi(c0):
                    return ("in", win_d[l, :, c0:c0 + 512].rearrange("(k p) n -> p k n", p=128))

                def wo(r0):
                    return ("row", wout_d[l, r0:r0 + 512, :].rearrange("(k p) n -> p k n", p=128))
                seq = [wi(C_CG), wi(C_CV),
                       wi(C_Q), wi(C_K), wi(C_V), wi(C_O), wi(C_GV), wo(512),
                       wi(C_Q + 512), wi(C_K + 512), wi(C_V + 512), wi(C_O + 512), wi(C_GU), wo(1536), wo(1024),
                       wo(0)]
                for j in range(DFF // 512):
                    seq.append(("in", wup_d[l, :, j * 512:(j + 1) * 512].rearrange("(k p) n -> p k n", p=128)))
                    seq.append(("row", wdn_d[l, j * 512:(j + 1) * 512, :].rearrange("(k p) n -> p k n", p=128)))
                wstream.extend(seq)
        wpos = {"issued": 0, "used": 0}

        def w_issue():
            i = wpos["issued"]
            if i >= len(wstream):
                return
            kind, src = wstream[i]
            slot = i % NW
            if kind == "in":
                dst = wbuf[slot][:, :].rearrange("p (k n) -> p k n", k=16)
            else:
                dst = wbuf[slot][:, :].rearrange("p (k n) -> p k n", k=4)
            S_.dma("pool", f"w{slot}", dst, src, reads=(), writes=[("wb", slot)])
            wpos["issued"] = i + 1

        def w_next():
            i = wpos["used"]
            wpos["used"] = i + 1
            kind, _ = wstream[i]
            slot = i % NW
            if kind == "in":
                v = wbuf[slot][:, :].rearrange("p (k n) -> p k n", k=16)
            else:
                v = wbuf[slot][:, :].rearrange("p (k n) -> p k n", k=4)
            return v, ("wb", slot)

        S_.dma("sp", "misc", cst[:, :], cst_d[:, :], writes=[k(cst)])
        for l in range(depth):
            S_.dma("sp", "misc", pp[l][:, :], pp_d[l, :, :], writes=[k(pp[l])])
        for l in range(depth):
            S_.dma("sp", "misc", FC[:, :, l * 128:(l + 1) * 128], gmw_d[l, :, :, :], reads=(), writes=[k(FC)])
        for e in ["pe", "act", "dve", "pool"]:
            S_.wait_all(e, ["misc"])
        for i in range(NW):
            w_issue()
        cp("dve", identb[:, :], cst[:, CS_IDENT:CS_IDENT + 128], [k(cst)], [k(identb)])
        memset("dve", epsc[:, :], EPS, [k(epsc)])
        memset("dve", sel4[:, :, :], 0.0, [k(sel4)])
        for h in range(4):
            ts("dve", sel4[:, h, :], cst[0:4, CS_ONES:CS_ONES + 128], cst[0:4, CS_IDENT + h:CS_IDENT + h + 1], None, ALU.mult, None,
               [k(cst)], [k(sel4)])
        for l in range(depth):
            ts("dve", nfb[l][:, :], pp[l][0:4, PP_FGB:PP_FGB + 1], -1.0, None, ALU.mult, None, [k(pp[l])], [k(nfb[l])])
            memset("dve", Bcar[l][:, :], 0.0, [k(Bcar[l])])
            memset("dve", Mcar[l][:, :], 0.0, [k(Mcar[l])])
            memset("dve", histA[l][:, :, :], 0.0, [k(histA[l])])
            memset("dve", histQ[l][:, :, :], 0.0, [k(histQ[l])])
            for h in range(4):
                memset("dve", Cst[l][h][:, :, :], 0.0, [k(Cst[l][h])])
            for g in range(4):
                tt("dve", WcT[l][:, g, :], FC[:, g, l * 128:(l + 1) * 128], cst[:, CS_MASKT:CS_MASKT + 128], ALU.mult,
                   [k(FC), k(cst)], [k(WcT[l])])
        memset("dve", vaug[:, :, :, 256:257], 1.0, [k(vaug)])

        def norm_act(l, c):
            act(hT[:, :, c * 128:(c + 1) * 128], x[:, c, :].rearrange("p (a b) -> p a b", a=KC), AF.Square,
                [k(x, c)], [k(hT, c), k(ss, c)], accum_out=ss[:, c:c + 1])
            rsqrt(rstd[:, c:c + 1], ss[:, c:c + 1], 1.0 / D, EPS, [k(ss, c)], [k(rstd, c)])
            act(xs0[:, :], x[:, c, :], AF.Copy, [k(x, c), k(rstd, c)], [k(xs0)], scale=rstd[:, c:c + 1])

        def norm_pe(l, goff, c):
            xb = xs0
            for kq in range(4):
                pbk, pk = npb()
                for i in range(4):
                    kk = kq * 4 + i
                    tr(pbk[:, i * 128:(i + 1) * 128], xb[:, kk * 128:(kk + 1) * 128], identb[:, :],
                       [k(xb), k(identb)], [pk])
                gb = pp[l][:, goff + kq * 4:goff + kq * 4 + 4].unsqueeze(2).to_broadcast([128, 4, 128])
                tt("dve", hT[:, kq * 4:kq * 4 + 4, c * 128:(c + 1) * 128],
                   pbk[:, 0:512].rearrange("p (a b) -> p a b", a=4), gb, ALU.mult, [pk, k(pp[l])], [k(hT, c)])

        def norm_to_hT(l, goff):
            for c in range(NCH):
                norm_act(l, c)
                norm_pe(l, goff, c)

        def norm_hook(l, goff):
            def hook(c):
                if c >= 1:
                    norm_pe(l, goff, c - 1)
                norm_act(l, c)
                if c == NCH - 1:
                    norm_pe(l, goff, c)
            return hook

        def hT_keys():
            return [k(hT, c) for c in range(NCH)]

        def z_fm(wv, wk_, j, out_ps):
            pairs = [(wv[:, kk, j * 128:(j + 1) * 128], hT[:, kk, 0:T]) for kk in range(KC)]
            return pairs

        def accum_into_x(aT, akey, wv, wkey, hook=None):
            for c in range(NCH):
                for db in range(4):
                    p, pk = nps()
                    mm(p[:, :], [(aT[:, kc, c * 128:(c + 1) * 128], wv[:, kc, db * 512:(db + 1) * 512]) for kc in range(4)],
                       [akey, wkey], [pk])
                    tt("dve", x[:, c, db * 512:(db + 1) * 512], x[:, c, db * 512:(db + 1) * 512], p[:, :], ALU.add,
                       [pk, k(x, c)], [k(x, c)])
                if hook is not None:
                    hook(c)

        def wg_prefetch(l):
            S_.dma("pool", "wg", wg[:, :, :], win_d[l, :, C_G:C_G + 8].rearrange("(k p) n -> p k n", p=128),
                   reads=(), writes=[k(wg)])

        def gates_p1(l):
            pI, kI = nps()
            pF, kF = nps()
            mm(pI[0:4, 0:T], [(wg[:, kk, 0:4], hT[:, kk, 0:T]) for kk in range(KC)], [k(wg)] + hT_keys(), [kI])
            mm(pF[0:4, 0:T], [(wg[:, kk, 4:8], hT[:, kk, 0:T]) for kk in range(KC)], [k(wg)] + hT_keys(), [kF])
            act(gU[0:4, 0:T], pI[0:4, 0:T], AF.Identity, [kI, k(pp[l])], [k(gU)], bias=pp[l][0:4, PP_IGB:PP_IGB + 1])
            act(gL[0:4, 0:T], pF[0:4, 0:T], AF.Exp, [kF, k(nfb[l])], [k(gL)], scale=-1.0, bias=nfb[l][:, 0:1])
            act(gL[0:4, 0:T], gL[0:4, 0:T], AF.Ln, [k(gL)], [k(gL)], bias=1.0)
            S_.op("dve", lambda: nc.vector.tensor_tensor_scan(out=gB[0:4, 0:T], data0=cst[0:4, CS_ONES:CS_ONES + 1].to_broadcast([4, T]), data1=gL[0:4, 0:T],
                                                              initial=Bcar[l][:, 0:1], op0=ALU.mult, op1=ALU.subtract),
                  [k(cst), k(gL), k(Bcar[l])], [k(gB)])
            tt("dve", gU[0:4, 0:T], gU[0:4, 0:T], gB[0:4, 0:T], ALU.subtract, [k(gU), k(gB)], [k(gU)])
            S_.op("dve", lambda: nc.vector.tensor_tensor_scan(out=gM[0:4, 0:T], data0=cst[0:4, CS_ONES:CS_ONES + 1].to_broadcast([4, T]), data1=gU[0:4, 0:T],
                                                              initial=Mcar[l][:, 0:1], op0=ALU.mult, op1=ALU.max),
                  [k(cst), k(gU), k(Mcar[l])], [k(gM)])
            cp("dve", Mend[:, 0:1], Mcar[l][:, 0:1], [k(Mcar[l])], [k(Mend)])
            for c in range(NCH):
                cp("dve", Mend[:, c + 1:c + 2], gM[0:4, c * 128 + 127:c * 128 + 128], [k(gM)], [k(Mend)])
            cp("dve", Bcar[l][:, 0:1], gB[0:4, T - 1:T], [k(gB)], [k(Bcar[l])])
            cp("dve", Mcar[l][:, 0:1], gM[0:4, T - 1:T], [k(gM)], [k(Mcar[l])])
            ts("dve", nMend[:, :], Mend[:, :], -1.0, -LN16, ALU.mult, ALU.add, [k(Mend)], [k(nMend)])
            tt("dve", DEC[:, :], Mend[:, 0:NCH], Mend[:, 1:NCH + 1], ALU.subtract, [k(Mend)], [k(DEC)])
            act(DEC[:, :], DEC[:, :], AF.Exp, [k(DEC)], [k(DEC)])
            ts("dve", FC[0:4, 0, 0:T], gM[0:4, 0:T], -1.0, None, ALU.mult, None, [k(gM)], [k(FC)])
            tt("dve", FC[0:4, 2, 0:T], gB[0:4, 0:T], gM[0:4, 0:T], ALU.add, [k(gB), k(gM)], [k(FC)])
            act(FC[0:4, 2, 0:T], FC[0:4, 2, 0:T], AF.Exp, [k(FC)], [k(FC)], scale=-1.0)
            for c in range(NCH):
                sl = slice(c * 128, (c + 1) * 128)
                act(FC[0:4, 1, sl], gM[0:4, sl], AF.Exp, [k(gM), k(Mend)], [k(FC)], scale=-1.0, bias=Mend[:, c:c + 1])
                act(FC[0:4, 3, sl], gU[0:4, sl], AF.Exp, [k(gU), k(nMend)], [k(FC)], bias=nMend[:, c + 1:c + 2])
            cp("dve", uP[:, :], gU[0:4, 0:T], [k(gU)], [k(uP)])

        def gates_p2(l):
            pT_, kT_ = nps()
            for c in range(NCH):
                for qi in range(4):
                    o0 = (c * 4 + qi) * 4
                    tr(pT_[:, o0:o0 + 4], FC[0:4, qi, c * 128:(c + 1) * 128], cst[0:4, CS_IDENT:CS_IDENT + 4],
                       [k(FC), k(cst)], [kT_])
            cp("dve", tok[:, :, :, :].rearrange("p c q h -> p (c q h)"), pT_[:, 0:NCH * 16], [kT_], [k(tok)])
            pD, kD = nps()
            for h in range(4):
                mm(pD[:, h * NCH:(h + 1) * NCH], [(sel4[:, h, :], DEC[:, :])], [k(sel4), k(DEC)], [kD])
            cp("dve", decb[:, :, :].rearrange("p h c -> p (h c)"), pD[:, 0:4 * NCH], [kD], [k(decb)])
        def qk_block(l, wv, wkey, dstT, gj0):
            for j in range(4):
                gj = gj0 + j
                zb = zq[j % 2]
                ca = cacc[j % 2]
                p, pk = nps()
                mm(p[:, 0:T], z_fm(wv, wkey, j, p), [wkey] + hT_keys(), [pk])
                cp("dve", zb[:, 0:3], histQ[l][:, gj, :], [k(histQ[l], gj)], [k(zb)])
                cp("act", zb[:, 3:3 + T], p[:, 0:T], [pk], [k(zb)])
                cp("dve", histQ[l][:, gj, :], zb[:, T:T + 3], [k(zb)], [k(histQ[l], gj)])
                w0 = PP_QKW + gj * 4
                e = "dve"
                ts(e, ca[:, :], zb[:, 0:T], pp[l][:, w0:w0 + 1], pp[l][:, PP_QKB + gj:PP_QKB + gj + 1], ALU.mult, ALU.add,
                   [k(zb), k(pp[l])], [k(ca)])
                for t_ in range(1, 4):
                    stt(e, ca[:, :], zb[:, t_:t_ + T], pp[l][:, w0 + t_:w0 + t_ + 1], ca[:, :], ALU.mult, ALU.add,
                        [k(zb), k(pp[l]), k(ca)], [k(ca)])
                act(dstT[:, j, :], ca[:, :], AF.Silu, [k(ca)], [k(dstT)])

        def mlstm_head(l, hg, hh):
            GO = FC
            h = hg * 2 + hh
            i2 = hh
            s_ = sm[i2]
            for c in range(NCH):
                sl = slice(c * 128, (c + 1) * 128)
                pS, kS = nps()
                mm(pS[:, 0:128], [(qT[:, hh * 2 + dc, sl], kT[:, hh * 2 + dc, sl]) for dc in range(2)],
                   [k(qT), k(kT)], [kS])
                mm(pS[:, 128:256], [(sel4[:, h, :], uP[:, sl])], [k(sel4), k(uP)], [kS])
                ts("dve", ppb[i2][:, :], pS[:, 128:256], tok[:, c, 0, h:h + 1], 0.0, ALU.add, ALU.min,
                   [kS, k(tok)], [k(ppb[i2])])
                yield
                act(ppb[i2][:, :], ppb[i2][:, :], AF.Exp, [k(ppb[i2])], [k(ppb[i2])])
                yield
                tt("pool", ppb[i2][:, :], ppb[i2][:, :], cst[:, CS_MASK:CS_MASK + 128], ALU.mult,
                   [k(ppb[i2]), k(cst)], [k(ppb[i2])])
                yield
                stt("dve", pbf[i2][:, :], pS[:, 0:128], 0.0625, ppb[i2][:, :], ALU.mult, ALU.mult,
                    [kS, k(ppb[i2])], [k(pbf[i2])])
                yield
                pb1, kb1 = npb()
                tr(pb1[:, 0:128], pbf[i2][:, :], identb[:, :], [k(pbf[i2]), k(identb)], [kb1])
                pB, kB = nps()
                mm(pB[:, 0:257], [(qT[:, hh * 2 + dc, sl], Cbf[l][h][:, dc, :]) for dc in range(2)],
                   [k(qT), k(Cbf[l][h])], [kB])
                yield
                cp("act", pTs[i2][:, :], pb1[:, 0:128], [kb1], [k(pTs[i2])])
                act(tB[i2][:, :], pB[:, 0:257], AF.Copy, [kB, k(tok)], [k(tB[i2])], scale=tok[:, c, 1, h:h + 1])
                yield
                pA, kA = nps()
                mm(pA[:, 0:257], [(pTs[i2][:, :], vaug[:, c, hh, :])], [k(pTs[i2]), k(vaug)], [kA])
                pb3, kb3 = npb()
                for dc in range(2):
                    tr(pb3[:, dc * 128:(dc + 1) * 128], kT[:, hh * 2 + dc, sl], identb[:, :], [k(kT), k(identb)], [kb3])
                yield
                tt("dve", tot[i2][:, :], tB[i2][:, :], pA[:, 0:257], ALU.add, [k(tB[i2]), kA], [k(tot[i2])])
                act(kw[i2][:, :], pb3[:, 0:256], AF.Copy, [kb3, k(tok)], [k(kw[i2])], scale=tok[:, c, 3, h:h + 1])
                yield
                act(s_[:, 0:1], tot[i2][:, 256:257], AF.Abs, [k(tot[i2])], [k(s_)])
                for dc in range(2):
                    pC, kC = nps()
                    mm(pC[:, 0:257], [(kw[i2][:, dc * 128:(dc + 1) * 128], vaug[:, c, hh, :])],
                       [k(kw[i2]), k(vaug)], [kC])
                    stt("dve", Cst[l][h][:, dc, :], Cst[l][h][:, dc, :], decb[:, h, c:c + 1], pC[:, 0:257],
                        ALU.mult, ALU.add, [k(Cst[l][h]), k(decb), kC], [k(Cst[l][h])])
                yield
                tt("dve", s_[:, 0:1], s_[:, 0:1], tok[:, c, 2, h:h + 1], ALU.max, [k(s_), k(tok)], [k(s_)])
                cp("pool", Cbf[l][h][:, :, :], Cst[l][h][:, :, :], [k(Cst[l][h])], [k(Cbf[l][h])])
                yield
                S_.op("dve", lambda: nc.vector.reciprocal(out=s_[:, 1:2], in_=s_[:, 0:1]), [k(s_)], [k(s_)])
                yield
                act(hb[i2][:, :], tot[i2][:, 0:256], AF.Square, [k(tot[i2]), k(s_)], [k(hb[i2]), k(s_, "ss")],
                    scale=s_[:, 1:2], accum_out=s_[:, 2:3])
                yield
                act(s_[:, 3:4], s_[:, 2:3], AF.Ln, [k(s_, "ss")], [k(s_, "r")], scale=1.0 / 256, bias=epsc[:, 0:1])
                yield
                act(s_[:, 3:4], s_[:, 3:4], AF.Exp, [k(s_, "r")], [k(s_, "r")], scale=-0.5)
                yield
                tt("dve", s_[:, 4:5], s_[:, 3:4], s_[:, 1:2], ALU.mult, [k(s_, "r"), k(s_)], [k(s_, "f")])
                yield
                stt("dve", hb[i2][:, :], tot[i2][:, 0:256], s_[:, 4:5], GO[:, c, hh * 256:(hh + 1) * 256],
                    ALU.mult, ALU.mult, [k(tot[i2]), k(s_, "f"), k(FC)], [k(hb[i2])])
                yield
                pb2, kb2 = npb()
                for ec in range(2):
                    tr(pb2[:, ec * 128:(ec + 1) * 128], hb[i2][:, ec * 128:(ec + 1) * 128], identb[:, :],
                       [k(hb[i2]), k(identb)], [kb2])
                yield
                cp("act", actG[:, hh * 2:hh * 2 + 2, sl], pb2[:, 0:256].rearrange("p (e t) -> p e t", e=2),
                   [kb2], [k(actG)])
                yield

        def conv_chunk(l, j, e):
            acc = FA[:, j, 0:T]
            a_in = FB
            w0 = PP_CONVW + j * 31
            ts(e, acc, a_in[:, j, 0:T], pp[l][:, w0:w0 + 1], pp[l][:, PP_CONVB + j:PP_CONVB + j + 1], ALU.mult, ALU.add,
               [k(FB, j), k(pp[l])], [k(FA, j)])
            yield
            for t_ in range(1, 31):
                stt(e, acc, a_in[:, j, t_:t_ + T], pp[l][:, w0 + t_:w0 + t_ + 1], acc, ALU.mult, ALU.add,
                    [k(FB, j), k(pp[l]), k(FA, j)], [k(FA, j)])
                yield

        def interleave(gens):
            gens = list(gens)
            while gens:
                for g in list(gens):
                    try:
                        next(g)
                    except StopIteration:
                        gens.remove(g)

        def group_B(l, hg, pre_out=None):
            for hh in range(2):
                h = hg * 2 + hh
                cp("act", Cbf[l][h][:, :, :], Cst[l][h][:, :, :], [k(Cst[l][h])], [k(Cbf[l][h])])
            def zphase():
                wv, wk_ = w_next()
                qk_block(l, wv, wk_, qT, hg * 4)
                w_issue()
                wv, wk_ = w_next()
                qk_block(l, wv, wk_, kT, 8 + hg * 4)
                w_issue()
                wv, wk_ = w_next()
                for c in range(NCH):
                    p, pk = nps()
                    mm(p[:, :], [(hT[:, kk, c * 128:(c + 1) * 128], wv[:, kk, :]) for kk in range(KC)], [wk_, k(hT, c)], [pk])
                    cp("act", vaug[:, c, :, 0:256], p[:, :].rearrange("p (h e) -> p h e", h=2), [pk], [k(vaug)])
                w_issue()
                wv, wk_ = w_next()
                for c in range(NCH):
                    p, pk = nps()
                    mm(p[:, :], [(hT[:, kk, c * 128:(c + 1) * 128], wv[:, kk, :]) for kk in range(KC)], [wk_, k(hT, c)], [pk])
                    act(FC[:, c, 0:512], p[:, :], AF.Sigmoid, [pk], [k(FC)])
                    tt("pool", FC[:, c, 0:512], FC[:, c, 0:512], pbt[:, PB_MNG + hg * 512:PB_MNG + (hg + 1) * 512], ALU.mult,
                       [k(FC), k(pbt)], [k(FC)])
                w_issue()

            def drain(g):
                for _ in g:
                    pass
            COOP.run([zphase, lambda: drain(conv_chunk(l, 2 * hg, "dve"))], weights=[1, 3], pools=[None, None])
            fns = [lambda: drain(mlstm_head(l, hg, 0)), lambda: drain(mlstm_head(l, hg, 1)),
                   lambda: drain(conv_chunk(l, 2 * hg + 1, "dve")),
                   (lambda: C_main(l)) if hg == 0 else (lambda: C_out(l))]
            pools = [{"ps": [0, 1], "pb": [0], "psi": 0, "pbi": 0}, {"ps": [2, 3], "pb": [1], "psi": 0, "pbi": 0},
                     None, {"ps": [4, 5], "pb": [0], "psi": 0, "pbi": 0}]
            COOP.run(fns, weights=[1, 1, 4, 1], pools=pools)
            st_ = pre_out() if pre_out is not None else None
            wv, wk_ = w_next()
            accum_into_x(actG, k(actG), wv, wk_)
            w_issue()
            return st_

        def group_A_z(l):
            sg = FA
            a_in = FB
            wv, wk_ = w_next()
            for j in range(4):
                p, pk = nps()
                mm(p[:, 0:T], z_fm(wv, wk_, j, p), [wk_] + hT_keys(), [pk])
                act(sg[:, j, 0:T], p[:, 0:T], AF.Sigmoid, [pk], [k(FA, j)])
            w_issue()
            wv, wk_ = w_next()
            for j in range(4):
                p, pk = nps()
                mm(p[:, 0:T], z_fm(wv, wk_, j, p), [wk_] + hT_keys(), [pk])
                cp("pool", a_in[:, j, 0:30], histA[l][:, j, :], [k(histA[l], j)], [k(FB, j)])
                tt("dve", a_in[:, j, 30:30 + T], p[:, 0:T], sg[:, j, 0:T], ALU.mult, [pk, k(FA, j)], [k(FB, j)])
                cp("pool", histA[l][:, j, :], a_in[:, j, T:T + 30], [k(FB, j)], [k(histA[l], j)])
            w_issue()

        def A_fin_p1(l):
            sg = FA
            a_in = FB
            for j in range(4):
                act(a_in[:, j, 0:T], sg[:, j, 0:T], AF.Square, [k(FA, j)], [k(FB, j)])
            p1, k1 = nps()
            p2, k2 = nps()
            ones = cst[:, CS_ONES:CS_ONES + 128]
            mm(p1[:, 0:T], [(ones, sg[:, j, 0:T]) for j in range(4)], [k(cst)] + [k(FA, j) for j in range(4)], [k1])
            mm(p2[:, 0:T], [(ones, a_in[:, j, 0:T]) for j in range(4)], [k(cst)] + [k(FB, j) for j in range(4)], [k2])
            mean = FC[:, 0, 0:T]
            var = FC[:, 1, 0:T]
            tmp = FC[:, 2, 0:T]
            ts("dve", mean, p1[:, 0:T], 1.0 / 512, None, ALU.mult, None, [k1], [k(FC)])
            tt("dve", tmp, mean, mean, ALU.mult, [k(FC)], [k(FC)])
            stt("dve", var, p2[:, 0:T], 1.0 / 512, tmp, ALU.mult, ALU.subtract, [k2, k(FC)], [k(FC)])
            rsqrt(var, var, 1.0, EPS, [k(FC)], [k(FC)])
            return None

        def A_fin_p2(l, st_):
            sg = FA
            actA = gT[1]
            mean = FC[:, 0, 0:T]
            var = FC[:, 1, 0:T]
            for j in range(4):
                e = "dve" if j % 2 == 0 else "pool"
                tt(e, sg[:, j, 0:T], sg[:, j, 0:T], mean, ALU.subtract, [k(FA, j), k(FC)], [k(FA, j)])
                tt(e, sg[:, j, 0:T], sg[:, j, 0:T], var, ALU.mult, [k(FA, j), k(FC)], [k(FA, j)])
                act(actA[:, j, :], sg[:, j, 0:T], AF.Silu, [k(FA, j), k(pp[l])], [k(gT[1])],
                    scale=pp[l][:, PP_CNG + j:PP_CNG + j + 1], bias=pp[l][:, PP_CNB + j:PP_CNB + j + 1])
            wv, wk_ = w_next()
            accum_into_x(actA, k(gT[1]), wv, wk_, hook=norm_hook(l, PP_LNMLP))
            w_issue()

        def C_main(l):
            guT = gT[0]
            gvn = gT[1]
            gng = zq[0][:, 0:512]
            gnb = zq[1][:, 0:512]
            gmb = cacc[0][:, 0:512]
            g_ = cacc[1][:, 0:512]
            jk = xs[0]
            S_.dma("sp", "pbc0", gng, pb_d[l:l + 1, PB_GNG:PB_GNG + 512].partition_broadcast(128), reads=(), writes=[k(zq[0])])
            S_.dma("sp", "pbc1", gnb, pb_d[l:l + 1, PB_GNB:PB_GNB + 512].partition_broadcast(128), reads=(), writes=[k(zq[1])])
            wv, wk_ = w_next()
            for c in range(NCH):
                p, pk = nps()
                mm(p[:, :], [(hT[:, kk, c * 128:(c + 1) * 128], wv[:, kk, :]) for kk in range(KC)], [wk_, k(hT, c)], [pk])
                act(g_, p[:, :], AF.Gelu, [pk], [k(cacc[1]), k(cs1, c, 0)], accum_out=cs1[:, 4 * c:4 * c + 1])
                act(jk[:, 0:512], g_, AF.Square, [k(cacc[1])], [k(jk), k(cs1, c, 1)], accum_out=cs1[:, 4 * c + 1:4 * c + 2])
                m_ = cs1[:, 4 * c:4 * c + 1]
                q1 = cs1[:, 4 * c + 1:4 * c + 2]
                v_ = cs1[:, 4 * c + 3:4 * c + 4]
                kc_ = [k(cs1, c, i) for i in range(2)]
                ts("dve", m_, m_, 1.0 / 512, None, ALU.mult, None, kc_, [k(cs1, c, 0)])
                tt("dve", v_, m_, m_, ALU.mult, kc_, [k(cs1, c, 3)])
                stt("dve", v_, q1, 1.0 / 512, v_, ALU.mult, ALU.subtract, kc_ + [k(cs1, c, 3)], [k(cs1, c, 3)])
                rsqrt(v_, v_, 1.0, EPS, [k(cs1, c, 3)], [k(cs1, c, 3)])
                ts("dve", g_, g_, m_, v_, ALU.subtract, ALU.mult, [k(cacc[1]), k(cs1, c, 0), k(cs1, c, 3)], [k(cacc[1])])
                tt("pool", g_, g_, gng, ALU.mult, [k(cacc[1]), k(zq[0])], [k(cacc[1])])
                tt("dve", gvn[:, c, :], g_, gnb, ALU.add, [k(cacc[1]), k(zq[1])], [k(gT[1])])
            w_issue()

        def C_main_b(l):
            guT = gT[0]
            gvn = gT[1]
            gmb = cacc[0][:, 0:512]
            g_ = cacc[1][:, 0:512]
            S_.dma("sp", "pbc2", gmb, pb_d[l:l + 1, PB_GMB:PB_GMB + 512].partition_broadcast(128), reads=(), writes=[k(cacc[0])])
            wv, wk_ = w_next()
            for j in range(4):
                p, pk = nps()
                mm(p[:, 0:T], z_fm(wv, wk_, j, p), [wk_] + hT_keys(), [pk])
                act(guT[:, j, 0:T], p[:, 0:T], AF.Gelu, [pk], [k(gT[0])])
            w_issue()
            for c in range(NCH):
                sl = slice(c * 128, (c + 1) * 128)
                p, pk = nps()

                def emit4(p=p, c=c):
                    ins = None
                    for g in range(4):
                        ins = nc.tensor.matmul(p[:, g * 128:(g + 1) * 128], gvn[:, c, g * 128:(g + 1) * 128], WcT[l][:, g, :],
                                               start=True, stop=True)
                    return ins
                S_.op("pe", emit4, [k(gT[1]), k(WcT[l])], [pk])
                tt("dve", g_, p[:, :], gmb, ALU.add, [pk, k(cacc[0])], [k(cacc[1])])
                tt("dve", guT[:, :, sl], g_.rearrange("p (g t) -> p g t", g=4), guT[:, :, sl], ALU.mult,
                   [k(cacc[1]), k(gT[0])], [k(gT[0])])

        def C_out(l):
            C_main_b(l)
            wv, wk_ = w_next()
            accum_into_x(gT[0], k(gT[0]), wv, wk_)
            w_issue()

        def ffn(l, nxt=None, tt_=0):
            NB_ = DFF // 512
            if nxt is not None:
                wg_prefetch(nxt)
            if l == depth - 1:
                fg_prefetch()
            for jf in range(NB_):
                g_ = gT[jf % 2]
                wv, wk_ = w_next()
                for j in range(4):
                    r_ = rt[j % 2]
                    p, pk = nps()
                    mm(p[:, 0:T], z_fm(wv, wk_, j, p), [wk_] + hT_keys(), [pk])
                    act(r_[:, :], p[:, 0:T], AF.Relu, [pk], [k(r_)])
                    tt("pool", g_[:, j, :], r_[:, :], r_[:, :], ALU.mult, [k(r_)], [k(g_)])
                w_issue()
                wv, wk_ = w_next()
                hk = None
                if jf == NB_ - 1:
                    if l + 1 < depth:
                        hk = norm_hook(l + 1, PP_LNMIX)
                    elif tt_ + 1 < NT:
                        def hk(c, tt_=tt_):
                            final_chunk(tt_, c)
                            r0 = (tt_ + 1) * T + c * 128
                            S_.dma("sp", f"xin{c}", x[:, c, :], x_d[r0:r0 + 128, :], reads=(), writes=[k(x, c)])
                            if c >= 1:
                                norm_act(0, c - 1)
                                norm_pe(0, PP_LNMIX, c - 1)
                            if c == NCH - 1:
                                norm_act(0, c)
                                norm_pe(0, PP_LNMIX, c)
                    else:
                        hk = (lambda c: final_chunk(tt_, c))
                accum_into_x(g_, k(g_), wv, wk_, hook=hk)
                w_issue()

        fgt = FA[:, :, :].rearrange("p a b -> p (a b)")[:, 0:D]

        def fg_prefetch():
            S_.dma("sp", "fg", fgt, fg_d[0:1, :].partition_broadcast(128), reads=(), writes=[k(FA, j) for j in range(4)])

        def final_chunk(tt_, c):
            o_t = FB if c % 2 == 0 else FC
            o_ = o_t[:, :, :].rearrange("p a b -> p (a b)")[:, 0:D]
            okeys = [k(FB, j) for j in range(4)] if c % 2 == 0 else [k(FC)]
            act(xs0[:, :], x[:, c, :], AF.Square, [k(x, c)], [k(xs0), k(ss, c)], accum_out=ss[:, c:c + 1])
            rsqrt(rstd[:, c:c + 1], ss[:, c:c + 1], 1.0 / D, EPS, [k(ss, c)], [k(rstd, c)])
            stt("dve", o_, x[:, c, :], rstd[:, c:c + 1], fgt, ALU.mult, ALU.mult,
                [k(x, c), k(rstd, c)] + [k(FA, j) for j in range(4)], okeys)
            r0 = tt_ * T + c * 128
            S_.dma("sp", f"st{c % 2}", y_d[r0:r0 + 128, :], o_, reads=okeys, writes=())

        wg_prefetch(0)
        for tt_ in range(NT):
            if tt_ == 0:
                for c in range(NCH):
                    r0 = tt_ * T + c * 128
                    S_.dma("sp", f"xin{c}", x[:, c, :], x_d[r0:r0 + 128, :], reads=(), writes=[k(x, c)])
            for l in range(depth):
                S_.dma("sp", "pbt", pbt[:, :], pb_d[l:l + 1, 0:1024].partition_broadcast(128), reads=(), writes=[k(pbt)])
                if l == 0 and tt_ == 0:
                    norm_to_hT(l, PP_LNMIX)
                gates_p1(l)
                group_A_z(l)
                gates_p2(l)
                group_B(l, 0)
                st_ = group_B(l, 1, pre_out=lambda: A_fin_p1(l))
                A_fin_p2(l, st_)
                nxt = l + 1 if l + 1 < depth else (0 if tt_ + 1 < NT else None)
                ffn(l, nxt, tt_)
        S_.wait_all("sp", ["st0", "st1"])
        for e in ["act", "dve", "pool", "pe"]:
            S_.wait_all(e, ["st0", "st1"])
        assert wpos["used"] == len(wstream), (wpos, len(wstream))
    return nc


def _host_layout(inputs, depth=DEPTH):
    f = lambda a: np.ascontiguousarray(np.asarray(a, dtype=np.float32))
    pp = np.zeros((depth, 128, NPP), np.float32)
    pb = np.zeros((depth, NPB), np.float32)
    for l in range(depth):
        pp[l, :, PP_LNMIX:PP_LNMIX + 16] = f(inputs["ln_mix_g"])[l].reshape(16, 128).T
        pp[l, :, PP_LNMLP:PP_LNMLP + 16] = f(inputs["ln_mlp_g"])[l].reshape(16, 128).T
        cw = f(inputs["conv_w"])[l]
        pp[l, :, PP_CONVW:PP_CONVW + 124] = cw.T.reshape(4, 128, 31).transpose(1, 0, 2).reshape(128, 124)
        pp[l, :, PP_CONVB:PP_CONVB + 4] = f(inputs["conv_b"])[l].reshape(4, 128).T
        pp[l, :, PP_CNG:PP_CNG + 4] = f(inputs["conv_norm_g"])[l].reshape(4, 128).T
        pp[l, :, PP_CNB:PP_CNB + 4] = f(inputs["conv_norm_b"])[l].reshape(4, 128).T
        qw = f(inputs["qk_conv_w"])[l]
        pp[l, :, PP_QKW:PP_QKW + 64] = qw.T.reshape(16, 128, 4).transpose(1, 0, 2).reshape(128, 64)
        pp[l, :, PP_QKB:PP_QKB + 16] = f(inputs["qk_conv_b"])[l].reshape(16, 128).T
        pp[l, 0:4, PP_IGB] = f(inputs["igate_b"])[l]
        pp[l, 0:4, PP_FGB] = f(inputs["fgate_b"])[l]
        pb[l, PB_MNG:PB_MNG + 1024] = f(inputs["mlstm_norm_g"])[l]
        pb[l, PB_GNG:PB_GNG + 512] = f(inputs["gm_norm_g"])[l]
        pb[l, PB_GNB:PB_GNB + 512] = f(inputs["gm_norm_b"])[l]
        pb[l, PB_GMB:PB_GMB + 512] = f(inputs["gm_b"])[l].reshape(512)
    gmwT = np.ascontiguousarray(f(inputs["gm_w"])[:depth].transpose(0, 3, 1, 2))
    cst = np.zeros((128, NCST), np.float32)
    cst[:, CS_IDENT:CS_IDENT + 128] = np.eye(128, dtype=np.float32)
    cst[:, CS_MASK:CS_MASK + 128] = np.tril(np.ones((128, 128), np.float32))
    cst[:, CS_MASKT:CS_MASKT + 128] = np.triu(np.ones((128, 128), np.float32))
    cst[:, CS_ONES:CS_ONES + 128] = 1.0
    return {
        "w_in": f(inputs["w_in"])[:depth], "w_out": f(inputs["w_out"])[:depth],
        "w_up": f(inputs["w_up"])[:depth], "w_down": f(inputs["w_down"])[:depth],
        "pp": pp, "pb": pb, "final_g": f(inputs["final_g"]).reshape(1, D), "gm_wT": gmwT, "cst": cst,
    }


_NC_CACHE = {}


def kernel(**inputs):
    x = np.asarray(inputs["x"], dtype=np.float32)
    B, S, _ = x.shape
    T = 512
    key = (S, T)
    if key not in _NC_CACHE:
        _NC_CACHE[key] = build_nc(S, T)
    nc = _NC_CACHE[key]
    shared = _host_layout(inputs)
    in_maps = []
    for b in range(B):
        m = dict(shared)
        m["x"] = np.ascontiguousarray(x[b])
        in_maps.append(m)
    res = run_bass_kernel_spmd(nc, in_maps, core_ids=list(range(B)))
    return np.stack([np.asarray(r["y"], dtype=np.float32) for r in res.results], axis=0)
```

```python
import numpy as np
import concourse.bass as bass
import concourse.mybir as mybir
from concourse.bass_utils import run_bass_kernel_spmd

F32 = mybir.dt.float32
BF16 = mybir.dt.bfloat16
AF = mybir.ActivationFunctionType
ALU = mybir.AluOpType

D = 2048
NIN = 6152
DFF = 8192
DEPTH = 2
EPS = 1e-6
KC = 16
LN16 = float(np.log(16.0))

C_CV, C_CG, C_Q, C_K, C_V, C_O, C_G, C_GU, C_GV = 0, 512, 1024, 2048, 3072, 4096, 5120, 5128, 5640

PP_LNMIX = 0
PP_LNMLP = 16
PP_CONVW = 32
PP_CONVB = 156
PP_CNG = 160
PP_CNB = 164
PP_QKW = 168
PP_QKB = 232
PP_IGB = 248
PP_FGB = 249
NPP = 256
PB_MNG = 0
PB_GNG = 1024
PB_GNB = 1536
PB_GMB = 2048
NPB = 2560
CS_IDENT = 0
CS_MASK = 128
CS_MASKT = 256
CS_ONES = 384
NCST = 512


import threading

_tls = threading.local()


class Coop:
    def __init__(self):
        self.cv = threading.Condition()
        self.cur = -1
        self.done = []
        self.active = False

    def _next(self, i):
        n = len(self.done)
        for d in range(1, n + 1):
            j = (i + d) % n
            if not self.done[j]:
                return j
        return -1

    def switch(self):
        if not self.active:
            return
        i = getattr(_tls, "task", None)
        if i is None:
            return
        w = getattr(_tls, "weight", 1)
        with self.cv:
            for _ in range(w):
                j = self._next(i)
                if j == i or j < 0:
                    return
                self.cur = j
                self.cv.notify_all()
                while self.cur != i:
                    self.cv.wait()

    def run(self, fns, weights=None, pools=None):
        n = len(fns)
        self.done = [False] * n
        errs = []

        def worker(i):
            _tls.task = i
            _tls.weight = (weights or [1] * n)[i]
            _tls.pools = (pools or [None] * n)[i]
            with self.cv:
                while self.cur != i:
                    self.cv.wait()
            try:
                fns[i]()
            except BaseException as e:
                errs.append(e)
            finally:
                with self.cv:
                    self.done[i] = True
                    self.cur = self._next(i)
                    self.cv.notify_all()

        self.active = True
        self.cur = 0
        ths = [threading.Thread(target=worker, args=(i,)) for i in range(n)]
        for t in ths:
            t.start()
        for t in ths:
            t.join()
        self.active = False
        self.cur = -1
        if errs:
            raise errs[0]


COOP = Coop()


class Sched:
    def __init__(self, nc, sems):
        self.nc = nc
        self.eng = {"pe": nc.tensor, "act": nc.scalar, "dve": nc.vector, "pool": nc.gpsimd, "sp": nc.sync}
        self.sem = dict(sems)
        self.scale = {k: 1 for k in sems}
        self.cnt = {k: 0 for k in sems}
        self.seen = {e: {} for e in self.eng}
        self.res = {}

    def add_dma_sem(self, name, sem):
        self.sem[name] = sem
        self.scale[name] = 16
        self.cnt[name] = 0

    def _deps(self, e, reads, writes):
        deps = {}

        def add(p):
            f, idx = p
            if f == e and e == "pe":
                return
            if idx > deps.get(f, 0):
                deps[f] = idx

        for k in reads:
            r = self.res.get(k)
            if r and r[0]:
                add(r[0])
        for k in writes:
            r = self.res.get(k)
            if r:
                if r[0]:
                    add(r[0])
                for p in r[1].items():
                    add(p)
        for f, idx in deps.items():
            if idx > self.seen[e].get(f, 0):
                self.eng[e].wait_ge(self.sem[f], idx * self.scale[f])
                self.seen[e][f] = idx

    def _mark(self, who, idx, reads, writes):
        for k in reads:
            r = self.res.get(k)
            if r is None:
                r = [None, {}]
                self.res[k] = r
            r[1][who] = idx
        for k in writes:
            self.res[k] = [(who, idx), {}]

    def op(self, e, emit, reads=(), writes=()):
        COOP.switch()
        self._deps(e, reads, writes)
        ins = emit()
        self.cnt[e] += 1
        idx = self.cnt[e]
        ins.then_inc(self.sem[e], 1)
        self._mark(e, idx, reads, writes)

    def dma(self, q, dsem, out, in_, reads=(), writes=()):
        COOP.switch()
        self._deps(q, reads, writes)
        ins = self.eng[q].dma_start(out=out, in_=in_)
        self.cnt[dsem] += 1
        ins.then_inc(self.sem[dsem], 16)
        self._mark(dsem, self.cnt[dsem], reads, writes)

    def wait_all(self, e, names):
        for f in names:
            idx = self.cnt[f]
            if idx > self.seen[e].get(f, 0):
                self.eng[e].wait_ge(self.sem[f], idx * self.scale[f])
                self.seen[e][f] = idx


def build_nc(S, T, depth=DEPTH):
    NT = S // T
    NCH = T // 128
    nc = bass.Bass("TRN2", target_bir_lowering=False)
    x_d = nc.dram_tensor("x", [S, D], F32, kind="ExternalInput").ap()
    win_d = nc.dram_tensor("w_in", [depth, D, NIN], F32, kind="ExternalInput").ap()
    wout_d = nc.dram_tensor("w_out", [depth, D, D], F32, kind="ExternalInput").ap()
    wup_d = nc.dram_tensor("w_up", [depth, D, DFF], F32, kind="ExternalInput").ap()
    wdn_d = nc.dram_tensor("w_down", [depth, DFF, D], F32, kind="ExternalInput").ap()
    pp_d = nc.dram_tensor("pp", [depth, 128, NPP], F32, kind="ExternalInput").ap()
    pb_d = nc.dram_tensor("pb", [depth, NPB], F32, kind="ExternalInput").ap()
    fg_d = nc.dram_tensor("final_g", [1, D], F32, kind="ExternalInput").ap()
    gmw_d = nc.dram_tensor("gm_wT", [depth, 128, 4, 128], F32, kind="ExternalInput").ap()
    cst_d = nc.dram_tensor("cst", [128, NCST], F32, kind="ExternalInput").ap()
    y_d = nc.dram_tensor("y", [S, D], F32, kind="ExternalOutput").ap()

    import contextlib
    es = contextlib.ExitStack()
    with es:
        def sb(name, shape, dt):
            return es.enter_context(nc.sbuf_tensor("s_" + name, shape, dt))

        def psum(name, shape, dt):
            return es.enter_context(nc.psum_tensor("p_" + name, shape, dt))

        def semaphore(name):
            return es.enter_context(nc.semaphore("m_" + name))

        S_ = Sched(nc, {e: semaphore("s_" + e) for e in ["pe", "act", "dve", "pool", "sp"]})
        NW = 3
        for i in range(NW):
            S_.add_dma_sem(f"w{i}", semaphore(f"w{i}"))
        for nm in ["misc", "xin0", "xin1", "xin2", "xin3", "st0", "st1", "fg", "pbt", "pbc0", "pbc1", "pbc2", "wg"]:
            S_.add_dma_sem(nm, semaphore(nm))

        x = sb("x", [128, NCH, D], F32)
        hT = sb("hT", [128, KC, T], BF16)
        wbuf = [sb(f"wb{i}", [128, 8192], BF16) for i in range(NW)]
        wg = sb("wg", [128, KC, 8], BF16)
        cst = sb("cst", [128, NCST], F32)
        identb = sb("identb", [128, 128], BF16)
        pp = [sb(f"pp{l}", [128, NPP], F32) for l in range(depth)]
        pbt = sb("pbt", [128, 1024], F32)
        WcT = [sb(f"WcT{l}", [128, 4, 128], BF16) for l in range(depth)]
        nfb = [sb(f"nfb{l}", [4, 1], F32) for l in range(depth)]
        FA = sb("FA", [128, 4, 544], F32)
        FB = sb("FB", [128, 4, 544], F32)
        FC = sb("FC", [128, 4, 512], F32)
        actG = sb("actG", [128, 4, T], BF16)
        qT = sb("qT", [128, 4, T], BF16)
        kT = sb("kT", [128, 4, T], BF16)
        gT = [sb(f"gT{i}", [128, 4, T], BF16) for i in range(2)]
        vaug = sb("vaug", [128, NCH, 2, 257], BF16)
        xs0 = sb("xs0", [128, D], BF16)
        xs = [xs0, xs0]
        ss = sb("ss", [128, 16], F32)
        rstd = sb("rstd", [128, 16], F32)
        Mend = sb("Mend", [4, NCH + 1], F32)
        nMend = sb("nMend", [4, NCH + 1], F32)
        DEC = sb("DEC", [4, NCH], F32)
        sel4 = sb("sel4", [4, 4, 128], F32)
        Bcar = [sb(f"Bcar{l}", [4, 1], F32) for l in range(depth)]
        Mcar = [sb(f"Mcar{l}", [4, 1], F32) for l in range(depth)]
        uP = sb("uP", [4, T], F32)
        tok = sb("tok", [128, NCH, 4, 4], F32)
        decb = sb("decb", [128, 4, NCH], F32)
        Cst = [[sb(f"Cst{l}_{h}", [128, 2, 257], F32) for h in range(4)] for l in range(depth)]
        Cbf1 = [sb(f"Cbf_{h}", [128, 2, 257], BF16) for h in range(4)]
        Cbf = [Cbf1 for l in range(depth)]
        histA = [sb(f"histA{l}", [128, 4, 30], F32) for l in range(depth)]
        histQ = [sb(f"histQ{l}", [128, 16, 3], F32) for l in range(depth)]
        zq = [sb(f"zq{i}", [128, T + 3], F32) for i in range(2)]
        cacc = [sb(f"cacc{i}", [128, T], F32) for i in range(2)]
        ppb = [sb(f"ppb{i}", [128, 128], F32) for i in range(2)]
        pbf = [sb(f"pbf{i}", [128, 128], BF16) for i in range(2)]
        pTs = [sb(f"pTs{i}", [128, 128], BF16) for i in range(2)]
        tB = [sb(f"tB{i}", [128, 257], F32) for i in range(2)]
        tot = [sb(f"tot{i}", [128, 257], F32) for i in range(2)]
        sm = [sb(f"sm{i}", [128, 8], F32) for i in range(2)]
        hb = [sb(f"hb{i}", [128, 256], BF16) for i in range(2)]
        kw = [sb(f"kw{i}", [128, 256], BF16) for i in range(2)]
        rt = cacc
        gt = zq
        cs1 = sb("cs1", [128, 16], F32)
        epsc = sb("epsc", [128, 1], F32)
        gU = zq[0]
        gM = zq[1]
        gL = cacc[0]
        gB = cacc[1]

        ps = [psum(f"ps{i}", [128, 512], F32) for i in range(6)]
        pb = [psum(f"pbk{i}", [128, 1024], BF16) for i in range(2)]

        state = {"ps": 0, "pb": 0, "tmp": 0}

        def nps():
            pools = getattr(_tls, "pools", None)
            if pools is not None:
                lst = pools["ps"]
                i = lst[pools["psi"] % len(lst)]
                pools["psi"] += 1
                return ps[i], ("ps", i)
            i = state["ps"]
            state["ps"] = (i + 1) % 6
            return ps[i], ("ps", i)

        def npb():
            pools = getattr(_tls, "pools", None)
            if pools is not None:
                lst = pools["pb"]
                i = lst[pools["pbi"] % len(lst)]
                pools["pbi"] += 1
                return pb[i], ("pb", i)
            i = state["pb"]
            state["pb"] = (i + 1) % 2
            return pb[i], ("pb", i)

        def k(t, *idx):
            return (t.name,) + idx

        def mm(out_ap, pairs, reads, writes):
            def emit():
                n = len(pairs)
                ins = None
                for i, (l, r) in enumerate(pairs):
                    ins = nc.tensor.matmul(out_ap, l, r, start=(i == 0), stop=(i == n - 1))
                return ins
            S_.op("pe", emit, reads, writes)

        def tr(out_ap, in_ap, ident_ap, reads, writes):
            S_.op("pe", lambda: nc.tensor.transpose(out_ap, in_ap, ident_ap), reads, writes)

        def act(out, in_, func, reads, writes, **kw_):
            S_.op("act", lambda: nc.scalar.activation(out=out, in_=in_, func=func, **kw_), reads, writes)

        def ts(e, out, in0, s1, s2, op0, op1, reads, writes):
            eng = nc.vector if e == "dve" else nc.gpsimd
            if op1 is None:
                S_.op(e, lambda: eng.tensor_scalar(out=out, in0=in0, scalar1=s1, scalar2=None, op0=op0), reads, writes)
            else:
                S_.op(e, lambda: eng.tensor_scalar(out=out, in0=in0, scalar1=s1, scalar2=s2, op0=op0, op1=op1), reads, writes)

        def tt(e, out, in0, in1, op, reads, writes):
            eng = nc.vector if e == "dve" else nc.gpsimd
            S_.op(e, lambda: eng.tensor_tensor(out=out, in0=in0, in1=in1, op=op), reads, writes)

        def stt(e, out, in0, scalar, in1, op0, op1, reads, writes):
            eng = nc.vector if e == "dve" else nc.gpsimd
            S_.op(e, lambda: eng.scalar_tensor_tensor(out=out, in0=in0, scalar=scalar, in1=in1, op0=op0, op1=op1), reads, writes)

        def rsqrt(out, in_, scale, eps, reads, writes):
            assert eps == EPS
            act(out, in_, AF.Ln, list(reads) + [k(epsc)], writes, scale=scale, bias=epsc[:, 0:1])
            act(out, out, AF.Exp, writes, writes, scale=-0.5)

        def cp(e, out, in_, reads, writes):
            if e == "act":
                S_.op("act", lambda: nc.scalar.copy(out=out, in_=in_), reads, writes)
            else:
                eng = nc.vector if e == "dve" else nc.gpsimd
                S_.op(e, lambda: eng.tensor_copy(out=out, in_=in_), reads, writes)

        def memset(e, ap, val, writes):
            eng = nc.vector if e == "dve" else nc.gpsimd
            S_.op(e, lambda: eng.memset(ap, val), (), writes)

        wstream = []
        for tt_ in range(NT):
            for l in range(depth):
                def wi(c0):
                    return ("in", win_d[l, :, c0:c0 + 512].rearrange("(k p) n -> p k n", p=128))

                def wo(r0):
                    return ("row", wout_d[l, r0:r0 + 512, :].rearrange("(k p) n -> p k n", p=128))
                seq = [wi(C_CG), wi(C_CV),
                       wi(C_Q), wi(C_K), wi(C_V), wi(C_O), wi(C_GV), wo(512),
                       wi(C_Q + 512), wi(C_K + 512), wi(C_V + 512), wi(C_O + 512), wi(C_GU), wo(1536), wo(1024),
                       wo(0)]
                for j in range(DFF // 512):
                    seq.append(("in", wup_d[l, :, j * 512:(j + 1) * 512].rearrange("(k p) n -> p k n", p=128)))
                    seq.append(("row", wdn_d[l, j * 512:(j + 1) * 512, :].rearrange("(k p) n -> p k n", p=128)))
                wstream.extend(seq)
        wpos = {"issued": 0, "used": 0}

        def w_issue():
            i = wpos["issued"]
            if i >= len(wstream):
                return
            kind, src = wstream[i]
            slot = i % NW
            if kind == "in":
                dst = wbuf[slot][:, :].rearrange("p (k n) -> p k n", k=16)
            else:
                dst = wbuf[slot][:, :].rearrange("p (k n) -> p k n", k=4)
            S_.dma("pool", f"w{slot}", dst, src, reads=(), writes=[("wb", slot)])
            wpos["issued"] = i + 1

        def w_next():
            i = wpos["used"]
            wpos["used"] = i + 1
            kind, _ = wstream[i]
            slot = i % NW
            if kind == "in":
                v = wbuf[slot][:, :].rearrange("p (k n) -> p k n", k=16)
            else:
                v = wbuf[slot][:, :].rearrange("p (k n) -> p k n", k=4)
            return v, ("wb", slot)

        S_.dma("sp", "misc", cst[:, :], cst_d[:, :], writes=[k(cst)])
        for l in range(depth):
            S_.dma("sp", "misc", pp[l][:, :], pp_d[l, :, :], writes=[k(pp[l])])
        for l in range(depth):
            S_.dma("sp", "misc", FC[:, :, l * 128:(l + 1) * 128], gmw_d[l, :, :, :], reads=(), writes=[k(FC)])
        for e in ["pe", "act", "dve", "pool"]:
            S_.wait_all(e, ["misc"])
        for i in range(NW):
            w_issue()
        cp("dve", identb[:, :], cst[:, CS_IDENT:CS_IDENT + 128], [k(cst)], [k(identb)])
        memset("dve", epsc[:, :], EPS, [k(epsc)])
        memset("dve", sel4[:, :, :], 0.0, [k(sel4)])
        for h in range(4):
            ts("dve", sel4[:, h, :], cst[0:4, CS_ONES:CS_ONES + 128], cst[0:4, CS_IDENT + h:CS_IDENT + h + 1], None, ALU.mult, None,
               [k(cst)], [k(sel4)])
        for l in range(depth):
            ts("dve", nfb[l][:, :], pp[l][0:4, PP_FGB:PP_FGB + 1], -1.0, None, ALU.mult, None, [k(pp[l])], [k(nfb[l])])
            memset("dve", Bcar[l][:, :], 0.0, [k(Bcar[l])])
            memset("dve", Mcar[l][:, :], 0.0, [k(Mcar[l])])
            memset("dve", histA[l][:, :, :], 0.0, [k(histA[l])])
            memset("dve", histQ[l][:, :, :], 0.0, [k(histQ[l])])
            for h in range(4):
                memset("dve", Cst[l][h][:, :, :], 0.0, [k(Cst[l][h])])
            for g in range(4):
                tt("dve", WcT[l][:, g, :], FC[:, g, l * 128:(l + 1) * 128], cst[:, CS_MASKT:CS_MASKT + 128], ALU.mult,
                   [k(FC), k(cst)], [k(WcT[l])])
        memset("dve", vaug[:, :, :, 256:257], 1.0, [k(vaug)])

        def norm_act(l, c):
            act(hT[:, :, c * 128:(c + 1) * 128], x[:, c, :].rearrange("p (a b) -> p a b", a=KC), AF.Square,
                [k(x, c)], [k(hT, c), k(ss, c)], accum_out=ss[:, c:c + 1])
            rsqrt(rstd[:, c:c + 1], ss[:, c:c + 1], 1.0 / D, EPS, [k(ss, c)], [k(rstd, c)])
            act(xs0[:, :], x[:, c, :], AF.Copy, [k(x, c), k(rstd, c)], [k(xs0)], scale=rstd[:, c:c + 1])

        def norm_pe(l, goff, c):
            xb = xs0
            for kq in range(4):
                pbk, pk = npb()
                for i in range(4):
                    kk = kq * 4 + i
                    tr(pbk[:, i * 128:(i + 1) * 128], xb[:, kk * 128:(kk + 1) * 128], identb[:, :],
                       [k(xb), k(identb)], [pk])
                gb = pp[l][:, goff + kq * 4:goff + kq * 4 + 4].unsqueeze(2).to_broadcast([128, 4, 128])
                tt("dve", hT[:, kq * 4:kq * 4 + 4, c * 128:(c + 1) * 128],
                   pbk[:, 0:512].rearrange("p (a b) -> p a b", a=4), gb, ALU.mult, [pk, k(pp[l])], [k(hT, c)])

        def norm_to_hT(l, goff):
            for c in range(NCH):
                norm_act(l, c)
                norm_pe(l, goff, c)

        def norm_hook(l, goff):
            def hook(c):
                if c >= 1:
                    norm_pe(l, goff, c - 1)
                norm_act(l, c)
                if c == NCH - 1:
                    norm_pe(l, goff, c)
            return hook

        def hT_keys():
            return [k(hT, c) for c in range(NCH)]

        def z_fm(wv, wk_, j, out_ps):
            pairs = [(wv[:, kk, j * 128:(j + 1) * 128], hT[:, kk, 0:T]) for kk in range(KC)]
            return pairs

        def accum_into_x(aT, akey, wv, wkey, hook=None):
            for c in range(NCH):
                for db in range(4):
                    p, pk = nps()
                    mm(p[:, :], [(aT[:, kc, c * 128:(c + 1) * 128], wv[:, kc, db * 512:(db + 1) * 512]) for kc in range(4)],
                       [akey, wkey], [pk])
                    tt("dve", x[:, c, db * 512:(db + 1) * 512], x[:, c, db * 512:(db + 1) * 512], p[:, :], ALU.add,
                       [pk, k(x, c)], [k(x, c)])
                if hook is not None:
                    hook(c)

        def wg_prefetch(l):
            S_.dma("pool", "wg", wg[:, :, :], win_d[l, :, C_G:C_G + 8].rearrange("(k p) n -> p k n", p=128),
                   reads=(), writes=[k(wg)])

        def gates_p1(l):
            pI, kI = nps()
            pF, kF = nps()
            mm(pI[0:4, 0:T], [(wg[:, kk, 0:4], hT[:, kk, 0:T]) for kk in range(KC)], [k(wg)] + hT_keys(), [kI])
            mm(pF[0:4, 0:T], [(wg[:, kk, 4:8], hT[:, kk, 0:T]) for kk in range(KC)], [k(wg)] + hT_keys(), [kF])
            act(gU[0:4, 0:T], pI[0:4, 0:T], AF.Identity, [kI, k(pp[l])], [k(gU)], bias=pp[l][0:4, PP_IGB:PP_IGB + 1])
            act(gL[0:4, 0:T], pF[0:4, 0:T], AF.Exp, [kF, k(nfb[l])], [k(gL)], scale=-1.0, bias=nfb[l][:, 0:1])
            act(gL[0:4, 0:T], gL[0:4, 0:T], AF.Ln, [k(gL)], [k(gL)], bias=1.0)
            S_.op("dve", lambda: nc.vector.tensor_tensor_scan(out=gB[0:4, 0:T], data0=cst[0:4, CS_ONES:CS_ONES + 1].to_broadcast([4, T]), data1=gL[0:4, 0:T],
                                                              initial=Bcar[l][:, 0:1], op0=ALU.mult, op1=ALU.subtract),
                  [k(cst), k(gL), k(Bcar[l])], [k(gB)])
            tt("dve", gU[0:4, 0:T], gU[0:4, 0:T], gB[0:4, 0:T], ALU.subtract, [k(gU), k(gB)], [k(gU)])
            S_.op("dve", lambda: nc.vector.tensor_tensor_scan(out=gM[0:4, 0:T], data0=cst[0:4, CS_ONES:CS_ONES + 1].to_broadcast([4, T]), data1=gU[0:4, 0:T],
                                                              initial=Mcar[l][:, 0:1], op0=ALU.mult, op1=ALU.max),
                  [k(cst), k(gU), k(Mcar[l])], [k(gM)])
            cp("dve", Mend[:, 0:1], Mcar[l][:, 0:1], [k(Mcar[l])], [k(Mend)])
            for c in range(NCH):
                cp("dve", Mend[:, c + 1:c + 2], gM[0:4, c * 128 + 127:c * 128 + 128], [k(gM)], [k(Mend)])
            cp("dve", Bcar[l][:, 0:1], gB[0:4, T - 1:T], [k(gB)], [k(Bcar[l])])
            cp("dve", Mcar[l][:, 0:1], gM[0:4, T - 1:T], [k(gM)], [k(Mcar[l])])
            ts("dve", nMend[:, :], Mend[:, :], -1.0, -LN16, ALU.mult, ALU.add, [k(Mend)], [k(nMend)])
            tt("dve", DEC[:, :], Mend[:, 0:NCH], Mend[:, 1:NCH + 1], ALU.subtract, [k(Mend)], [k(DEC)])
            act(DEC[:, :], DEC[:, :], AF.Exp, [k(DEC)], [k(DEC)])
            ts("dve", FC[0:4, 0, 0:T], gM[0:4, 0:T], -1.0, None, ALU.mult, None, [k(gM)], [k(FC)])
            tt("dve", FC[0:4, 2, 0:T], gB[0:4, 0:T], gM[0:4, 0:T], ALU.add, [k(gB), k(gM)], [k(FC)])
            act(FC[0:4, 2, 0:T], FC[0:4, 2, 0:T], AF.Exp, [k(FC)], [k(FC)], scale=-1.0)
            for c in range(NCH):
                sl = slice(c * 128, (c + 1) * 128)
                act(FC[0:4, 1, sl], gM[0:4, sl], AF.Exp, [k(gM), k(Mend)], [k(FC)], scale=-1.0, bias=Mend[:, c:c + 1])
                act(FC[0:4, 3, sl], gU[0:4, sl], AF.Exp, [k(gU), k(nMend)], [k(FC)], bias=nMend[:, c + 1:c + 2])
            cp("dve", uP[:, :], gU[0:4, 0:T], [k(gU)], [k(uP)])

        def gates_p2(l):
            pT_, kT_ = nps()
            for c in range(NCH):
                for qi in range(4):
                    o0 = (c * 4 + qi) * 4
                    tr(pT_[:, o0:o0 + 4], FC[0:4, qi, c * 128:(c + 1) * 128], cst[0:4, CS_IDENT:CS_IDENT + 4],
                       [k(FC), k(cst)], [kT_])
            cp("dve", tok[:, :, :, :].rearrange("p c q h -> p (c q h)"), pT_[:, 0:NCH * 16], [kT_], [k(tok)])
            pD, kD = nps()
            for h in range(4):
                mm(pD[:, h * NCH:(h + 1) * NCH], [(sel4[:, h, :], DEC[:, :])], [k(sel4), k(DEC)], [kD])
            cp("dve", decb[:, :, :].rearrange("p h c -> p (h c)"), pD[:, 0:4 * NCH], [kD], [k(decb)])
        def qk_block(l, wv, wkey, dstT, gj0):
            for j in range(4):
                gj = gj0 + j
                zb = zq[j % 2]
                ca = cacc[j % 2]
                p, pk = nps()
                mm(p[:, 0:T], z_fm(wv, wkey, j, p), [wkey] + hT_keys(), [pk])
                cp("dve", zb[:, 0:3], histQ[l][:, gj, :], [k(histQ[l], gj)], [k(zb)])
                cp("act", zb[:, 3:3 + T], p[:, 0:T], [pk], [k(zb)])
                cp("dve", histQ[l][:, gj, :], zb[:, T:T + 3], [k(zb)], [k(histQ[l], gj)])
                w0 = PP_QKW + gj * 4
                e = "dve"
                ts(e, ca[:, :], zb[:, 0:T], pp[l][:, w0:w0 + 1], pp[l][:, PP_QKB + gj:PP_QKB + gj + 1], ALU.mult, ALU.add,
                   [k(zb), k(pp[l])], [k(ca)])
                for t_ in range(1, 4):
                    stt(e, ca[:, :], zb[:, t_:t_ + T], pp[l][:, w0 + t_:w0 + t_ + 1], ca[:, :], ALU.mult, ALU.add,
                        [k(zb), k(pp[l]), k(ca)], [k(ca)])
                act(dstT[:, j, :], ca[:, :], AF.Silu, [k(ca)], [k(dstT)])

        def mlstm_head(l, hg, hh):
            GO = FC
            h = hg * 2 + hh
            i2 = hh
            s_ = sm[i2]
            for c in range(NCH):
                sl = slice(c * 128, (c + 1) * 128)
                pS, kS = nps()
                mm(pS[:, 0:128], [(qT[:, hh * 2 + dc, sl], kT[:, hh * 2 + dc, sl]) for dc in range(2)],
                   [k(qT), k(kT)], [kS])
                mm(pS[:, 128:256], [(sel4[:, h, :], uP[:, sl])], [k(sel4), k(uP)], [kS])
                ts("dve", ppb[i2][:, :], pS[:, 128:256], tok[:, c, 0, h:h + 1], 0.0, ALU.add, ALU.min,
                   [kS, k(tok)], [k(ppb[i2])])
                yield
                act(ppb[i2][:, :], ppb[i2][:, :], AF.Exp, [k(ppb[i2])], [k(ppb[i2])])
                yield
                tt("pool", ppb[i2][:, :], ppb[i2][:, :], cst[:, CS_MASK:CS_MASK + 128], ALU.mult,
                   [k(ppb[i2]), k(cst)], [k(ppb[i2])])
                yield
                stt("dve", pbf[i2][:, :], pS[:, 0:128], 0.0625, ppb[i2][:, :], ALU.mult, ALU.mult,
                    [kS, k(ppb[i2])], [k(pbf[i2])])
                yield
                pb1, kb1 = npb()
                tr(pb1[:, 0:128], pbf[i2][:, :], identb[:, :], [k(pbf[i2]), k(identb)], [kb1])
                pB, kB = nps()
                mm(pB[:, 0:257], [(qT[:, hh * 2 + dc, sl], Cbf[l][h][:, dc, :]) for dc in range(2)],
                   [k(qT), k(Cbf[l][h])], [kB])
                yield
                cp("act", pTs[i2][:, :], pb1[:, 0:128], [kb1], [k(pTs[i2])])
                act(tB[i2][:, :], pB[:, 0:257], AF.Copy, [kB, k(tok)], [k(tB[i2])], scale=tok[:, c, 1, h:h + 1])
                yield
                pA, kA = nps()
                mm(pA[:, 0:257], [(pTs[i2][:, :], vaug[:, c, hh, :])], [k(pTs[i2]), k(vaug)], [kA])
                pb3, kb3 = npb()
                for dc in range(2):
                    tr(pb3[:, dc * 128:(dc + 1) * 128], kT[:, hh * 2 + dc, sl], identb[:, :], [k(kT), k(identb)], [kb3])
                yield
                tt("dve", tot[i2][:, :], tB[i2][:, :], pA[:, 0:257], ALU.add, [k(tB[i2]), kA], [k(tot[i2])])
                act(kw[i2][:, :], pb3[:, 0:256], AF.Copy, [kb3, k(tok)], [k(kw[i2])], scale=tok[:, c, 3, h:h + 1])
                yield
                act(hb[i2][:, :], tot[i2][:, 0:256], AF.Square, [k(tot[i2])], [k(hb[i2]), k(s_, "ss")],
                    accum_out=s_[:, 2:3])
                act(s_[:, 0:1], tot[i2][:, 256:257], AF.Abs, [k(tot[i2])], [k(s_)])
                for dc in range(2):
                    pC, kC = nps()
                    mm(pC[:, 0:257], [(kw[i2][:, dc * 128:(dc + 1) * 128], vaug[:, c, hh, :])],
                       [k(kw[i2]), k(vaug)], [kC])
                    stt("dve", Cst[l][h][:, dc, :], Cst[l][h][:, dc, :], decb[:, h, c:c + 1], pC[:, 0:257],
                        ALU.mult, ALU.add, [k(Cst[l][h]), k(decb), kC], [k(Cst[l][h])])
                yield
                tt("dve", s_[:, 0:1], s_[:, 0:1], tok[:, c, 2, h:h + 1], ALU.max, [k(s_), k(tok)], [k(s_)])
                cp("pool", Cbf[l][h][:, :, :], Cst[l][h][:, :, :], [k(Cst[l][h])], [k(Cbf[l][h])])
                yield
                stt("dve", s_[:, 1:2], s_[:, 0:1], EPS, s_[:, 0:1], ALU.mult, ALU.mult, [k(s_)], [k(s_, "e")])
                yield
                act(s_[:, 3:4], s_[:, 2:3], AF.Ln, [k(s_, "ss"), k(s_, "e")], [k(s_, "r")], scale=1.0 / 256, bias=s_[:, 1:2])
                yield
                act(s_[:, 4:5], s_[:, 3:4], AF.Exp, [k(s_, "r")], [k(s_, "f")], scale=-0.5)
                yield
                stt("dve", hb[i2][:, :], tot[i2][:, 0:256], s_[:, 4:5], GO[:, c, hh * 256:(hh + 1) * 256],
                    ALU.mult, ALU.mult, [k(tot[i2]), k(s_, "f"), k(FC)], [k(hb[i2])])
                yield
                pb2, kb2 = npb()
                for ec in range(2):
                    tr(pb2[:, ec * 128:(ec + 1) * 128], hb[i2][:, ec * 128:(ec + 1) * 128], identb[:, :],
                       [k(hb[i2]), k(identb)], [kb2])
                yield
                cp("act", actG[:, hh * 2:hh * 2 + 2, sl], pb2[:, 0:256].rearrange("p (e t) -> p e t", e=2),
                   [kb2], [k(actG)])
                yield

        def conv_chunk(l, j, e):
            acc = FA[:, j, 0:T]
            a_in = FB
            w0 = PP_CONVW + j * 31
            ts(e, acc, a_in[:, j, 0:T], pp[l][:, w0:w0 + 1], pp[l][:, PP_CONVB + j:PP_CONVB + j + 1], ALU.mult, ALU.add,
               [k(FB, j), k(pp[l])], [k(FA, j)])
            yield
            for t_ in range(1, 31):
                stt(e, acc, a_in[:, j, t_:t_ + T], pp[l][:, w0 + t_:w0 + t_ + 1], acc, ALU.mult, ALU.add,
                    [k(FB, j), k(pp[l]), k(FA, j)], [k(FA, j)])
                yield

        def interleave(gens):
            gens = list(gens)
            while gens:
                for g in list(gens):
                    try:
                        next(g)
                    except StopIteration:
                        gens.remove(g)

        def group_B(l, hg, pre_out=None):
            for hh in range(2):
                h = hg * 2 + hh
                cp("act", Cbf[l][h][:, :, :], Cst[l][h][:, :, :], [k(Cst[l][h])], [k(Cbf[l][h])])
            def zphase():
                wv, wk_ = w_next()
                qk_block(l, wv, wk_, qT, hg * 4)
                w_issue()
                wv, wk_ = w_next()
                qk_block(l, wv, wk_, kT, 8 + hg * 4)
                w_issue()
                wv, wk_ = w_next()
                for c in range(NCH):
                    p, pk = nps()
                    mm(p[:, :], [(hT[:, kk, c * 128:(c + 1) * 128], wv[:, kk, :]) for kk in range(KC)], [wk_, k(hT, c)], [pk])
                    cp("act", vaug[:, c, :, 0:256], p[:, :].rearrange("p (h e) -> p h e", h=2), [pk], [k(vaug)])
                w_issue()
                wv, wk_ = w_next()
                for c in range(NCH):
                    p, pk = nps()
                    mm(p[:, :], [(hT[:, kk, c * 128:(c + 1) * 128], wv[:, kk, :]) for kk in range(KC)], [wk_, k(hT, c)], [pk])
                    act(FC[:, c, 0:512], p[:, :], AF.Sigmoid, [pk], [k(FC)])
                    tt("pool", FC[:, c, 0:512], FC[:, c, 0:512], pbt[:, PB_MNG + hg * 512:PB_MNG + (hg + 1) * 512], ALU.mult,
                       [k(FC), k(pbt)], [k(FC)])
                w_issue()

            def drain(g):
                for _ in g:
                    pass
            COOP.run([zphase, lambda: drain(conv_chunk(l, 2 * hg, "dve"))], weights=[1, 3], pools=[None, None])
            fns = [lambda: drain(mlstm_head(l, hg, 0)), lambda: drain(mlstm_head(l, hg, 1)),
                   lambda: drain(conv_chunk(l, 2 * hg + 1, "dve")),
                   (lambda: C_main(l)) if hg == 0 else (lambda: C_out(l))]
            pools = [{"ps": [0, 1], "pb": [0], "psi": 0, "pbi": 0}, {"ps": [2, 3], "pb": [1], "psi": 0, "pbi": 0},
                     None, {"ps": [4, 5], "pb": [0], "psi": 0, "pbi": 0}]
            COOP.run(fns, weights=[1, 1, 4, 1], pools=pools)
            st_ = pre_out() if pre_out is not None else None
            wv, wk_ = w_next()
            accum_into_x(actG, k(actG), wv, wk_)
            w_issue()
            return st_

        def group_A_z(l):
            sg = FA
            a_in = FB
            wv, wk_ = w_next()
            for j in range(4):
                p, pk = nps()
                mm(p[:, 0:T], z_fm(wv, wk_, j, p), [wk_] + hT_keys(), [pk])
                act(sg[:, j, 0:T], p[:, 0:T], AF.Sigmoid, [pk], [k(FA, j)])
            w_issue()
            wv, wk_ = w_next()
            for j in range(4):
                p, pk = nps()
                mm(p[:, 0:T], z_fm(wv, wk_, j, p), [wk_] + hT_keys(), [pk])
                cp("pool", a_in[:, j, 0:30], histA[l][:, j, :], [k(histA[l], j)], [k(FB, j)])
                tt("dve", a_in[:, j, 30:30 + T], p[:, 0:T], sg[:, j, 0:T], ALU.mult, [pk, k(FA, j)], [k(FB, j)])
                cp("pool", histA[l][:, j, :], a_in[:, j, T:T + 30], [k(FB, j)], [k(histA[l], j)])
            w_issue()

        def A_fin_p1(l):
            sg = FA
            a_in = FB
            for j in range(4):
                act(a_in[:, j, 0:T], sg[:, j, 0:T], AF.Square, [k(FA, j)], [k(FB, j)])
            p1, k1 = nps()
            p2, k2 = nps()
            ones = cst[:, CS_ONES:CS_ONES + 128]
            mm(p1[:, 0:T], [(ones, sg[:, j, 0:T]) for j in range(4)], [k(cst)] + [k(FA, j) for j in range(4)], [k1])
            mm(p2[:, 0:T], [(ones, a_in[:, j, 0:T]) for j in range(4)], [k(cst)] + [k(FB, j) for j in range(4)], [k2])
            mean = FC[:, 0, 0:T]
            var = FC[:, 1, 0:T]
            tmp = FC[:, 2, 0:T]
            ts("dve", mean, p1[:, 0:T], 1.0 / 512, None, ALU.mult, None, [k1], [k(FC)])
            tt("dve", tmp, mean, mean, ALU.mult, [k(FC)], [k(FC)])
            stt("dve", var, p2[:, 0:T], 1.0 / 512, tmp, ALU.mult, ALU.subtract, [k2, k(FC)], [k(FC)])
            rsqrt(var, var, 1.0, EPS, [k(FC)], [k(FC)])
            return None

        def A_fin_p2(l, st_):
            sg = FA
            actA = gT[1]
            mean = FC[:, 0, 0:T]
            var = FC[:, 1, 0:T]
            for j in range(4):
                e = "dve" if j % 2 == 0 else "pool"
                tt(e, sg[:, j, 0:T], sg[:, j, 0:T], mean, ALU.subtract, [k(FA, j), k(FC)], [k(FA, j)])
                tt(e, sg[:, j, 0:T], sg[:, j, 0:T], var, ALU.mult, [k(FA, j), k(FC)], [k(FA, j)])
                act(actA[:, j, :], sg[:, j, 0:T], AF.Silu, [k(FA, j), k(pp[l])], [k(gT[1])],
                    scale=pp[l][:, PP_CNG + j:PP_CNG + j + 1], bias=pp[l][:, PP_CNB + j:PP_CNB + j + 1])
            wv, wk_ = w_next()
            accum_into_x(actA, k(gT[1]), wv, wk_, hook=norm_hook(l, PP_LNMLP))
            w_issue()

        def C_main(l):
            guT = gT[0]
            gvn = gT[1]
            gng = zq[0][:, 0:512]
            gnb = zq[1][:, 0:512]
            gmb = cacc[0][:, 0:512]
            g_ = cacc[1][:, 0:512]
            jk = xs[0]
            S_.dma("sp", "pbc0", gng, pb_d[l:l + 1, PB_GNG:PB_GNG + 512].partition_broadcast(128), reads=(), writes=[k(zq[0])])
            S_.dma("sp", "pbc1", gnb, pb_d[l:l + 1, PB_GNB:PB_GNB + 512].partition_broadcast(128), reads=(), writes=[k(zq[1])])
            wv, wk_ = w_next()
            for c in range(NCH):
                p, pk = nps()
                mm(p[:, :], [(hT[:, kk, c * 128:(c + 1) * 128], wv[:, kk, :]) for kk in range(KC)], [wk_, k(hT, c)], [pk])
                act(g_, p[:, :], AF.Gelu, [pk], [k(cacc[1]), k(cs1, c, 0)], accum_out=cs1[:, 4 * c:4 * c + 1])
                act(jk[:, 0:512], g_, AF.Square, [k(cacc[1])], [k(jk), k(cs1, c, 1)], accum_out=cs1[:, 4 * c + 1:4 * c + 2])
                m_ = cs1[:, 4 * c:4 * c + 1]
                q1 = cs1[:, 4 * c + 1:4 * c + 2]
                v_ = cs1[:, 4 * c + 3:4 * c + 4]
                kc_ = [k(cs1, c, i) for i in range(2)]
                ts("dve", m_, m_, 1.0 / 512, None, ALU.mult, None, kc_, [k(cs1, c, 0)])
                tt("dve", v_, m_, m_, ALU.mult, kc_, [k(cs1, c, 3)])
                stt("dve", v_, q1, 1.0 / 512, v_, ALU.mult, ALU.subtract, kc_ + [k(cs1, c, 3)], [k(cs1, c, 3)])
                rsqrt(v_, v_, 1.0, EPS, [k(cs1, c, 3)], [k(cs1, c, 3)])
                ts("dve", g_, g_, m_, v_, ALU.subtract, ALU.mult, [k(cacc[1]), k(cs1, c, 0), k(cs1, c, 3)], [k(cacc[1])])
                tt("pool", g_, g_, gng, ALU.mult, [k(cacc[1]), k(zq[0])], [k(cacc[1])])
                tt("dve", gvn[:, c, :], g_, gnb, ALU.add, [k(cacc[1]), k(zq[1])], [k(gT[1])])
            w_issue()

        def C_main_b(l):
            guT = gT[0]
            gvn = gT[1]
            gmb = cacc[0][:, 0:512]
            g_ = cacc[1][:, 0:512]
            S_.dma("sp", "pbc2", gmb, pb_d[l:l + 1, PB_GMB:PB_GMB + 512].partition_broadcast(128), reads=(), writes=[k(cacc[0])])
            wv, wk_ = w_next()
            for j in range(4):
                p, pk = nps()
                mm(p[:, 0:T], z_fm(wv, wk_, j, p), [wk_] + hT_keys(), [pk])
                act(guT[:, j, 0:T], p[:, 0:T], AF.Gelu, [pk], [k(gT[0])])
            w_issue()
            for c in range(NCH):
                sl = slice(c * 128, (c + 1) * 128)
                p, pk = nps()

                def emit4(p=p, c=c):
                    ins = None
                    for g in range(4):
                        ins = nc.tensor.matmul(p[:, g * 128:(g + 1) * 128], gvn[:, c, g * 128:(g + 1) * 128], WcT[l][:, g, :],
                                               start=True, stop=True)
                    return ins
                S_.op("pe", emit4, [k(gT[1]), k(WcT[l])], [pk])
                tt("dve", g_, p[:, :], gmb, ALU.add, [pk, k(cacc[0])], [k(cacc[1])])
                tt("dve", guT[:, :, sl], g_.rearrange("p (g t) -> p g t", g=4), guT[:, :, sl], ALU.mult,
                   [k(cacc[1]), k(gT[0])], [k(gT[0])])

        def C_out(l):
            C_main_b(l)
            wv, wk_ = w_next()
            accum_into_x(gT[0], k(gT[0]), wv, wk_)
            w_issue()

        def ffn(l, nxt=None, tt_=0):
            NB_ = DFF // 512
            if nxt is not None:
                wg_prefetch(nxt)
            if l == depth - 1:
                fg_prefetch()
            for jf in range(NB_):
                g_ = gT[jf % 2]
                wv, wk_ = w_next()
                for j in range(4):
                    r_ = rt[j % 2]
                    p, pk = nps()
                    mm(p[:, 0:T], z_fm(wv, wk_, j, p), [wk_] + hT_keys(), [pk])
                    act(r_[:, :], p[:, 0:T], AF.Relu, [pk], [k(r_)])
                    tt("pool", g_[:, j, :], r_[:, :], r_[:, :], ALU.mult, [k(r_)], [k(g_)])
                w_issue()
                wv, wk_ = w_next()
                hk = None
                if jf == NB_ - 1:
                    if l + 1 < depth:
                        hk = norm_hook(l + 1, PP_LNMIX)
                    elif tt_ + 1 < NT:
                        def hk(c, tt_=tt_):
                            final_chunk(tt_, c)
                            r0 = (tt_ + 1) * T + c * 128
                            S_.dma("sp", f"xin{c}", x[:, c, :], x_d[r0:r0 + 128, :], reads=(), writes=[k(x, c)])
                            if c >= 1:
                                norm_act(0, c - 1)
                                norm_pe(0, PP_LNMIX, c - 1)
                            if c == NCH - 1:
                                norm_act(0, c)
                                norm_pe(0, PP_LNMIX, c)
                    else:
                        hk = (lambda c: final_chunk(tt_, c))
                accum_into_x(g_, k(g_), wv, wk_, hook=hk)
                w_issue()

        fgt = FA[:, :, :].rearrange("p a b -> p (a b)")[:, 0:D]

        def fg_prefetch():
            S_.dma("sp", "fg", fgt, fg_d[0:1, :].partition_broadcast(128), reads=(), writes=[k(FA, j) for j in range(4)])

        def final_chunk(tt_, c):
            o_t = FB if c % 2 == 0 else FC
            o_ = o_t[:, :, :].rearrange("p a b -> p (a b)")[:, 0:D]
            okeys = [k(FB, j) for j in range(4)] if c % 2 == 0 else [k(FC)]
            act(xs0[:, :], x[:, c, :], AF.Square, [k(x, c)], [k(xs0), k(ss, c)], accum_out=ss[:, c:c + 1])
            rsqrt(rstd[:, c:c + 1], ss[:, c:c + 1], 1.0 / D, EPS, [k(ss, c)], [k(rstd, c)])
            stt("dve", o_, x[:, c, :], rstd[:, c:c + 1], fgt, ALU.mult, ALU.mult,
                [k(x, c), k(rstd, c)] + [k(FA, j) for j in range(4)], okeys)
            r0 = tt_ * T + c * 128
            S_.dma("sp", f"st{c % 2}", y_d[r0:r0 + 128, :], o_, reads=okeys, writes=())

        wg_prefetch(0)
        for tt_ in range(NT):
            if tt_ == 0:
                for c in range(NCH):
                    r0 = tt_ * T + c * 128
                    S_.dma("sp", f"xin{c}", x[:, c, :], x_d[r0:r0 + 128, :], reads=(), writes=[k(x, c)])
            for l in range(depth):
                S_.dma("sp", "pbt", pbt[:, :], pb_d[l:l + 1, 0:1024].partition_broadcast(128), reads=(), writes=[k(pbt)])
                if l == 0 and tt_ == 0:
                    norm_to_hT(l, PP_LNMIX)
                gates_p1(l)
                group_A_z(l)
                gates_p2(l)
                group_B(l, 0)
                st_ = group_B(l, 1, pre_out=lambda: A_fin_p1(l))
                A_fin_p2(l, st_)
                nxt = l + 1 if l + 1 < depth else (0 if tt_ + 1 < NT else None)
                ffn(l, nxt, tt_)
        S_.wait_all("sp", ["st0", "st1"])
        for e in ["act", "dve", "pool", "pe"]:
            S_.wait_all(e, ["st0", "st1"])
        assert wpos["used"] == len(wstream), (wpos, len(wstream))
    return nc


def _host_layout(inputs, depth=DEPTH):
    f = lambda a: np.ascontiguousarray(np.asarray(a, dtype=np.float32))
    pp = np.zeros((depth, 128, NPP), np.float32)
    pb = np.zeros((depth, NPB), np.float32)
    for l in range(depth):
        pp[l, :, PP_LNMIX:PP_LNMIX + 16] = f(inputs["ln_mix_g"])[l].reshape(16, 128).T
        pp[l, :, PP_LNMLP:PP_LNMLP + 16] = f(inputs["ln_mlp_g"])[l].reshape(16, 128).T
        cw = f(inputs["conv_w"])[l]
        pp[l, :, PP_CONVW:PP_CONVW + 124] = cw.T.reshape(4, 128, 31).transpose(1, 0, 2).reshape(128, 124)
        pp[l, :, PP_CONVB:PP_CONVB + 4] = f(inputs["conv_b"])[l].reshape(4, 128).T
        pp[l, :, PP_CNG:PP_CNG + 4] = f(inputs["conv_norm_g"])[l].reshape(4, 128).T
        pp[l, :, PP_CNB:PP_CNB + 4] = f(inputs["conv_norm_b"])[l].reshape(4, 128).T
        qw = f(inputs["qk_conv_w"])[l]
        pp[l, :, PP_QKW:PP_QKW + 64] = qw.T.reshape(16, 128, 4).transpose(1, 0, 2).reshape(128, 64)
        pp[l, :, PP_QKB:PP_QKB + 16] = f(inputs["qk_conv_b"])[l].reshape(16, 128).T
        pp[l, 0:4, PP_IGB] = f(inputs["igate_b"])[l]
        pp[l, 0:4, PP_FGB] = f(inputs["fgate_b"])[l]
        pb[l, PB_MNG:PB_MNG + 1024] = f(inputs["mlstm_norm_g"])[l]
        pb[l, PB_GNG:PB_GNG + 512] = f(inputs["gm_norm_g"])[l]
        pb[l, PB_GNB:PB_GNB + 512] = f(inputs["gm_norm_b"])[l]
        pb[l, PB_GMB:PB_GMB + 512] = f(inputs["gm_b"])[l].reshape(512)
    gmwT = np.ascontiguousarray(f(inputs["gm_w"])[:depth].transpose(0, 3, 1, 2))
    cst = np.zeros((128, NCST), np.float32)
    cst[:, CS_IDENT:CS_IDENT + 128] = np.eye(128, dtype=np.float32)
    cst[:, CS_MASK:CS_MASK + 128] = np.tril(np.ones((128, 128), np.float32))
    cst[:, CS_MASKT:CS_MASKT + 128] = np.triu(np.ones((128, 128), np.float32))
    cst[:, CS_ONES:CS_ONES + 128] = 1.0
    return {
        "w_in": f(inputs["w_in"])[:depth], "w_out": f(inputs["w_out"])[:depth],
        "w_up": f(inputs["w_up"])[:depth], "w_down": f(inputs["w_down"])[:depth],
        "pp": pp, "pb": pb, "final_g": f(inputs["final_g"]).reshape(1, D), "gm_wT": gmwT, "cst": cst,
    }


_NC_CACHE = {}


def kernel(**inputs):
    x = np.asarray(inputs["x"], dtype=np.float32)
    B, S, _ = x.shape
    T = 512
    key = (S, T)
    if key not in _NC_CACHE:
        _NC_CACHE[key] = build_nc(S, T)
    nc = _NC_CACHE[key]
    shared = _host_layout(inputs)
    in_maps = []
    for b in range(B):
        m = dict(shared)
        m["x"] = np.ascontiguousarray(x[b])
        in_maps.append(m)
    res = run_bass_kernel_spmd(nc, in_maps, core_ids=list(range(B)))
    return np.stack([np.asarray(r["y"], dtype=np.float32) for r in res.results], axis=0)
```

```python
import numpy as np
import concourse.bass as bass
import concourse.mybir as mybir
from concourse.bass_utils import run_bass_kernel_spmd

F32 = mybir.dt.float32
BF16 = mybir.dt.bfloat16
AF = mybir.ActivationFunctionType
ALU = mybir.AluOpType

D = 2048
NIN = 6152
DFF = 8192
DEPTH = 2
EPS = 1e-6
KC = 16
LN16 = float(np.log(16.0))

C_CV, C_CG, C_Q, C_K, C_V, C_O, C_G, C_GU, C_GV = 0, 512, 1024, 2048, 3072, 4096, 5120, 5128, 5640

PP_LNMIX = 0
PP_LNMLP = 16
PP_CONVW = 32
PP_CONVB = 156
PP_CNG = 160
PP_CNB = 164
PP_QKW = 168
PP_QKB = 232
PP_IGB = 248
PP_FGB = 249
NPP = 256
PB_MNG = 0
PB_GNG = 1024
PB_GNB = 1536
PB_GMB = 2048
NPB = 2560
CS_IDENT = 0
CS_MASK = 128
CS_MASKT = 256
CS_ONES = 384
NCST = 512


import threading

_tls = threading.local()


class Coop:
    def __init__(self):
        self.cv = threading.Condition()
        self.cur = -1
        self.done = []
        self.active = False

    def _next(self, i):
        n = len(self.done)
        for d in range(1, n + 1):
            j = (i + d) % n
            if not self.done[j]:
                return j
        return -1

    def switch(self):
        if not self.active:
            return
        i = getattr(_tls, "task", None)
        if i is None:
            return
        w = getattr(_tls, "weight", 1)
        with self.cv:
            for _ in range(w):
                j = self._next(i)
                if j == i or j < 0:
                    return
                self.cur = j
                self.cv.notify_all()
                while self.cur != i:
                    self.cv.wait()

    def run(self, fns, weights=None, pools=None):
        n = len(fns)
        self.done = [False] * n
        errs = []

        def worker(i):
            _tls.task = i
            _tls.weight = (weights or [1] * n)[i]
            _tls.pools = (pools or [None] * n)[i]
            with self.cv:
                while self.cur != i:
                    self.cv.wait()
            try:
                fns[i]()
            except BaseException as e:
                errs.append(e)
            finally:
                with self.cv:
                    self.done[i] = True
                    self.cur = self._next(i)
                    self.cv.notify_all()

        self.active = True
        self.cur = 0
        ths = [threading.Thread(target=worker, args=(i,)) for i in range(n)]
        for t in ths:
            t.start()
        for t in ths:
            t.join()
        self.active = False
        self.cur = -1
        if errs:
            raise errs[0]


COOP = Coop()


class Sched:
    def __init__(self, nc, sems):
        self.nc = nc
        self.eng = {"pe": nc.tensor, "act": nc.scalar, "dve": nc.vector, "pool": nc.gpsimd, "sp": nc.sync}
        self.sem = dict(sems)
        self.scale = {k: 1 for k in sems}
        self.cnt = {k: 0 for k in sems}
        self.seen = {e: {} for e in self.eng}
        self.res = {}

    def add_dma_sem(self, name, sem):
        self.sem[name] = sem
        self.scale[name] = 16
        self.cnt[name] = 0

    def _deps(self, e, reads, writes):
        deps = {}

        def add(p):
            f, idx = p
            if f == e and e == "pe":
                return
            if idx > deps.get(f, 0):
                deps[f] = idx

        for k in reads:
            r = self.res.get(k)
            if r and r[0]:
                add(r[0])
        for k in writes:
            r = self.res.get(k)
            if r:
                if r[0]:
                    add(r[0])
                for p in r[1].items():
                    add(p)
        for f, idx in deps.items():
            if idx > self.seen[e].get(f, 0):
                self.eng[e].wait_ge(self.sem[f], idx * self.scale[f])
                self.seen[e][f] = idx

    def _mark(self, who, idx, reads, writes):
        for k in reads:
            r = self.res.get(k)
            if r is None:
                r = [None, {}]
                self.res[k] = r
            r[1][who] = idx
        for k in writes:
            self.res[k] = [(who, idx), {}]

    def op(self, e, emit, reads=(), writes=()):
        COOP.switch()
        self._deps(e, reads, writes)
        ins = emit()
        self.cnt[e] += 1
        idx = self.cnt[e]
        ins.then_inc(self.sem[e], 1)
        self._mark(e, idx, reads, writes)

    def dma(self, q, dsem, out, in_, reads=(), writes=()):
        COOP.switch()
        self._deps(q, reads, writes)
        ins = self.eng[q].dma_start(out=out, in_=in_)
        self.cnt[dsem] += 1
        ins.then_inc(self.sem[dsem], 16)
        self._mark(dsem, self.cnt[dsem], reads, writes)

    def wait_all(self, e, names):
        for f in names:
            idx = self.cnt[f]
            if idx > self.seen[e].get(f, 0):
                self.eng[e].wait_ge(self.sem[f], idx * self.scale[f])
                self.seen[e][f] = idx


def build_nc(S, T, depth=DEPTH):
    NT = S // T
    NCH = T // 128
    nc = bass.Bass("TRN2", target_bir_lowering=False)
    x_d = nc.dram_tensor("x", [S, D], F32, kind="ExternalInput").ap()
    win_d = nc.dram_tensor("w_in", [depth, D, NIN], F32, kind="ExternalInput").ap()
    wout_d = nc.dram_tensor("w_out", [depth, D, D], F32, kind="ExternalInput").ap()
    wup_d = nc.dram_tensor("w_up", [depth, D, DFF], F32, kind="ExternalInput").ap()
    wdn_d = nc.dram_tensor("w_down", [depth, DFF, D], F32, kind="ExternalInput").ap()
    pp_d = nc.dram_tensor("pp", [depth, 128, NPP], F32, kind="ExternalInput").ap()
    pb_d = nc.dram_tensor("pb", [depth, NPB], F32, kind="ExternalInput").ap()
    fg_d = nc.dram_tensor("final_g", [1, D], F32, kind="ExternalInput").ap()
    gmw_d = nc.dram_tensor("gm_wT", [depth, 128, 4, 128], F32, kind="ExternalInput").ap()
    cst_d = nc.dram_tensor("cst", [128, NCST], F32, kind="ExternalInput").ap()
    y_d = nc.dram_tensor("y", [S, D], F32, kind="ExternalOutput").ap()

    import contextlib
    es = contextlib.ExitStack()
    with es:
        def sb(name, shape, dt):
            return es.enter_context(nc.sbuf_tensor("s_" + name, shape, dt))

        def psum(name, shape, dt):
            return es.enter_context(nc.psum_tensor("p_" + name, shape, dt))

        def semaphore(name):
            return es.enter_context(nc.semaphore("m_" + name))

        S_ = Sched(nc, {e: semaphore("s_" + e) for e in ["pe", "act", "dve", "pool", "sp"]})
        NW = 3
        for i in range(NW):
            S_.add_dma_sem(f"w{i}", semaphore(f"w{i}"))
        for nm in ["misc", "xin0", "xin1", "xin2", "xin3", "st0", "st1", "fg", "pbt", "pbc0", "pbc1", "pbc2", "wg"]:
            S_.add_dma_sem(nm, semaphore(nm))

        x = sb("x", [128, NCH, D], F32)
        hT = sb("hT", [128, KC, T], BF16)
        wbuf = [sb(f"wb{i}", [128, 8192], BF16) for i in range(NW)]
        wg = sb("wg", [128, KC, 8], BF16)
        cst = sb("cst", [128, NCST], F32)
        identb = sb("identb", [128, 128], BF16)
        pp = [sb(f"pp{l}", [128, NPP], F32) for l in range(depth)]
        pbt = sb("pbt", [128, 1024], F32)
        WcT = [sb(f"WcT{l}", [128, 4, 128], BF16) for l in range(depth)]
        nfb = [sb(f"nfb{l}", [4, 1], F32) for l in range(depth)]
        FA = sb("FA", [128, 4, 544], F32)
        FB = sb("FB", [128, 4, 544], F32)
        FC = sb("FC", [128, 4, 512], F32)
        actG = sb("actG", [128, 4, T], BF16)
        qT = sb("qT", [128, 4, T], BF16)
        kT = sb("kT", [128, 4, T], BF16)
        gT = [sb(f"gT{i}", [128, 4, T], BF16) for i in range(2)]
        vaug = sb("vaug", [128, NCH, 2, 257], BF16)
        xs0 = sb("xs0", [128, D], BF16)
        xs = [xs0, xs0]
        ss = sb("ss", [128, 16], F32)
        rstd = sb("rstd", [128, 16], F32)
        Mend = sb("Mend", [4, NCH + 1], F32)
        nMend = sb("nMend", [4, NCH + 1], F32)
        DEC = sb("DEC", [4, NCH], F32)
        sel4 = sb("sel4", [4, 4, 128], F32)
        Bcar = [sb(f"Bcar{l}", [4, 1], F32) for l in range(depth)]
        Mcar = [sb(f"Mcar{l}", [4, 1], F32) for l in range(depth)]
        uP = sb("uP", [4, T], F32)
        tok = sb("tok", [128, NCH, 4, 4], F32)
        decb = sb("decb", [128, 4, NCH], F32)
        Cst = [[sb(f"Cst{l}_{h}", [128, 2, 257], F32) for h in range(4)] for l in range(depth)]
        Cbf1 = [sb(f"Cbf_{h}", [128, 2, 257], BF16) for h in range(4)]
        Cbf = [Cbf1 for l in range(depth)]
        histA = [sb(f"histA{l}", [128, 4, 30], F32) for l in range(depth)]
        histQ = [sb(f"histQ{l}", [128, 16, 3], F32) for l in range(depth)]
        zq = [sb(f"zq{i}", [128, T + 3], F32) for i in range(2)]
        cacc = [sb(f"cacc{i}", [128, T], F32) for i in range(2)]
        ppb = [sb(f"ppb{i}", [128, 128], F32) for i in range(2)]
        pbf = [sb(f"pbf{i}", [128, 128], BF16) for i in range(2)]
        pTs = [sb(f"pTs{i}", [128, 128], BF16) for i in range(2)]
        tB = [sb(f"tB{i}", [128, 257], F32) for i in range(2)]
        tot = [sb(f"tot{i}", [128, 257], F32) for i in range(2)]
        sm = [sb(f"sm{i}", [128, 8], F32) for i in range(2)]
        hb = [sb(f"hb{i}", [128, 256], BF16) for i in range(2)]
        kw = [sb(f"kw{i}", [128, 256], BF16) for i in range(2)]
        rt = cacc
        gt = zq
        cs1 = sb("cs1", [128, 16], F32)
        epsc = sb("epsc", [128, 1], F32)
        gU = zq[0]
        gM = zq[1]
        gL = cacc[0]
        gB = cacc[1]

        ps = [psum(f"ps{i}", [128, 512], F32) for i in range(6)]
        pb = [psum(f"pbk{i}", [128, 1024], BF16) for i in range(2)]

        state = {"ps": 0, "pb": 0, "tmp": 0}

        def nps():
            pools = getattr(_tls, "pools", None)
            if pools is not None:
                lst = pools["ps"]
                i = lst[pools["psi"] % len(lst)]
                pools["psi"] += 1
                return ps[i], ("ps", i)
            i = state["ps"]
            state["ps"] = (i + 1) % 6
            return ps[i], ("ps", i)

        def npb():
            pools = getattr(_tls, "pools", None)
            if pools is not None:
                lst = pools["pb"]
                i = lst[pools["pbi"] % len(lst)]
                pools["pbi"] += 1
                return pb[i], ("pb", i)
            i = state["pb"]
            state["pb"] = (i + 1) % 2
            return pb[i], ("pb", i)

        def k(t, *idx):
            return (t.name,) + idx

        def mm(out_ap, pairs, reads, writes):
            def emit():
                n = len(pairs)
                ins = None
                for i, (l, r) in enumerate(pairs):
                    ins = nc.tensor.matmul(out_ap, l, r, start=(i == 0), stop=(i == n - 1))
                return ins
            S_.op("pe", emit, reads, writes)

        def tr(out_ap, in_ap, ident_ap, reads, writes):
            S_.op("pe", lambda: nc.tensor.transpose(out_ap, in_ap, ident_ap), reads, writes)

        def act(out, in_, func, reads, writes, **kw_):
            S_.op("act", lambda: nc.scalar.activation(out=out, in_=in_, func=func, **kw_), reads, writes)

        def ts(e, out, in0, s1, s2, op0, op1, reads, writes):
            eng = nc.vector if e == "dve" else nc.gpsimd
            if op1 is None:
                S_.op(e, lambda: eng.tensor_scalar(out=out, in0=in0, scalar1=s1, scalar2=None, op0=op0), reads, writes)
            else:
                S_.op(e, lambda: eng.tensor_scalar(out=out, in0=in0, scalar1=s1, scalar2=s2, op0=op0, op1=op1), reads, writes)

        def tt(e, out, in0, in1, op, reads, writes):
            eng = nc.vector if e == "dve" else nc.gpsimd
            S_.op(e, lambda: eng.tensor_tensor(out=out, in0=in0, in1=in1, op=op), reads, writes)

        def stt(e, out, in0, scalar, in1, op0, op1, reads, writes):
            eng = nc.vector if e == "dve" else nc.gpsimd
            S_.op(e, lambda: eng.scalar_tensor_tensor(out=out, in0=in0, scalar=scalar, in1=in1, op0=op0, op1=op1), reads, writes)

        def rsqrt(out, in_, scale, eps, reads, writes):
            assert eps == EPS
            act(out, in_, AF.Ln, list(reads) + [k(epsc)], writes, scale=scale, bias=epsc[:, 0:1])
            act(out, out, AF.Exp, writes, writes, scale=-0.5)

        def cp(e, out, in_, reads, writes):
            if e == "act":
                S_.op("act", lambda: nc.scalar.copy(out=out, in_=in_), reads, writes)
            else:
                eng = nc.vector if e == "dve" else nc.gpsimd
                S_.op(e, lambda: eng.tensor_copy(out=out, in_=in_), reads, writes)

        def memset(e, ap, val, writes):
            eng = nc.vector if e == "dve" else nc.gpsimd
            S_.op(e, lambda: eng.memset(ap, val), (), writes)

        wstream = []
        for tt_ in range(NT):
            for l in range(depth):
                def wi(c0):
                    return ("in", win_d[l, :, c0:c0 + 512].rearrange("(k p) n -> p k n", p=128))

                def wo(r0):
                    return ("row", wout_d[l, r0:r0 + 512, :].rearrange("(k p) n -> p k n", p=128))
                seq = [wi(C_CG), wi(C_CV),
                       wi(C_Q), wi(C_K), wi(C_V), wi(C_O), wi(C_GV), wo(512),
                       wi(C_Q + 512), wi(C_K + 512), wi(C_V + 512), wi(C_O + 512), wi(C_GU), wo(1536), wo(1024),
                       wo(0)]
                for j in range(DFF // 512):
                    seq.append(("in", wup_d[l, :, j * 512:(j + 1) * 512].rearrange("(k p) n -> p k n", p=128)))
                    seq.append(("row", wdn_d[l, j * 512:(j + 1) * 512, :].rearrange("(k p) n -> p k n", p=128)))
                wstream.extend(seq)
        wpos = {"issued": 0, "used": 0}

        def w_issue():
            i = wpos["issued"]
            if i >= len(wstream):
                return
            kind, src = wstream[i]
            slot = i % NW
            if kind == "in":
                dst = wbuf[slot][:, :].rearrange("p (k n) -> p k n", k=16)
            else:
                dst = wbuf[slot][:, :].rearrange("p (k n) -> p k n", k=4)
            S_.dma("pool", f"w{slot}", dst, src, reads=(), writes=[("wb", slot)])
            wpos["issued"] = i + 1

        def w_next():
            i = wpos["used"]
            wpos["used"] = i + 1
            kind, _ = wstream[i]
            slot = i % NW
            if kind == "in":
                v = wbuf[slot][:, :].rearrange("p (k n) -> p k n", k=16)
            else:
                v = wbuf[slot][:, :].rearrange("p (k n) -> p k n", k=4)
            return v, ("wb", slot)

        S_.dma("sp", "misc", cst[:, :], cst_d[:, :], writes=[k(cst)])
        for l in range(depth):
            S_.dma("sp", "misc", pp[l][:, :], pp_d[l, :, :], writes=[k(pp[l])])
        for l in range(depth):
            S_.dma("sp", "misc", FC[:, :, l * 128:(l + 1) * 128], gmw_d[l, :, :, :], reads=(), writes=[k(FC)])
        S_.dma("pool", "wg", wg[:, :, :], win_d[0, :, C_G:C_G + 8].rearrange("(k p) n -> p k n", p=128),
               reads=(), writes=[k(wg)])
        for i in range(NW):
            w_issue()
        for e in ["pe", "act", "dve", "pool"]:
            S_.wait_all(e, ["misc"])
        cp("dve", identb[:, :], cst[:, CS_IDENT:CS_IDENT + 128], [k(cst)], [k(identb)])
        memset("dve", epsc[:, :], EPS, [k(epsc)])
        memset("dve", sel4[:, :, :], 0.0, [k(sel4)])
        for h in range(4):
            ts("dve", sel4[:, h, :], cst[0:4, CS_ONES:CS_ONES + 128], cst[0:4, CS_IDENT + h:CS_IDENT + h + 1], None, ALU.mult, None,
               [k(cst)], [k(sel4)])
        for l in range(depth):
            ts("dve", nfb[l][:, :], pp[l][0:4, PP_FGB:PP_FGB + 1], -1.0, None, ALU.mult, None, [k(pp[l])], [k(nfb[l])])
            memset("dve", Bcar[l][:, :], 0.0, [k(Bcar[l])])
            memset("dve", Mcar[l][:, :], 0.0, [k(Mcar[l])])
            memset("dve", histA[l][:, :, :], 0.0, [k(histA[l])])
            memset("dve", histQ[l][:, :, :], 0.0, [k(histQ[l])])
            for h in range(4):
                memset("dve", Cst[l][h][:, :, :], 0.0, [k(Cst[l][h])])
            for g in range(4):
                tt("dve", WcT[l][:, g, :], FC[:, g, l * 128:(l + 1) * 128], cst[:, CS_MASKT:CS_MASKT + 128], ALU.mult,
                   [k(FC), k(cst)], [k(WcT[l])])
        memset("dve", vaug[:, :, :, 256:257], 1.0, [k(vaug)])

        def norm_act(l, c):
            act(hT[:, :, c * 128:(c + 1) * 128], x[:, c, :].rearrange("p (a b) -> p a b", a=KC), AF.Square,
                [k(x, c)], [k(hT, c), k(ss, c)], accum_out=ss[:, c:c + 1])
            rsqrt(rstd[:, c:c + 1], ss[:, c:c + 1], 1.0 / D, EPS, [k(ss, c)], [k(rstd, c)])
            act(xs0[:, :], x[:, c, :], AF.Copy, [k(x, c), k(rstd, c)], [k(xs0)], scale=rstd[:, c:c + 1])

        def norm_pe(l, goff, c):
            xb = xs0
            for kq in range(4):
                pbk, pk = npb()
                for i in range(4):
                    kk = kq * 4 + i
                    tr(pbk[:, i * 128:(i + 1) * 128], xb[:, kk * 128:(kk + 1) * 128], identb[:, :],
                       [k(xb), k(identb)], [pk])
                gb = pp[l][:, goff + kq * 4:goff + kq * 4 + 4].unsqueeze(2).to_broadcast([128, 4, 128])
                tt("dve", hT[:, kq * 4:kq * 4 + 4, c * 128:(c + 1) * 128],
                   pbk[:, 0:512].rearrange("p (a b) -> p a b", a=4), gb, ALU.mult, [pk, k(pp[l])], [k(hT, c)])

        def norm_to_hT(l, goff):
            for c in range(NCH):
                norm_act(l, c)
                norm_pe(l, goff, c)

        def norm_hook(l, goff):
            def hook(c):
                if c >= 1:
                    norm_pe(l, goff, c - 1)
                norm_act(l, c)
                if c == NCH - 1:
                    norm_pe(l, goff, c)
            return hook

        def hT_keys():
            return [k(hT, c) for c in range(NCH)]

        def z_fm(wv, wk_, j, out_ps):
            pairs = [(wv[:, kk, j * 128:(j + 1) * 128], hT[:, kk, 0:T]) for kk in range(KC)]
            return pairs

        def accum_into_x(aT, akey, wv, wkey, hook=None):
            for c in range(NCH):
                for db in range(4):
                    p, pk = nps()
                    mm(p[:, :], [(aT[:, kc, c * 128:(c + 1) * 128], wv[:, kc, db * 512:(db + 1) * 512]) for kc in range(4)],
                       [akey, wkey], [pk])
                    tt("dve", x[:, c, db * 512:(db + 1) * 512], x[:, c, db * 512:(db + 1) * 512], p[:, :], ALU.add,
                       [pk, k(x, c)], [k(x, c)])
                if hook is not None:
                    hook(c)

        def wg_prefetch(l):
            S_.dma("pool", "wg", wg[:, :, :], win_d[l, :, C_G:C_G + 8].rearrange("(k p) n -> p k n", p=128),
                   reads=(), writes=[k(wg)])

        def gates_p1(l):
            pI, kI = nps()
            pF, kF = nps()
            mm(pI[0:4, 0:T], [(wg[:, kk, 0:4], hT[:, kk, 0:T]) for kk in range(KC)], [k(wg)] + hT_keys(), [kI])
            mm(pF[0:4, 0:T], [(wg[:, kk, 4:8], hT[:, kk, 0:T]) for kk in range(KC)], [k(wg)] + hT_keys(), [kF])
            act(gU[0:4, 0:T], pI[0:4, 0:T], AF.Identity, [kI, k(pp[l])], [k(gU)], bias=pp[l][0:4, PP_IGB:PP_IGB + 1])
            act(gL[0:4, 0:T], pF[0:4, 0:T], AF.Exp, [kF, k(nfb[l])], [k(gL)], scale=-1.0, bias=nfb[l][:, 0:1])
            act(gL[0:4, 0:T], gL[0:4, 0:T], AF.Ln, [k(gL)], [k(gL)], bias=1.0)
            S_.op("dve", lambda: nc.vector.tensor_tensor_scan(out=gB[0:4, 0:T], data0=cst[0:4, CS_ONES:CS_ONES + 1].to_broadcast([4, T]), data1=gL[0:4, 0:T],
                                                              initial=Bcar[l][:, 0:1], op0=ALU.mult, op1=ALU.subtract),
                  [k(cst), k(gL), k(Bcar[l])], [k(gB)])
            tt("dve", gU[0:4, 0:T], gU[0:4, 0:T], gB[0:4, 0:T], ALU.subtract, [k(gU), k(gB)], [k(gU)])
            S_.op("dve", lambda: nc.vector.tensor_tensor_scan(out=gM[0:4, 0:T], data0=cst[0:4, CS_ONES:CS_ONES + 1].to_broadcast([4, T]), data1=gU[0:4, 0:T],
                                                              initial=Mcar[l][:, 0:1], op0=ALU.mult, op1=ALU.max),
                  [k(cst), k(gU), k(Mcar[l])], [k(gM)])
            cp("dve", Mend[:, 0:1], Mcar[l][:, 0:1], [k(Mcar[l])], [k(Mend)])
            for c in range(NCH):
                cp("dve", Mend[:, c + 1:c + 2], gM[0:4, c * 128 + 127:c * 128 + 128], [k(gM)], [k(Mend)])
            cp("dve", Bcar[l][:, 0:1], gB[0:4, T - 1:T], [k(gB)], [k(Bcar[l])])
            cp("dve", Mcar[l][:, 0:1], gM[0:4, T - 1:T], [k(gM)], [k(Mcar[l])])
            ts("dve", nMend[:, :], Mend[:, :], -1.0, -LN16, ALU.mult, ALU.add, [k(Mend)], [k(nMend)])
            tt("dve", DEC[:, :], Mend[:, 0:NCH], Mend[:, 1:NCH + 1], ALU.subtract, [k(Mend)], [k(DEC)])
            act(DEC[:, :], DEC[:, :], AF.Exp, [k(DEC)], [k(DEC)])
            ts("dve", FC[0:4, 0, 0:T], gM[0:4, 0:T], -1.0, None, ALU.mult, None, [k(gM)], [k(FC)])
            tt("dve", FC[0:4, 2, 0:T], gB[0:4, 0:T], gM[0:4, 0:T], ALU.add, [k(gB), k(gM)], [k(FC)])
            act(FC[0:4, 2, 0:T], FC[0:4, 2, 0:T], AF.Exp, [k(FC)], [k(FC)], scale=-1.0)
            for c in range(NCH):
                sl = slice(c * 128, (c + 1) * 128)
                act(FC[0:4, 1, sl], gM[0:4, sl], AF.Exp, [k(gM), k(Mend)], [k(FC)], scale=-1.0, bias=Mend[:, c:c + 1])
                act(FC[0:4, 3, sl], gU[0:4, sl], AF.Exp, [k(gU), k(nMend)], [k(FC)], bias=nMend[:, c + 1:c + 2])
            cp("dve", uP[:, :], gU[0:4, 0:T], [k(gU)], [k(uP)])

        def gates_p2(l):
            pT_, kT_ = nps()
            for c in range(NCH):
                for qi in range(4):
                    o0 = (c * 4 + qi) * 4
                    tr(pT_[:, o0:o0 + 4], FC[0:4, qi, c * 128:(c + 1) * 128], cst[0:4, CS_IDENT:CS_IDENT + 4],
                       [k(FC), k(cst)], [kT_])
            cp("dve", tok[:, :, :, :].rearrange("p c q h -> p (c q h)"), pT_[:, 0:NCH * 16], [kT_], [k(tok)])
            pD, kD = nps()
            for h in range(4):
                mm(pD[:, h * NCH:(h + 1) * NCH], [(sel4[:, h, :], DEC[:, :])], [k(sel4), k(DEC)], [kD])
            cp("dve", decb[:, :, :].rearrange("p h c -> p (h c)"), pD[:, 0:4 * NCH], [kD], [k(decb)])
        def qk_block(l, wv, wkey, dstT, gj0):
            for j in range(4):
                gj = gj0 + j
                zb = zq[j % 2]
                ca = cacc[j % 2]
                p, pk = nps()
                mm(p[:, 0:T], z_fm(wv, wkey, j, p), [wkey] + hT_keys(), [pk])
                cp("dve", zb[:, 0:3], histQ[l][:, gj, :], [k(histQ[l], gj)], [k(zb)])
                cp("act", zb[:, 3:3 + T], p[:, 0:T], [pk], [k(zb)])
                cp("dve", histQ[l][:, gj, :], zb[:, T:T + 3], [k(zb)], [k(histQ[l], gj)])
                w0 = PP_QKW + gj * 4
                e = "dve"
                ts(e, ca[:, :], zb[:, 0:T], pp[l][:, w0:w0 + 1], pp[l][:, PP_QKB + gj:PP_QKB + gj + 1], ALU.mult, ALU.add,
                   [k(zb), k(pp[l])], [k(ca)])
                for t_ in range(1, 4):
                    stt(e, ca[:, :], zb[:, t_:t_ + T], pp[l][:, w0 + t_:w0 + t_ + 1], ca[:, :], ALU.mult, ALU.add,
                        [k(zb), k(pp[l]), k(ca)], [k(ca)])
                act(dstT[:, j, :], ca[:, :], AF.Silu, [k(ca)], [k(dstT)])

        def mlstm_head(l, hg, hh):
            GO = FC
            h = hg * 2 + hh
            i2 = hh
            s_ = sm[i2]
            for c in range(NCH):
                sl = slice(c * 128, (c + 1) * 128)
                pS, kS = nps()
                mm(pS[:, 0:128], [(qT[:, hh * 2 + dc, sl], kT[:, hh * 2 + dc, sl]) for dc in range(2)],
                   [k(qT), k(kT)], [kS])
                mm(pS[:, 128:256], [(sel4[:, h, :], uP[:, sl])], [k(sel4), k(uP)], [kS])
                ts("dve", ppb[i2][:, :], pS[:, 128:256], tok[:, c, 0, h:h + 1], 0.0, ALU.add, ALU.min,
                   [kS, k(tok)], [k(ppb[i2])])
                yield
                act(ppb[i2][:, :], ppb[i2][:, :], AF.Exp, [k(ppb[i2])], [k(ppb[i2])])
                yield
                tt("pool", ppb[i2][:, :], ppb[i2][:, :], cst[:, CS_MASK:CS_MASK + 128], ALU.mult,
                   [k(ppb[i2]), k(cst)], [k(ppb[i2])])
                yield
                stt("dve", pbf[i2][:, :], pS[:, 0:128], 0.0625, ppb[i2][:, :], ALU.mult, ALU.mult,
                    [kS, k(ppb[i2])], [k(pbf[i2])])
                yield
                pb1, kb1 = npb()
                tr(pb1[:, 0:128], pbf[i2][:, :], identb[:, :], [k(pbf[i2]), k(identb)], [kb1])
                pB, kB = nps()
                mm(pB[:, 0:257], [(qT[:, hh * 2 + dc, sl], Cbf[l][h][:, dc, :]) for dc in range(2)],
                   [k(qT), k(Cbf[l][h])], [kB])
                yield
                cp("act", pTs[i2][:, :], pb1[:, 0:128], [kb1], [k(pTs[i2])])
                act(tB[i2][:, :], pB[:, 0:257], AF.Copy, [kB, k(tok)], [k(tB[i2])], scale=tok[:, c, 1, h:h + 1])
                yield
                pA, kA = nps()
                mm(pA[:, 0:257], [(pTs[i2][:, :], vaug[:, c, hh, :])], [k(pTs[i2]), k(vaug)], [kA])
                pb3, kb3 = npb()
                for dc in range(2):
                    tr(pb3[:, dc * 128:(dc + 1) * 128], kT[:, hh * 2 + dc, sl], identb[:, :], [k(kT), k(identb)], [kb3])
                yield
                tt("dve", tot[i2][:, :], tB[i2][:, :], pA[:, 0:257], ALU.add, [k(tB[i2]), kA], [k(tot[i2])])
                act(kw[i2][:, :], pb3[:, 0:256], AF.Copy, [kb3, k(tok)], [k(kw[i2])], scale=tok[:, c, 3, h:h + 1])
                yield
                act(hb[i2][:, :], tot[i2][:, 0:256], AF.Square, [k(tot[i2])], [k(hb[i2]), k(s_, "ss")],
                    accum_out=s_[:, 2:3])
                act(s_[:, 0:1], tot[i2][:, 256:257], AF.Abs, [k(tot[i2])], [k(s_)])
                for dc in range(2):
                    pC, kC = nps()
                    mm(pC[:, 0:257], [(kw[i2][:, dc * 128:(dc + 1) * 128], vaug[:, c, hh, :])],
                       [k(kw[i2]), k(vaug)], [kC])
                    stt("dve", Cst[l][h][:, dc, :], Cst[l][h][:, dc, :], decb[:, h, c:c + 1], pC[:, 0:257],
                        ALU.mult, ALU.add, [k(Cst[l][h]), k(decb), kC], [k(Cst[l][h])])
                yield
                tt("dve", s_[:, 0:1], s_[:, 0:1], tok[:, c, 2, h:h + 1], ALU.max, [k(s_), k(tok)], [k(s_)])
                cp("pool", Cbf[l][h][:, :, :], Cst[l][h][:, :, :], [k(Cst[l][h])], [k(Cbf[l][h])])
                yield
                stt("dve", s_[:, 1:2], s_[:, 0:1], EPS, s_[:, 0:1], ALU.mult, ALU.mult, [k(s_)], [k(s_, "e")])
                yield
                act(s_[:, 3:4], s_[:, 2:3], AF.Ln, [k(s_, "ss"), k(s_, "e")], [k(s_, "r")], scale=1.0 / 256, bias=s_[:, 1:2])
                yield
                act(s_[:, 4:5], s_[:, 3:4], AF.Exp, [k(s_, "r")], [k(s_, "f")], scale=-0.5)
                yield
                stt("dve", hb[i2][:, :], tot[i2][:, 0:256], s_[:, 4:5], GO[:, c, hh * 256:(hh + 1) * 256],
                    ALU.mult, ALU.mult, [k(tot[i2]), k(s_, "f"), k(FC)], [k(hb[i2])])
                yield
                pb2, kb2 = npb()
                for ec in range(2):
                    tr(pb2[:, ec * 128:(ec + 1) * 128], hb[i2][:, ec * 128:(ec + 1) * 128], identb[:, :],
                       [k(hb[i2]), k(identb)], [kb2])
                yield
                cp("act", actG[:, hh * 2:hh * 2 + 2, sl], pb2[:, 0:256].rearrange("p (e t) -> p e t", e=2),
                   [kb2], [k(actG)])
                yield

        def conv_chunk(l, j, e):
            acc = FA[:, j, 0:T]
            a_in = FB
            w0 = PP_CONVW + j * 31
            ts(e, acc, a_in[:, j, 0:T], pp[l][:, w0:w0 + 1], pp[l][:, PP_CONVB + j:PP_CONVB + j + 1], ALU.mult, ALU.add,
               [k(FB, j), k(pp[l])], [k(FA, j)])
            yield
            for t_ in range(1, 31):
                stt(e, acc, a_in[:, j, t_:t_ + T], pp[l][:, w0 + t_:w0 + t_ + 1], acc, ALU.mult, ALU.add,
                    [k(FB, j), k(pp[l]), k(FA, j)], [k(FA, j)])
                yield

        def interleave(gens):
            gens = list(gens)
            while gens:
                for g in list(gens):
                    try:
                        next(g)
                    except StopIteration:
                        gens.remove(g)

        def group_B(l, hg, pre_out=None):
            for hh in range(2):
                h = hg * 2 + hh
                cp("act", Cbf[l][h][:, :, :], Cst[l][h][:, :, :], [k(Cst[l][h])], [k(Cbf[l][h])])
            def zphase():
                wv, wk_ = w_next()
                qk_block(l, wv, wk_, qT, hg * 4)
                w_issue()
                wv, wk_ = w_next()
                qk_block(l, wv, wk_, kT, 8 + hg * 4)
                w_issue()
                wv, wk_ = w_next()
                for c in range(NCH):
                    p, pk = nps()
                    mm(p[:, :], [(hT[:, kk, c * 128:(c + 1) * 128], wv[:, kk, :]) for kk in range(KC)], [wk_, k(hT, c)], [pk])
                    cp("act", vaug[:, c, :, 0:256], p[:, :].rearrange("p (h e) -> p h e", h=2), [pk], [k(vaug)])
                w_issue()
                wv, wk_ = w_next()
                for c in range(NCH):
                    p, pk = nps()
                    mm(p[:, :], [(hT[:, kk, c * 128:(c + 1) * 128], wv[:, kk, :]) for kk in range(KC)], [wk_, k(hT, c)], [pk])
                    act(FC[:, c, 0:512], p[:, :], AF.Sigmoid, [pk], [k(FC)])
                    tt("pool", FC[:, c, 0:512], FC[:, c, 0:512], pbt[:, PB_MNG + hg * 512:PB_MNG + (hg + 1) * 512], ALU.mult,
                       [k(FC), k(pbt)], [k(FC)])
                w_issue()

            def drain(g):
                for _ in g:
                    pass
            COOP.run([zphase, lambda: drain(conv_chunk(l, 2 * hg, "dve"))], weights=[1, 3], pools=[None, None])
            fns = [lambda: drain(mlstm_head(l, hg, 0)), lambda: drain(mlstm_head(l, hg, 1)),
                   lambda: drain(conv_chunk(l, 2 * hg + 1, "dve")),
                   (lambda: C_main(l)) if hg == 0 else (lambda: C_out(l))]
            pools = [{"ps": [0, 1], "pb": [0], "psi": 0, "pbi": 0}, {"ps": [2, 3], "pb": [1], "psi": 0, "pbi": 0},
                     None, {"ps": [4, 5], "pb": [0], "psi": 0, "pbi": 0}]
            COOP.run(fns, weights=[1, 1, 4, 1], pools=pools)
            st_ = pre_out() if pre_out is not None else None
            wv, wk_ = w_next()
            accum_into_x(actG, k(actG), wv, wk_)
            w_issue()
            return st_

        def group_A_z(l):
            sg = FA
            a_in = FB
            wv, wk_ = w_next()
            for j in range(4):
                p, pk = nps()
                mm(p[:, 0:T], z_fm(wv, wk_, j, p), [wk_] + hT_keys(), [pk])
                act(sg[:, j, 0:T], p[:, 0:T], AF.Sigmoid, [pk], [k(FA, j)])
            w_issue()
            wv, wk_ = w_next()
            for j in range(4):
                p, pk = nps()
                mm(p[:, 0:T], z_fm(wv, wk_, j, p), [wk_] + hT_keys(), [pk])
                cp("pool", a_in[:, j, 0:30], histA[l][:, j, :], [k(histA[l], j)], [k(FB, j)])
                tt("dve", a_in[:, j, 30:30 + T], p[:, 0:T], sg[:, j, 0:T], ALU.mult, [pk, k(FA, j)], [k(FB, j)])
                cp("pool", histA[l][:, j, :], a_in[:, j, T:T + 30], [k(FB, j)], [k(histA[l], j)])
            w_issue()

        def A_fin_p1(l):
            sg = FA
            a_in = FB
            for j in range(4):
                act(a_in[:, j, 0:T], sg[:, j, 0:T], AF.Square, [k(FA, j)], [k(FB, j)])
            p1, k1 = nps()
            p2, k2 = nps()
            ones = cst[:, CS_ONES:CS_ONES + 128]
            mm(p1[:, 0:T], [(ones, sg[:, j, 0:T]) for j in range(4)], [k(cst)] + [k(FA, j) for j in range(4)], [k1])
            mm(p2[:, 0:T], [(ones, a_in[:, j, 0:T]) for j in range(4)], [k(cst)] + [k(FB, j) for j in range(4)], [k2])
            mean = FC[:, 0, 0:T]
            var = FC[:, 1, 0:T]
            tmp = FC[:, 2, 0:T]
            ts("dve", mean, p1[:, 0:T], 1.0 / 512, None, ALU.mult, None, [k1], [k(FC)])
            tt("dve", tmp, mean, mean, ALU.mult, [k(FC)], [k(FC)])
            stt("dve", var, p2[:, 0:T], 1.0 / 512, tmp, ALU.mult, ALU.subtract, [k2, k(FC)], [k(FC)])
            rsqrt(var, var, 1.0, EPS, [k(FC)], [k(FC)])
            return None

        def A_fin_p2(l, st_):
            sg = FA
            actA = gT[1]
            mean = FC[:, 0, 0:T]
            var = FC[:, 1, 0:T]
            for j in range(4):
                e = "dve" if j % 2 == 0 else "pool"
                tt(e, sg[:, j, 0:T], sg[:, j, 0:T], mean, ALU.subtract, [k(FA, j), k(FC)], [k(FA, j)])
                tt(e, sg[:, j, 0:T], sg[:, j, 0:T], var, ALU.mult, [k(FA, j), k(FC)], [k(FA, j)])
                act(actA[:, j, :], sg[:, j, 0:T], AF.Silu, [k(FA, j), k(pp[l])], [k(gT[1])],
                    scale=pp[l][:, PP_CNG + j:PP_CNG + j + 1], bias=pp[l][:, PP_CNB + j:PP_CNB + j + 1])
            wv, wk_ = w_next()
            accum_into_x(actA, k(gT[1]), wv, wk_, hook=norm_hook(l, PP_LNMLP))
            w_issue()

        def C_main(l):
            guT = gT[0]
            gvn = gT[1]
            gng = zq[0][:, 0:512]
            gnb = zq[1][:, 0:512]
            gmb = cacc[0][:, 0:512]
            g_ = cacc[1][:, 0:512]
            jk = xs[0]
            S_.dma("sp", "pbc0", gng, pb_d[l:l + 1, PB_GNG:PB_GNG + 512].partition_broadcast(128), reads=(), writes=[k(zq[0])])
            S_.dma("sp", "pbc1", gnb, pb_d[l:l + 1, PB_GNB:PB_GNB + 512].partition_broadcast(128), reads=(), writes=[k(zq[1])])
            wv, wk_ = w_next()
            for c in range(NCH):
                p, pk = nps()
                mm(p[:, :], [(hT[:, kk, c * 128:(c + 1) * 128], wv[:, kk, :]) for kk in range(KC)], [wk_, k(hT, c)], [pk])
                act(g_, p[:, :], AF.Gelu, [pk], [k(cacc[1]), k(cs1, c, 0)], accum_out=cs1[:, 4 * c:4 * c + 1])
                act(jk[:, 0:512], g_, AF.Square, [k(cacc[1])], [k(jk), k(cs1, c, 1)], accum_out=cs1[:, 4 * c + 1:4 * c + 2])
                m_ = cs1[:, 4 * c:4 * c + 1]
                q1 = cs1[:, 4 * c + 1:4 * c + 2]
                v_ = cs1[:, 4 * c + 3:4 * c + 4]
                kc_ = [k(cs1, c, i) for i in range(2)]
                ts("dve", m_, m_, 1.0 / 512, None, ALU.mult, None, kc_, [k(cs1, c, 0)])
                tt("dve", v_, m_, m_, ALU.mult, kc_, [k(cs1, c, 3)])
                stt("dve", v_, q1, 1.0 / 512, v_, ALU.mult, ALU.subtract, kc_ + [k(cs1, c, 3)], [k(cs1, c, 3)])
                rsqrt(v_, v_, 1.0, EPS, [k(cs1, c, 3)], [k(cs1, c, 3)])
                ts("dve", g_, g_, m_, v_, ALU.subtract, ALU.mult, [k(cacc[1]), k(cs1, c, 0), k(cs1, c, 3)], [k(cacc[1])])
                tt("pool", g_, g_, gng, ALU.mult, [k(cacc[1]), k(zq[0])], [k(cacc[1])])
                tt("dve", gvn[:, c, :], g_, gnb, ALU.add, [k(cacc[1]), k(zq[1])], [k(gT[1])])
            w_issue()

        def C_main_b(l):
            guT = gT[0]
            gvn = gT[1]
            gmb = cacc[0][:, 0:512]
            g_ = cacc[1][:, 0:512]
            S_.dma("sp", "pbc2", gmb, pb_d[l:l + 1, PB_GMB:PB_GMB + 512].partition_broadcast(128), reads=(), writes=[k(cacc[0])])
            wv, wk_ = w_next()
            for j in range(4):
                p, pk = nps()
                mm(p[:, 0:T], z_fm(wv, wk_, j, p), [wk_] + hT_keys(), [pk])
                act(guT[:, j, 0:T], p[:, 0:T], AF.Gelu, [pk], [k(gT[0])])
            w_issue()
            for c in range(NCH):
                sl = slice(c * 128, (c + 1) * 128)
                p, pk = nps()

                def emit4(p=p, c=c):
                    ins = None
                    for g in range(4):
                        ins = nc.tensor.matmul(p[:, g * 128:(g + 1) * 128], gvn[:, c, g * 128:(g + 1) * 128], WcT[l][:, g, :],
                                               start=True, stop=True)
                    return ins
                S_.op("pe", emit4, [k(gT[1]), k(WcT[l])], [pk])
                tt("dve", g_, p[:, :], gmb, ALU.add, [pk, k(cacc[0])], [k(cacc[1])])
                tt("dve", guT[:, :, sl], g_.rearrange("p (g t) -> p g t", g=4), guT[:, :, sl], ALU.mult,
                   [k(cacc[1]), k(gT[0])], [k(gT[0])])

        def C_out(l):
            C_main_b(l)
            wv, wk_ = w_next()
            accum_into_x(gT[0], k(gT[0]), wv, wk_)
            w_issue()

        def ffn(l, nxt=None, tt_=0):
            NB_ = DFF // 512
            if nxt is not None:
                wg_prefetch(nxt)
            if l == depth - 1:
                fg_prefetch()
            for jf in range(NB_):
                g_ = gT[jf % 2]
                wv, wk_ = w_next()
                for j in range(4):
                    r_ = rt[j % 2]
                    p, pk = nps()
                    mm(p[:, 0:T], z_fm(wv, wk_, j, p), [wk_] + hT_keys(), [pk])
                    act(r_[:, :], p[:, 0:T], AF.Relu, [pk], [k(r_)])
                    tt("pool", g_[:, j, :], r_[:, :], r_[:, :], ALU.mult, [k(r_)], [k(g_)])
                w_issue()
                wv, wk_ = w_next()
                hk = None
                if jf == NB_ - 1:
                    if l + 1 < depth:
                        hk = norm_hook(l + 1, PP_LNMIX)
                    elif tt_ + 1 < NT:
                        def hk(c, tt_=tt_):
                            final_chunk(tt_, c)
                            r0 = (tt_ + 1) * T + c * 128
                            S_.dma("sp", f"xin{c}", x[:, c, :], x_d[r0:r0 + 128, :], reads=(), writes=[k(x, c)])
                            if c >= 1:
                                norm_act(0, c - 1)
                                norm_pe(0, PP_LNMIX, c - 1)
                            if c == NCH - 1:
                                norm_act(0, c)
                                norm_pe(0, PP_LNMIX, c)
                    else:
                        hk = (lambda c: final_chunk(tt_, c))
                accum_into_x(g_, k(g_), wv, wk_, hook=hk)
                w_issue()

        fgt = FA[:, :, :].rearrange("p a b -> p (a b)")[:, 0:D]

        def fg_prefetch():
            S_.dma("sp", "fg", fgt, fg_d[0:1, :].partition_broadcast(128), reads=(), writes=[k(FA, j) for j in range(4)])

        def final_chunk(tt_, c):
            o_t = FB if c % 2 == 0 else FC
            o_ = o_t[:, :, :].rearrange("p a b -> p (a b)")[:, 0:D]
            okeys = [k(FB, j) for j in range(4)] if c % 2 == 0 else [k(FC)]
            act(xs0[:, :], x[:, c, :], AF.Square, [k(x, c)], [k(xs0), k(ss, c)], accum_out=ss[:, c:c + 1])
            rsqrt(rstd[:, c:c + 1], ss[:, c:c + 1], 1.0 / D, EPS, [k(ss, c)], [k(rstd, c)])
            stt("dve", o_, x[:, c, :], rstd[:, c:c + 1], fgt, ALU.mult, ALU.mult,
                [k(x, c), k(rstd, c)] + [k(FA, j) for j in range(4)], okeys)
            r0 = tt_ * T + c * 128
            S_.dma("sp", f"st{c % 2}", y_d[r0:r0 + 128, :], o_, reads=okeys, writes=())

        for tt_ in range(NT):
            if tt_ == 0:
                for c in range(NCH):
                    r0 = tt_ * T + c * 128
                    S_.dma("sp", f"xin{c}", x[:, c, :], x_d[r0:r0 + 128, :], reads=(), writes=[k(x, c)])
            for l in range(depth):
                S_.dma("sp", "pbt", pbt[:, :], pb_d[l:l + 1, 0:1024].partition_broadcast(128), reads=(), writes=[k(pbt)])
                if l == 0 and tt_ == 0:
                    norm_to_hT(l, PP_LNMIX)
                gates_p1(l)
                group_A_z(l)
                gates_p2(l)
                group_B(l, 0)
                st_ = group_B(l, 1, pre_out=lambda: A_fin_p1(l))
                A_fin_p2(l, st_)
                nxt = l + 1 if l + 1 < depth else (0 if tt_ + 1 < NT else None)
                ffn(l, nxt, tt_)
        S_.wait_all("sp", ["st0", "st1"])
        for e in ["act", "dve", "pool", "pe"]:
            S_.wait_all(e, ["st0", "st1"])
        assert wpos["used"] == len(wstream), (wpos, len(wstream))
    return nc


def _host_layout(inputs, depth=DEPTH):
    f = lambda a: np.ascontiguousarray(np.asarray(a, dtype=np.float32))
    pp = np.zeros((depth, 128, NPP), np.float32)
    pb = np.zeros((depth, NPB), np.float32)
    for l in range(depth):
        pp[l, :, PP_LNMIX:PP_LNMIX + 16] = f(inputs["ln_mix_g"])[l].reshape(16, 128).T
        pp[l, :, PP_LNMLP:PP_LNMLP + 16] = f(inputs["ln_mlp_g"])[l].reshape(16, 128).T
        cw = f(inputs["conv_w"])[l]
        pp[l, :, PP_CONVW:PP_CONVW + 124] = cw.T.reshape(4, 128, 31).transpose(1, 0, 2).reshape(128, 124)
        pp[l, :, PP_CONVB:PP_CONVB + 4] = f(inputs["conv_b"])[l].reshape(4, 128).T
        pp[l, :, PP_CNG:PP_CNG + 4] = f(inputs["conv_norm_g"])[l].reshape(4, 128).T
        pp[l, :, PP_CNB:PP_CNB + 4] = f(inputs["conv_norm_b"])[l].reshape(4, 128).T
        qw = f(inputs["qk_conv_w"])[l]
        pp[l, :, PP_QKW:PP_QKW + 64] = qw.T.reshape(16, 128, 4).transpose(1, 0, 2).reshape(128, 64)
        pp[l, :, PP_QKB:PP_QKB + 16] = f(inputs["qk_conv_b"])[l].reshape(16, 128).T
        pp[l, 0:4, PP_IGB] = f(inputs["igate_b"])[l]
        pp[l, 0:4, PP_FGB] = f(inputs["fgate_b"])[l]
        pb[l, PB_MNG:PB_MNG + 1024] = f(inputs["mlstm_norm_g"])[l]
        pb[l, PB_GNG:PB_GNG + 512] = f(inputs["gm_norm_g"])[l]
        pb[l, PB_GNB:PB_GNB + 512] = f(inputs["gm_norm_b"])[l]
        pb[l, PB_GMB:PB_GMB + 512] = f(inputs["gm_b"])[l].reshape(512)
    gmwT = np.ascontiguousarray(f(inputs["gm_w"])[:depth].transpose(0, 3, 1, 2))
    cst = np.zeros((128, NCST), np.float32)
    cst[:, CS_IDENT:CS_IDENT + 128] = np.eye(128, dtype=np.float32)
    cst[:, CS_MASK:CS_MASK + 128] = np.tril(np.ones((128, 128), np.float32))
    cst[:, CS_MASKT:CS_MASKT + 128] = np.triu(np.ones((128, 128), np.float32))
    cst[:, CS_ONES:CS_ONES + 128] = 1.0
    return {
        "w_in": f(inputs["w_in"])[:depth], "w_out": f(inputs["w_out"])[:depth],
        "w_up": f(inputs["w_up"])[:depth], "w_down": f(inputs["w_down"])[:depth],
        "pp": pp, "pb": pb, "final_g": f(inputs["final_g"]).reshape(1, D), "gm_wT": gmwT, "cst": cst,
    }


_NC_CACHE = {}


def kernel(**inputs):
    x = np.asarray(inputs["x"], dtype=np.float32)
    B, S, _ = x.shape
    T = 512
    key = (S, T)
    if key not in _NC_CACHE:
        _NC_CACHE[key] = build_nc(S, T)
    nc = _NC_CACHE[key]
    shared = _host_layout(inputs)
    in_maps = []
    for b in range(B):
        m = dict(shared)
        m["x"] = np.ascontiguousarray(x[b])
        in_maps.append(m)
    res = run_bass_kernel_spmd(nc, in_maps, core_ids=list(range(B)))
    return np.stack([np.asarray(r["y"], dtype=np.float32) for r in res.results], axis=0)
```
